# Optimizing a Trainium2 kernel written in Bass

```python
import math
import jax
import jax.numpy as jnp
from jax import lax
import numpy as np

D_MODEL = 1024
BATCH = 4
SEQ = 8192
DEPTH = 2

CHUNK = 64
Q_BLOCK = 128
MEM_LEN = 256
EPS = 1e-6

MLA_HEADS = 4
MLA_Q_RANK = 256
MLA_KV_RANK = 128
MLA_NOPE = 64
MLA_ROPE = 32
MLA_V = 128
ROPE_THETA = 10000.0
FOX_HEADS = 4
FOX_DIM = 64
CHK_HEADS = 4
CHK_DIM = 64
CHK_LEFT = 8
BAND = (CHK_LEFT + 1) * CHUNK
REL_MAX = 128
REL_SIZE = (CHUNK - 1) + REL_MAX + 1

A_WIDTH = MLA_HEADS * MLA_V
B_WIDTH = FOX_HEADS * FOX_DIM
C_WIDTH = CHK_HEADS * CHK_DIM
MIX_WIDTH = A_WIDTH + B_WIDTH + C_WIDTH

IN_SIZES = (MLA_Q_RANK, MLA_KV_RANK, MLA_ROPE,
            B_WIDTH, B_WIDTH, B_WIDTH, FOX_HEADS,
            C_WIDTH, C_WIDTH, C_WIDTH)
IN_WIDTH = sum(IN_SIZES)
IN_SPLIT_POINTS = tuple(int(v) for v in np.cumsum(IN_SIZES)[:-1])

CROSS_HEADS = 4
CROSS_DIM = 128
CROSS_WIDTH = CROSS_HEADS * CROSS_DIM

FFN_HIDDEN = ((-(-(8 * D_MODEL) // 3) + 255) // 256) * 256

kernel_name = 'hybrid_mla_fox_chunkrel_block'


def _rmsnorm(x, g):
    xf = x.astype(jnp.float32)
    y = xf * lax.rsqrt(jnp.mean(xf * xf, axis=-1, keepdims=True) + EPS)
    return (y * g.astype(jnp.float32)).astype(x.dtype)


def _rope_tables(seq):
    pos = jnp.arange(seq, dtype=jnp.float32)
    inv = ROPE_THETA ** (-jnp.arange(0, MLA_ROPE, 2, dtype=jnp.float32) / MLA_ROPE)
    ang = pos[:, None] * inv[None, :]
    return jnp.cos(ang), jnp.sin(ang)


def _rope(x, cos, sin):
    xf = x.astype(jnp.float32)
    x1, x2 = jnp.split(xf, 2, axis=-1)
    return jnp.concatenate([x1 * cos - x2 * sin, x1 * sin + x2 * cos], axis=-1).astype(x.dtype)


def _split_blocks(a, block):
    b, s = a.shape[:2]
    return jnp.moveaxis(a.reshape((b, s // block, block) + a.shape[2:]), 1, 0)


def _merge_blocks(a):
    nb, b, blk = a.shape[:3]
    return jnp.moveaxis(a, 0, 1).reshape(b, nb * blk, -1)


def _mla(c_q, c_kv, k_rope, q_norm, w_uq, kv_norm, w_ukv, cos, sin):
    b, s, _ = c_q.shape
    q = (_rmsnorm(c_q, q_norm) @ w_uq).reshape(b, s, MLA_HEADS, MLA_NOPE + MLA_ROPE)
    q_nope, q_pe = jnp.split(q, [MLA_NOPE], axis=-1)
    q = jnp.concatenate([q_nope, _rope(q_pe, cos[:, None], sin[:, None])], axis=-1)
    kv = (_rmsnorm(c_kv, kv_norm) @ w_ukv).reshape(b, s, MLA_HEADS, MLA_NOPE + MLA_V)
    k_nope, v = jnp.split(kv, [MLA_NOPE], axis=-1)
    k_pe = _rope(k_rope, cos, sin)
    k = jnp.concatenate(
        [k_nope, jnp.broadcast_to(k_pe[:, :, None, :], (b, s, MLA_HEADS, MLA_ROPE))], axis=-1)
    scale = (MLA_NOPE + MLA_ROPE) ** -0.5
    k_chunk = jnp.arange(s) // CHUNK

    def attend(args):
        qb, idx = args
        q_chunk = (idx * Q_BLOCK + jnp.arange(Q_BLOCK)) // CHUNK
        sc = jnp.einsum('bqhd,bkhd->bhqk', qb, k, preferred_element_type=jnp.float32) * scale
        sc = jnp.where(k_chunk[None, :] <= q_chunk[:, None], sc, -jnp.inf)
        p = jax.nn.softmax(sc, axis=-1).astype(v.dtype)
        return jnp.einsum('bhqk,bkhd->bqhd', p, v)

    out = lax.map(attend, (_split_blocks(q, Q_BLOCK), jnp.arange(s // Q_BLOCK)))
    return _merge_blocks(out)


def _fox(q, k, v, f_logit, f_bias):
    b, s, _ = q.shape
    q = q.reshape(b, s, FOX_HEADS, FOX_DIM)
    k = k.reshape(b, s, FOX_HEADS, FOX_DIM)
    v = v.reshape(b, s, FOX_HEADS, FOX_DIM)
    log_f = jax.nn.log_sigmoid(f_logit.astype(jnp.float32) + f_bias.astype(jnp.float32))
    cum = jnp.cumsum(log_f, axis=1)
    cum_k = jnp.transpose(cum, (0, 2, 1))
    scale = FOX_DIM ** -0.5
    k_pos = jnp.arange(s)

    def attend(args):
        qb, cq, idx = args
        q_pos = idx * Q_BLOCK + jnp.arange(Q_BLOCK)
        sc = jnp.einsum('bqhd,bkhd->bhqk', qb, k, preferred_element_type=jnp.float32) * scale
        sc = sc + jnp.transpose(cq, (0, 2, 1))[..., None] - cum_k[:, :, None, :]
        sc = jnp.where(k_pos[None, :] <= q_pos[:, None], sc, -jnp.inf)
        p = jax.nn.softmax(sc, axis=-1).astype(v.dtype)
        return jnp.einsum('bhqk,bkhd->bqhd', p, v)

    out = lax.map(attend, (_split_blocks(q, Q_BLOCK), _split_blocks(cum, Q_BLOCK),
                           jnp.arange(s // Q_BLOCK)))
    return _merge_blocks(out)


def _chunk_rel(q, k, v, rel_table):
    b, s, _ = q.shape
    q = q.reshape(b, s, CHK_HEADS, CHK_DIM)
    k = k.reshape(b, s, CHK_HEADS, CHK_DIM)
    v = v.reshape(b, s, CHK_HEADS, CHK_DIM)
    pad = CHK_LEFT * CHUNK
    kp = jnp.pad(k, ((0, 0), (pad, 0), (0, 0), (0, 0)))
    vp = jnp.pad(v, ((0, 0), (pad, 0), (0, 0), (0, 0)))
    qi = jnp.arange(CHUNK)
    ki = jnp.arange(BAND)
    rel = qi[:, None] + pad - ki[None, :]
    rel_idx = jnp.clip(rel, -(CHUNK - 1), REL_MAX) + (CHUNK - 1)
    bias = rel_table[:, rel_idx].astype(jnp.float32)
    scale = CHK_DIM ** -0.5

    def attend(args):
        qc, idx = args
        start = idx * CHUNK
        kb = lax.dynamic_slice_in_dim(kp, start, BAND, axis=1)
        vb = lax.dynamic_slice_in_dim(vp, start, BAND, axis=1)
        valid = (start - pad + ki) >= 0
        sc = jnp.einsum('bqhd,bkhd->bhqk', qc, kb, preferred_element_type=jnp.float32) * scale
        sc = jnp.where(valid, sc + bias[None], -jnp.inf)
        p = jax.nn.softmax(sc, axis=-1).astype(vb.dtype)
        return jnp.einsum('bhqk,bkhd->bqhd', p, vb)

    out = lax.map(attend, (_split_blocks(q, CHUNK), jnp.arange(s // CHUNK)))
    return _merge_blocks(out)


def _cross(h, m, w_cq, w_ckv, w_co):
    b, s, _ = h.shape
    n = m.shape[1]
    q = (h @ w_cq).reshape(b, s, CROSS_HEADS, CROSS_DIM)
    k, v = jnp.split(m @ w_ckv, 2, axis=-1)
    k = k.reshape(b, n, CROSS_HEADS, CROSS_DIM)
    v = v.reshape(b, n, CROSS_HEADS, CROSS_DIM)
    sc = jnp.einsum('bshd,bmhd->bhsm', q, k, preferred_element_type=jnp.float32) * (CROSS_DIM ** -0.5)
    p = jax.nn.softmax(sc, axis=-1).astype(v.dtype)
    o = jnp.einsum('bhsm,bmhd->bshd', p, v).reshape(b, s, CROSS_WIDTH)
    return o @ w_co


def setup_inputs(seed: int = 0) -> dict:
    key = jax.random.key(seed)
    ks = jax.random.split(key, 24)

    def nrm(k, shape, fan_in):
        return jax.random.normal(k, shape, jnp.float32) * (fan_in ** -0.5)

    def gain(k, shape):
        return 1.0 + 0.05 * jax.random.normal(k, shape, jnp.float32)

    L = DEPTH
    return {
        'x': jax.random.normal(ks[0], (BATCH, SEQ, D_MODEL), jnp.float32),
        'mem': jax.random.normal(ks[1], (BATCH, MEM_LEN, D_MODEL), jnp.float32),
        'norm_mix': gain(ks[2], (L, D_MODEL)),
        'w_in': nrm(ks[3], (L, D_MODEL, IN_WIDTH), D_MODEL),
        'q_norm': gain(ks[4], (L, MLA_Q_RANK)),
        'w_uq': nrm(ks[5], (L, MLA_Q_RANK, MLA_HEADS * (MLA_NOPE + MLA_ROPE)), MLA_Q_RANK),
        'kv_norm': gain(ks[6], (L, MLA_KV_RANK)),
        'w_ukv': nrm(ks[7], (L, MLA_KV_RANK, MLA_HEADS * (MLA_NOPE + MLA_V)), MLA_KV_RANK),
        'f_bias': jax.random.uniform(ks[8], (L, FOX_HEADS), jnp.float32, minval=1.0, maxval=4.0),
        'rel_bias': 0.3 * jax.random.normal(ks[9], (L, CHK_HEADS, REL_SIZE), jnp.float32),
        'out_norm': gain(ks[10], (L, MIX_WIDTH)),
        'w_o': nrm(ks[11], (L, MIX_WIDTH, D_MODEL), MIX_WIDTH),
        'norm_cross': gain(ks[12], (L, D_MODEL)),
        'norm_mem': gain(ks[13], (L, D_MODEL)),
        'w_cq': nrm(ks[14], (L, D_MODEL, CROSS_WIDTH), D_MODEL),
        'w_ckv': nrm(ks[15], (L, D_MODEL, 2 * CROSS_WIDTH), D_MODEL),
        'w_co': nrm(ks[16], (L, CROSS_WIDTH, D_MODEL), CROSS_WIDTH),
        'norm_ffn': gain(ks[17], (L, D_MODEL)),
        'w_gu': nrm(ks[18], (L, D_MODEL, 2 * FFN_HIDDEN), D_MODEL),
        'w_down': nrm(ks[19], (L, FFN_HIDDEN, D_MODEL), FFN_HIDDEN),
        'final_norm': gain(ks[20], (D_MODEL,)),
    }


def reference(x, mem, norm_mix, w_in, q_norm, w_uq, kv_norm, w_ukv, f_bias, rel_bias,
              out_norm, w_o, norm_cross, norm_mem, w_cq, w_ckv, w_co, norm_ffn, w_gu,
              w_down, final_norm):
    s = x.shape[1]
    cos, sin = _rope_tables(s)
    for l in range(DEPTH):
        h = _rmsnorm(x, norm_mix[l])
        proj = h @ w_in[l]
        (c_q, c_kv, k_rope, fq, fk, fv, f_logit, cq, ck, cv) = jnp.split(
            proj, IN_SPLIT_POINTS, axis=-1)
        ya = _mla(c_q, c_kv, k_rope, q_norm[l], w_uq[l], kv_norm[l], w_ukv[l], cos, sin)
        yb = _fox(fq, fk, fv, f_logit, f_bias[l])
        yc = _chunk_rel(cq, ck, cv, rel_bias[l])
        ga, gb, gc = jnp.split(out_norm[l], [A_WIDTH, A_WIDTH + B_WIDTH])
        y = jnp.concatenate([_rmsnorm(ya, ga), _rmsnorm(yb, gb), _rmsnorm(yc, gc)], axis=-1)
        x = x + y @ w_o[l]
        x = x + _cross(_rmsnorm(x, norm_cross[l]), _rmsnorm(mem, norm_mem[l]),
                       w_cq[l], w_ckv[l], w_co[l])
        h = _rmsnorm(x, norm_ffn[l])
        gate, up = jnp.split(h @ w_gu[l], 2, axis=-1)
        x = x + (jax.nn.silu(gate) * up) @ w_down[l]
    return _rmsnorm(x, final_norm)
```

```python
import os
import numpy as np
from contextlib import ExitStack
import concourse.bass as bass
import concourse.mybir as mybir
from concourse.bass_utils import run_bass_kernel_spmd

F32 = mybir.dt.float32
BF16 = mybir.dt.bfloat16
I32 = mybir.dt.int32
AF = mybir.ActivationFunctionType
ALU = mybir.AluOpType

D = 1024
NL = 2
EPS = 1e-6
FFN = 2816
NEG = -30000.0
COMPUTE = ("pe", "act", "dve", "pool")
ENGS = ("pe", "act", "dve", "pool", "sp")


class Buf:
    __slots__ = ("name", "w", "r", "sem")

    def __init__(self, name=""):
        self.name = name
        self.w = None
        self.r = {}
        self.sem = None


class Prog:
    def __init__(self, nc, es, arena_words):
        self.nc = nc
        self.es = es
        self.ops = {e: [] for e in ENGS}
        self.esem = {e: es.enter_context(nc.semaphore("s_" + e)) for e in COMPUTE}
        self.dsems = []
        self.free_ds = []
        self.arena = es.enter_context(nc.sbuf_tensor("arena", [128, arena_words], F32))
        self.arena_words = arena_words
        self.persist_top = 0
        self.top = 0
        self.banks = []
        for i in range(8):
            t = es.enter_context(nc.psum_tensor("bank%d" % i, [128, 512], F32))
            self.banks.append((t, Buf("bank%d" % i)))
        self.bank_i = 0

    def alloc(self, shape, dtype, parts=128):
        n = 1
        for s in shape:
            n *= s
        nbytes = n * (2 if dtype == BF16 else 4)
        words = (nbytes + 3) // 4
        words = (words + 7) // 8 * 8
        off = self.top
        self.top += words
        assert self.top <= self.arena_words, ("SBUF arena overflow", self.top * 4)
        v = self.arena[0:parts, off:off + (nbytes + 3) // 4]
        if dtype != F32:
            v = v.bitcast(dtype)
        if len(shape) == 2:
            v = v.rearrange("p (a b) -> p a b", a=shape[0])
        elif len(shape) == 3:
            v = v.rearrange("p (a b c) -> p a b c", a=shape[0], b=shape[1])
        return v

    rot = list(range(8))

    def bank(self):
        i = self.rot[self.bank_i % len(self.rot)]
        self.bank_i += 1
        return self.banks[i]

    def bank_at(self, i):
        return self.banks[i]

    def new_sem(self, buf):
        if self.free_ds:
            buf.sem = self.free_ds.pop()
        else:
            h = self.es.enter_context(self.nc.semaphore("d%d" % len(self.dsems)))
            self.dsems.append([h, 0, False])
            buf.sem = len(self.dsems) - 1
        return buf

    def dbuf(self, name=""):
        return self.new_sem(Buf(name))

    def _deps(self, eng, reads, writes):
        deps = {}

        def add(tok):
            if tok is None:
                return
            k, v = tok
            if k == "pe" and eng == "pe":
                return
            if not isinstance(k, str):
                ds = self.dsems[k[1]]
                v = ds[1]
                ds[2] = True
            if deps.get(k, -1) < v:
                deps[k] = v

        for b in reads:
            add(b.w)
        for b in writes:
            add(b.w)
            for t in b.r.values():
                add(t)
        return deps

    nrec = 0
    maxops = int(os.environ.get('K_MAXOPS', '0'))

    def op(self, eng, fn, reads=(), writes=()):
        Prog.nrec += 1
        if Prog.maxops and Prog.nrec > Prog.maxops:
            return None
        deps = self._deps(eng, reads, writes)
        tok = (eng, len(self.ops[eng]))
        self.ops[eng].append([fn, deps, False, None, 0])
        for b in reads:
            b.r[eng] = tok
        for b in writes:
            b.w = tok
            b.r = {}
        return tok

    def dma(self, q, out, in_, reads=(), writes=(), sem=None, slow=False):
        Prog.nrec += 1
        if Prog.maxops and Prog.nrec > Prog.maxops:
            return None
        deps = self._deps(q, reads, writes)
        s = sem.sem
        ds = self.dsems[s]
        key = ("d", s)
        if ds[2]:
            if deps.get(key, -1) < ds[1]:
                deps[key] = ds[1]
            ds[2] = False
        ds[1] += 16
        tok = (key, ds[1])
        if slow:
            fn = lambda e: e.dma_start(out=out, in_=in_, allow_slow_non_contiguous=True)
        else:
            fn = lambda e: e.dma_start(out=out, in_=in_)
        self.ops[q].append([fn, deps, False, s, 0])
        for b in reads:
            b.r[key] = tok
        for b in writes:
            b.w = tok
            b.r = {}
        return tok

    def barrier(self):
        deps = {}
        for e in COMPUTE:
            i = len(self.ops[e]) - 1
            while i >= 0 and self.ops[e][i][0] is None:
                i -= 1
            if i >= 0:
                deps[e] = i
        for i, ds in enumerate(self.dsems):
            if ds[1] > 0:
                deps[("d", i)] = ds[1]
                ds[2] = False
        for e in ENGS:
            d = dict(deps)
            if e == "pe":
                d.pop("pe", None)
            self.ops[e].append([None, d, False, None, 0])
        self.top = self.persist_top
        self.free_ds = list(range(len(self.dsems)))
        for t, b in self.banks:
            b.w = None
            b.r = {}

    def emit(self, block):
        for e in ENGS:
            w = {}
            for o in self.ops[e]:
                nd = []
                for k, v in o[1].items():
                    if w.get(k, -1) >= v:
                        continue
                    w[k] = v
                    nd.append((k, v))
                    if isinstance(k, str):
                        self.ops[k][v][2] = True
                o[1] = nd
        for e in COMPUTE:
            c = 0
            for o in self.ops[e]:
                if o[2]:
                    c += 1
                o[4] = c
        prog = self
        if os.environ.get('K_DUMP'):
            for e in ENGS:
                for i, o in enumerate(self.ops[e]):
                    ws = [(k, (self.ops[k][v][4] if isinstance(k, str) else v)) for k, v in o[1]]
                    print(e, i, 'waits', ws, 'fn' if o[0] else 'nofn', 'inc' if o[2] else '', 'dma%s' % o[3] if o[3] is not None else '', 'val', o[4])

        def body(e):
            def f(h):
                for o in prog.ops[e]:
                    for k, v in o[1]:
                        if isinstance(k, str):
                            h.wait_ge(prog.esem[k], prog.ops[k][v][4])
                        else:
                            h.wait_ge(prog.dsems[k[1]][0], v)
                    if o[0] is None:
                        continue
                    ins = o[0](h)
                    if o[3] is not None:
                        ins.then_inc(prog.dsems[o[3]][0], 16)
                    elif o[2]:
                        ins.then_inc(prog.esem[e], 1)
            return f

        block.sync(body("sp"))
        block.tensor(body("pe"))
        block.scalar(body("act"))
        block.vector(body("dve"))
        block.gpsimd(body("pool"))

    def mm(self, out, lhsT, rhs, start, stop, reads, writes):
        return self.op("pe", lambda e: e.matmul(out, lhsT=lhsT, rhs=rhs, start=start, stop=stop), reads, writes)

    def tr(self, out, in_, ident, reads, writes):
        return self.op("pe", lambda e: e.transpose(out=out, in_=in_, identity=ident), reads, writes)

    def act(self, out, in_, func, reads, writes, bias=None, scale=1.0, accum=None):
        kw = {}
        if bias is not None:
            kw["bias"] = bias
        if accum is not None:
            kw["accum_out"] = accum
        return self.op("act", lambda e: e.activation(out=out, in_=in_, func=func, scale=scale, **kw), reads, writes)

    def ts(self, eng, out, in0, s1, s2, op0, op1, reads, writes):
        if op1 is None:
            return self.op(eng, lambda e: e.tensor_scalar(out=out, in0=in0, scalar1=s1, scalar2=None, op0=op0), reads, writes)
        return self.op(eng, lambda e: e.tensor_scalar(out=out, in0=in0, scalar1=s1, scalar2=s2, op0=op0, op1=op1), reads, writes)

    def tt(self, eng, out, in0, in1, op, reads, writes):
        return self.op(eng, lambda e: e.tensor_tensor(out=out, in0=in0, in1=in1, op=op), reads, writes)

    def cp(self, eng, out, in_, reads, writes):
        if eng == "act":
            return self.op("act", lambda e: e.copy(out=out, in_=in_), reads, writes)
        return self.op(eng, lambda e: e.tensor_copy(out=out, in_=in_), reads, writes)

    def memset(self, eng, ap, val, writes):
        return self.op(eng, lambda e: e.memset(ap, val), (), writes)


def bf16_view(bank_t, parts=128):
    return bank_t[0:parts, :].bitcast(BF16)


def build_program(S, n_layers=NL, debug=False):
    NT = S // 128
    NG = S // 512
    nc = bass.Bass("TRN2", target_bir_lowering=False)

    def din(name, shape, dt=F32):
        return nc.dram_tensor(name, list(shape), dt, kind="ExternalInput").ap()

    x_in = din("x", [S, D])
    mem_in = din("mem", [256, D])
    w = {}
    for name, shape in [("norm_mix", [NL, D]), ("w_in", [NL, D, 1956]), ("q_norm", [NL, 256]),
                        ("w_uq", [NL, 256, 384]), ("kv_norm", [NL, 128]), ("w_ukv", [NL, 128, 768]),
                        ("f_bias", [NL, 4]), ("relT", [NL, 4, 128, 5, 128]), ("out_norm", [NL, D]),
                        ("w_o", [NL, D, D]), ("norm_cross", [NL, D]), ("norm_mem", [NL, D]),
                        ("w_cq", [NL, D, 512]), ("w_ckv", [NL, D, 1024]), ("w_co", [NL, 512, D]),
                        ("norm_ffn", [NL, D]), ("w_gu", [NL, D, 2 * FFN]), ("w_down", [NL, FFN, D]),
                        ("final_norm", [D])]:
        w[name] = din(name, shape)
    out_d = nc.dram_tensor("out", [S, D], F32, kind="ExternalOutput").ap()
    skind = "ExternalOutput" if debug else "Internal"

    def dscr(name, shape, dt):
        return nc.dram_tensor(name, list(shape), dt, kind=skind).ap()

    xres = dscr("xres", [S, D], F32)
    qTm = dscr("qTm", [4, 96, S], BF16)
    kTm = dscr("kTm", [4, 96, S], BF16)
    vm = dscr("vm", [S, 512], BF16)
    qTf = dscr("qTf", [4, 70, S], BF16)
    kTf = dscr("kTf", [4, 70, S], BF16)
    vf = dscr("vf", [S, 256], BF16)
    qTc = dscr("qTc", [256, S], BF16)
    kTc = dscr("kTc", [256, S], BF16)
    vc = dscr("vc", [S, 256], BF16)
    ymixT = dscr("ymixT", [D, S], BF16)
    ropeD = dscr("ropeD", [S, 128], F32)

    with ExitStack() as es:
        P = Prog(nc, es, arena_words=50 * 1024)
        ident = P.alloc([128], BF16)
        ones_bf = P.alloc([128], BF16)
        identf = P.alloc([128], F32)
        onesf = P.alloc([128], F32)
        triu = P.alloc([128], F32)
        triu_bf = P.alloc([128], BF16)
        eps_t = P.alloc([1], F32)
        one_t = P.alloc([1], F32)
        npi_t = P.alloc([1], F32)
        mbm = P.alloc([4, 512], BF16)
        mbf = P.alloc([4, 512], BF16)
        cB = Buf("consts")
        P.memset("pool", identf, 1.0, [cB])
        P.op("pool", lambda e: e.affine_select(out=identf, in_=identf, pattern=[[-1, 128]], compare_op=ALU.is_equal,
                                                fill=0.0, base=0, channel_multiplier=1), [cB], [cB])
        P.cp("pool", ident, identf, [cB], [cB])
        P.memset("pool", onesf, 1.0, [cB])
        P.memset("pool", ones_bf, 1.0, [cB])
        P.memset("pool", triu, 1.0, [cB])
        P.op("pool", lambda e: e.affine_select(out=triu, in_=triu, pattern=[[1, 128]], compare_op=ALU.is_ge,
                                                fill=0.0, base=0, channel_multiplier=-1), [cB], [cB])
        P.cp("pool", triu_bf, triu, [cB], [cB])
        P.memset("pool", eps_t, EPS, [cB])
        P.memset("pool", one_t, 1.0, [cB])
        P.memset("pool", npi_t, -np.pi, [cB])
        P.memset("pool", mbm, 0.0, [cB])
        P.memset("pool", mbf, 0.0, [cB])
        for j in range(4):
            if j > 0:
                P.memset("pool", mbm[:, j, 0:j * 128], NEG, [cB])
                P.memset("pool", mbf[:, j, 0:j * 128], NEG, [cB])
            P.memset("pool", mbm[64:128, j, j * 128:j * 128 + 64], NEG, [cB])
            blk = mbf[:, j, j * 128:(j + 1) * 128]
            P.op("pool", (lambda blk: lambda e: e.affine_select(out=blk, in_=blk, pattern=[[1, 128]], compare_op=ALU.is_ge,
                                                                 fill=NEG, base=0, channel_multiplier=-1))(blk), [cB], [cB])
        P.persist_top = P.top
        if os.environ.get('K_STOP', '') != 'c1':
          if True:
            rope = P.alloc([NT, 128], F32)
            posi = P.alloc([NT], I32)
            posf = P.alloc([NT], F32)
            invf = P.alloc([16], F32)
            ang = P.alloc([NT, 16], F32)
            ang2 = P.alloc([NT, 16], F32)
            sn = P.alloc([NT, 16], F32)
            csn = P.alloc([NT, 16], F32)
            P.op("pool", lambda e: e.iota(posi, pattern=[[128, NT]], base=0, channel_multiplier=1), [], [cB])
            P.cp("pool", posf, posi, [cB], [cB])
            for i in range(16):
                P.memset("pool", invf[:, i:i + 1], float(np.float32(10000.0) ** np.float32(-(2.0 * i) / 32.0)), [cB])
            for t in range(NT):
                P.ts("dve", ang[:, t, :], invf, posf[:, t:t + 1], None, ALU.mult, None, [cB], [cB])
            kint = P.alloc([NT, 16], I32)
            kf = P.alloc([NT, 16], F32)
            TWO_PI = float(2 * np.pi)

            def sin_of(dst, shift):
                if shift:
                    P.ts("dve", ang2, ang, float(shift), None, ALU.add, None, [cB], [cB])
                    src = ang2
                else:
                    src = ang
                P.ts("dve", kf, src, 1.0 / TWO_PI, None, ALU.mult, None, [cB], [cB])
                P.cp("dve", kint, kf, [cB], [cB])
                P.cp("dve", kf, kint, [cB], [cB])
                P.op("dve", lambda e: e.scalar_tensor_tensor(out=kf, in0=kf, scalar=-TWO_PI, in1=src, op0=ALU.mult, op1=ALU.add), [cB], [cB])
                P.ts("dve", dst, kf, float(np.pi), TWO_PI, ALU.is_gt, ALU.mult, [cB], [cB])
                P.tt("dve", kf, kf, dst, ALU.subtract, [cB], [cB])
                P.ts("dve", dst, kf, -float(np.pi), TWO_PI, ALU.is_lt, ALU.mult, [cB], [cB])
                P.tt("dve", kf, kf, dst, ALU.add, [cB], [cB])
                P.act(dst, kf, AF.Sin, [cB], [cB])

            sin_of(sn, 0.0)
            sin_of(csn, np.pi / 2)
            for hh in range(4):
                P.cp("pool", rope[:, :, hh * 16:(hh + 1) * 16], csn, [cB], [cB])
                P.cp("pool", rope[:, :, 64 + hh * 16:64 + (hh + 1) * 16], sn, [cB], [cB])
            rpB = P.dbuf("ropeout")
            P.dma("sp", ropeD.rearrange("(t p) c -> p t c", p=128), rope, [cB], [], sem=rpB)
        P.barrier()
        STOP = os.environ.get('K_STOP', '')

        def load_weight(dst, src, K, segs, gain=None, stg=None, stgB=None, wB=None, engs=("dve", "act")):
            i = 0
            for c in range(K // 128):
                for (s0, n, d0) in segs:
                    slot = i % len(stg)
                    P.dma("sp", stg[slot][:, 0:n], src[c * 128:(c + 1) * 128, s0:s0 + n], [], [stgB[slot]], sem=stgB[slot])
                    eng = engs[i % len(engs)]
                    o = dst[:, c, d0:d0 + n]
                    src_t = stg[slot][:, 0:n]
                    if gain is not None:
                        if eng == "act":
                            P.act(o, src_t, AF.Copy, [stgB[slot]], [wB], scale=gain[:, c:c + 1])
                        else:
                            P.ts("dve", o, src_t, gain[:, c:c + 1], None, ALU.mult, None, [stgB[slot]], [wB])
                    else:
                        P.cp(eng, o, src_t, [stgB[slot]], [wB])
                    i += 1

        def load_gain(dst, src_vec, K, gB):
            if K >= 128:
                P.dma("sp", dst, src_vec.rearrange("(c p) -> p c", p=128), [], [gB], sem=gB)
            return dst

        def rms_rstd(ss, n, dim, rB):
            P.act(ss, ss, AF.Ln, [rB], [rB], bias=eps_t[:, 0:1], scale=1.0 / dim)
            P.act(ss, ss, AF.Exp, [rB], [rB], scale=-0.5)

        for l in range(n_layers if STOP not in ('consts', 'c1') else 0):
            x_src = x_in if l == 0 else xres
            stg = [P.alloc([2048], F32) for _ in range(2)]
            stgB = [P.dbuf("stg%d" % i) for i in range(2)]
            gB = P.dbuf("gains")
            g_mix = P.alloc([8], F32)
            g_q = P.alloc([2], F32)
            g_kv = P.alloc([1], F32)
            fb = P.alloc([4], F32)
            P.dma("sp", g_mix, w["norm_mix"][l].rearrange("(c p) -> p c", p=128), [], [gB], sem=gB, slow=True)
            P.dma("sp", g_q, w["q_norm"][l].rearrange("(c p) -> p c", p=128), [], [gB], sem=gB, slow=True)
            P.dma("sp", g_kv, w["kv_norm"][l].rearrange("(c p) -> p c", p=128), [], [gB], sem=gB, slow=True)
            P.dma("sp", fb, w["f_bias"][l].partition_broadcast(128), [], [gB], sem=gB)
            Win = P.alloc([8, 1956], BF16)
            Wuq = P.alloc([2, 384], BF16)
            Wukv = P.alloc([1, 768], BF16)
            wB = Buf("wP")
            segs_in = [(0, 416, 0), (1184, 4, 416), (416, 512, 420), (928, 256, 932), (1700, 256, 1188),
                       (1188, 512, 1444)]
            load_weight(Win, w["w_in"][l], D, segs_in, gain=g_mix, stg=stg, stgB=stgB, wB=wB)
            segs_uq = []
            for h in range(4):
                segs_uq += [(h * 96, 64, h * 64), (h * 96 + 64, 16, 256 + h * 16), (h * 96 + 80, 16, 320 + h * 16)]
            load_weight(Wuq, w["w_uq"][l], 256, segs_uq, gain=g_q, stg=stg, stgB=stgB, wB=wB)
            segs_ukv = []
            for h in range(4):
                segs_ukv += [(h * 192, 64, h * 64), (h * 192 + 64, 128, 256 + h * 128)]
            load_weight(Wukv, w["w_ukv"][l], 128, segs_ukv, gain=g_kv, stg=stg, stgB=stgB, wB=wB)

            if STOP == 'weights':
                break
            NX = 3
            xs = [P.alloc([D], F32) for _ in range(NX)]
            xsB = [P.dbuf("xs%d" % i) for i in range(NX)]
            rps = [P.alloc([128], F32) for _ in range(NX)]
            junk = P.alloc([D], BF16)
            junkB = Buf("junk")
            st = [P.alloc([8], F32) for _ in range(2)]
            stB = [Buf("st%d" % i) for i in range(2)]
            hb = [P.alloc([D], BF16) for _ in range(2)]
            hbB = [Buf("hb%d" % i) for i in range(2)]
            hT = [P.alloc([D], BF16) for _ in range(2)]
            hTB = [Buf("hT%d" % i) for i in range(2)]
            cn = [P.alloc([384], BF16) for _ in range(2)]
            cnB = [Buf("cn%d" % i) for i in range(2)]
            cT = [P.alloc([384], BF16) for _ in range(2)]
            cTB = [Buf("cT%d" % i) for i in range(2)]
            tmp = [P.alloc([6, 64], F32) for _ in range(2)]
            tmpB = [Buf("tmp%d" % i) for i in range(2)]
            Qm = [P.alloc([4, 96], BF16) for _ in range(2)]
            Km = [P.alloc([4, 96], BF16) for _ in range(2)]
            Qf = [P.alloc([4, 70], BF16) for _ in range(2)]
            Kf = [P.alloc([4, 70], BF16) for _ in range(2)]
            Qc = [P.alloc([256], BF16) for _ in range(2)]
            Kc = [P.alloc([256], BF16) for _ in range(2)]
            tmB = [Buf("tm%d" % i) for i in range(2)]
            fx = [P.alloc([24], F32) for _ in range(2)]
            fxh = [P.alloc([3, 4], BF16) for _ in range(2)]
            fxs = [P.alloc([3, 4], BF16) for _ in range(2)]
            fxB = [Buf("fx%d" % i) for i in range(2)]
            tot = P.alloc([4], F32)
            totB = Buf("tot")
            P.memset("pool", tot, 0.0, [totB])
            for i in range(2):
                P.memset("pool", Qf[i][:, :, 67:70], 1.0, [tmB[i]])
                P.memset("pool", Kf[i][:, :, 64:67], 1.0, [tmB[i]])
            sQm = [P.alloc([4, 512], BF16) for _ in range(2)]
            sKm = [P.alloc([4, 512], BF16) for _ in range(2)]
            sQf = [P.alloc([4, 512], BF16) for _ in range(2)]
            sKf = [P.alloc([4, 512], BF16) for _ in range(2)]
            sQc = [P.alloc([2, 512], BF16) for _ in range(2)]
            sKc = [P.alloc([2, 512], BF16) for _ in range(2)]
            sV = [P.alloc([4, 1024], BF16) for _ in range(2)]
            sB = [P.dbuf("stage%d" % i) for i in range(2)]

            for t in range(NT):
                G, tt_ = divmod(t, 4)
                gs = G % 2
                i2 = t % 2
                xi = t % NX
                P.dma("sp", xs[xi], x_src[t * 128:(t + 1) * 128, :], [], [xsB[xi]], sem=xsB[xi])
                P.dma("sp", rps[xi], ropeD[t * 128:(t + 1) * 128, :], [], [xsB[xi]], sem=xsB[xi])
                P.act(junk, xs[xi], AF.Square, [xsB[xi]], [junkB, stB[i2]], accum=st[i2][:, 0:1])
                rms_rstd(st[i2][:, 0:1], 1, D, stB[i2])
                P.ts("dve", hb[i2], xs[xi], st[i2][:, 0:1], None, ALU.mult, None, [xsB[xi], stB[i2]], [hbB[i2]])
                bt, bb = P.bank()
                bv = bf16_view(bt)
                for c in range(8):
                    P.tr(bv[:, c * 128:(c + 1) * 128], hb[i2][:, c * 128:(c + 1) * 128], ident, [hbB[i2]], [bb])
                P.cp("dve", hT[i2], bv, [bb], [hTB[i2]])
                chunks = [(0, 420), (420, 512), (932, 512), (1444, 512)]
                pb = []
                for (c0, n) in chunks:
                    bt2, bb2 = P.bank()
                    for c in range(8):
                        P.mm(bt2[:, 0:n], hT[i2][:, c * 128:(c + 1) * 128], Win[:, c, c0:c0 + n], c == 0, c == 7,
                             [hTB[i2], wB], [bb2])
                    pb.append((bt2, bb2))
                (b0, b0B), (b1, b1B), (b2, b2B), (b3, b3B) = pb
                s2 = st[i2]
                P.act(junk[:, 0:256], b0[:, 0:256], AF.Square, [b0B], [junkB, stB[i2]], accum=s2[:, 1:2])
                P.act(junk[:, 0:128], b0[:, 256:384], AF.Square, [b0B], [junkB, stB[i2]], accum=s2[:, 2:3])
                P.act(s2[:, 1:2], s2[:, 1:2], AF.Ln, [stB[i2]], [stB[i2]], bias=eps_t[:, 0:1], scale=1.0 / 256)
                P.act(s2[:, 2:3], s2[:, 2:3], AF.Ln, [stB[i2]], [stB[i2]], bias=eps_t[:, 0:1], scale=1.0 / 128)
                P.act(s2[:, 1:3], s2[:, 1:3], AF.Exp, [stB[i2]], [stB[i2]], scale=-0.5)
                P.ts("dve", cn[i2][:, 0:256], b0[:, 0:256], s2[:, 1:2], None, ALU.mult, None, [b0B, stB[i2]], [cnB[i2]])
                P.ts("dve", cn[i2][:, 256:384], b0[:, 256:384], s2[:, 2:3], None, ALU.mult, None, [b0B, stB[i2]], [cnB[i2]])
                bt3, bb3 = P.bank()
                bv3 = bf16_view(bt3)
                for c in range(3):
                    P.tr(bv3[:, c * 128:(c + 1) * 128], cn[i2][:, c * 128:(c + 1) * 128], ident, [cnB[i2]], [bb3])
                P.cp("dve", cT[i2], bv3[:, 0:384], [bb3], [cTB[i2]])
                bq, bqB = P.bank()
                for c in range(2):
                    P.mm(bq[:, 0:384], cT[i2][:, c * 128:(c + 1) * 128], Wuq[:, c, :], c == 0, c == 1, [cTB[i2], wB], [bqB])
                bk, bkB = P.bank()
                P.mm(bk[:, 0:256], cT[i2][:, 256:384], Wukv[:, 0, 0:256], True, True, [cTB[i2], wB], [bkB])
                bvv, bvB = P.bank()
                P.mm(bvv[:, 0:512], cT[i2][:, 256:384], Wukv[:, 0, 256:768], True, True, [cTB[i2], wB], [bvB])
                cos4 = rps[xi][:, 0:64]
                sin4 = rps[xi][:, 64:128]
                tm_ = tmp[i2]
                qv = Qm[i2]
                kv_ = Km[i2]
                P.tt("dve", tm_[:, 0, :], bq[:, 256:320], cos4, ALU.mult, [bqB, xsB[xi]], [tmpB[i2]])
                P.tt("dve", tm_[:, 1, :], bq[:, 320:384], sin4, ALU.mult, [bqB, xsB[xi]], [tmpB[i2]])
                P.tt("dve", tm_[:, 2, :], bq[:, 256:320], sin4, ALU.mult, [bqB, xsB[xi]], [tmpB[i2]])
                P.tt("dve", tm_[:, 3, :], bq[:, 320:384], cos4, ALU.mult, [bqB, xsB[xi]], [tmpB[i2]])
                P.tt("dve", qv[:, :, 64:80], tm_[:, 0, :].rearrange("p (h d) -> p h d", h=4),
                     tm_[:, 1, :].rearrange("p (h d) -> p h d", h=4), ALU.subtract, [tmpB[i2]], [tmB[i2]])
                P.tt("dve", qv[:, :, 80:96], tm_[:, 2, :].rearrange("p (h d) -> p h d", h=4),
                     tm_[:, 3, :].rearrange("p (h d) -> p h d", h=4), ALU.add, [tmpB[i2]], [tmB[i2]])
                P.cp("act", qv[:, :, 0:64], bq[:, 0:256].rearrange("p (h d) -> p h d", h=4), [bqB], [tmB[i2]])
                P.tt("dve", tm_[:, 4, 0:16], b0[:, 384:400], cos4[:, 0:16], ALU.mult, [b0B, xsB[xi]], [tmpB[i2]])
                P.tt("dve", tm_[:, 4, 16:32], b0[:, 400:416], sin4[:, 0:16], ALU.mult, [b0B, xsB[xi]], [tmpB[i2]])
                P.tt("dve", tm_[:, 4, 32:48], b0[:, 384:400], sin4[:, 0:16], ALU.mult, [b0B, xsB[xi]], [tmpB[i2]])
                P.tt("dve", tm_[:, 4, 48:64], b0[:, 400:416], cos4[:, 0:16], ALU.mult, [b0B, xsB[xi]], [tmpB[i2]])
                P.tt("dve", tm_[:, 5, 0:16], tm_[:, 4, 0:16], tm_[:, 4, 16:32], ALU.subtract, [tmpB[i2]], [tmpB[i2]])
                P.tt("dve", tm_[:, 5, 16:32], tm_[:, 4, 32:48], tm_[:, 4, 48:64], ALU.add, [tmpB[i2]], [tmpB[i2]])
                for h in range(4):
                    P.cp("pool", kv_[:, h, 64:96], tm_[:, 5, 0:32], [tmpB[i2]], [tmB[i2]])
                P.cp("act", kv_[:, :, 0:64], bk[:, 0:256].rearrange("p (h d) -> p h d", h=4), [bkB], [tmB[i2]])
                P.cp("act", sV[gs][:, tt_, 0:512], bvv[:, 0:512], [bvB], [sB[gs]])
                f = fx[i2]
                P.tt("dve", f[:, 0:4], b0[:, 416:420], fb, ALU.add, [b0B, gB], [fxB[i2]])
                P.act(f[:, 0:4], f[:, 0:4], AF.Exp, [fxB[i2]], [fxB[i2]], scale=-1.0)
                P.act(f[:, 4:8], f[:, 0:4], AF.Ln, [fxB[i2]], [fxB[i2]], bias=one_t[:, 0:1], scale=1.0)
                spb = fxs[i2]
                P.cp("dve", spb[:, 0, :], f[:, 4:8], [fxB[i2]], [fxB[i2]])
                P.tt("dve", f[:, 12:16], f[:, 4:8], spb[:, 0, :], ALU.subtract, [fxB[i2]], [fxB[i2]])
                P.cp("dve", spb[:, 1, :], f[:, 12:16], [fxB[i2]], [fxB[i2]])
                P.tt("dve", f[:, 16:20], f[:, 12:16], spb[:, 1, :], ALU.subtract, [fxB[i2]], [fxB[i2]])
                P.cp("dve", spb[:, 2, :], f[:, 16:20], [fxB[i2]], [fxB[i2]])
                bc, bcB = P.bank()
                for pc in range(3):
                    P.mm(bc[:, 0:4], triu_bf, spb[:, pc, :], pc == 0, pc == 2, [fxB[i2], cB], [bcB])
                for pc in range(3):
                    P.mm(bc[:, 8:12], ones_bf, spb[:, pc, :], pc == 0, pc == 2, [fxB[i2], cB], [bcB])
                P.tt("dve", f[:, 8:12], bc[:, 0:4], tot, ALU.add, [bcB, totB], [fxB[i2]])
                P.tt("dve", tot, bc[:, 8:12], tot, ALU.add, [bcB, totB], [totB])
                ph = fxh[i2]
                P.cp("dve", ph[:, 0, :], f[:, 8:12], [fxB[i2]], [fxB[i2]])
                P.tt("dve", f[:, 12:16], f[:, 8:12], ph[:, 0, :], ALU.subtract, [fxB[i2]], [fxB[i2]])
                P.cp("dve", ph[:, 1, :], f[:, 12:16], [fxB[i2]], [fxB[i2]])
                P.tt("dve", f[:, 16:20], f[:, 12:16], ph[:, 1, :], ALU.subtract, [fxB[i2]], [fxB[i2]])
                P.cp("dve", ph[:, 2, :], f[:, 16:20], [fxB[i2]], [fxB[i2]])
                for pc in range(3):
                    P.cp("pool", Kf[i2][:, :, 67 + pc:68 + pc], ph[:, pc, :].rearrange("p (h o) -> p h o", o=1),
                         [fxB[i2]], [tmB[i2]])
                    P.ts("dve", Qf[i2][:, :, 64 + pc:65 + pc], ph[:, pc, :].rearrange("p (h o) -> p h o", o=1),
                         -1.0, None, ALU.mult, None, [fxB[i2]], [tmB[i2]])
                P.ts("dve", Qf[i2][:, :, 0:64], b1[:, 0:256].rearrange("p (h d) -> p h d", h=4), 0.125, None, ALU.mult, None,
                     [b1B], [tmB[i2]])
                P.cp("act", Kf[i2][:, :, 0:64], b1[:, 256:512].rearrange("p (h d) -> p h d", h=4), [b1B], [tmB[i2]])
                P.cp("act", sV[gs][:, tt_, 512:1024], b2[:, 0:512], [b2B], [sB[gs]])
                P.ts("dve", Qc[i2], b3[:, 0:256], 0.125, None, ALU.mult, None, [b3B], [tmB[i2]])
                P.cp("act", Kc[i2], b3[:, 256:512], [b3B], [tmB[i2]])
                cs = slice(tt_ * 128, (tt_ + 1) * 128)
                for (src, dst, rows) in ((Qm[i2], sQm[gs], 96), (Km[i2], sKm[gs], 96), (Qf[i2], sQf[gs], 70), (Kf[i2], sKf[gs], 70)):
                    btx, bbx = P.bank()
                    bvx = bf16_view(btx)
                    for h in range(4):
                        P.tr(bvx[0:rows, h * 128:(h + 1) * 128], src[:, h, :], ident, [tmB[i2]], [bbx])
                    P.cp("dve" if rows == 96 else "act", dst[0:rows, :, cs], bvx[0:rows, 0:512].rearrange("p (h t) -> p h t", h=4),
                         [bbx], [sB[gs]])
                btx, bbx = P.bank()
                bvx = bf16_view(btx)
                for c in range(2):
                    P.tr(bvx[:, c * 128:(c + 1) * 128], Qc[i2][:, c * 128:(c + 1) * 128], ident, [tmB[i2]], [bbx])
                    P.tr(bvx[:, (2 + c) * 128:(3 + c) * 128], Kc[i2][:, c * 128:(c + 1) * 128], ident, [tmB[i2]], [bbx])
                P.cp("dve", sQc[gs][:, :, cs], bvx[:, 0:256].rearrange("p (h t) -> p h t", h=2), [bbx], [sB[gs]])
                P.cp("act", sKc[gs][:, :, cs], bvx[:, 256:512].rearrange("p (h t) -> p h t", h=2), [bbx], [sB[gs]])
                if tt_ == 3:
                    gsl = slice(G * 512, (G + 1) * 512)
                    q = "sp"
                    for h in range(4):
                        P.dma(q, qTm[h, :, gsl], sQm[gs][0:96, h, :], [sB[gs]], [], sem=sB[gs])
                        P.dma(q, kTm[h, :, gsl], sKm[gs][0:96, h, :], [sB[gs]], [], sem=sB[gs])
                        P.dma(q, qTf[h, :, gsl], sQf[gs][0:70, h, :], [sB[gs]], [], sem=sB[gs])
                        P.dma(q, kTf[h, :, gsl], sKf[gs][0:70, h, :], [sB[gs]], [], sem=sB[gs])
                    for c in range(2):
                        P.dma(q, qTc[c * 128:(c + 1) * 128, gsl], sQc[gs][:, c, :], [sB[gs]], [], sem=sB[gs])
                        P.dma(q, kTc[c * 128:(c + 1) * 128, gsl], sKc[gs][:, c, :], [sB[gs]], [], sem=sB[gs])
                    rows = slice(G * 512, (G + 1) * 512)
                    P.dma(q, vm[rows, :].rearrange("(t p) c -> p t c", p=128), sV[gs][:, :, 0:512], [sB[gs]], [], sem=sB[gs])
                    P.dma(q, vf[rows, :].rearrange("(t p) c -> p t c", p=128), sV[gs][:, :, 512:768], [sB[gs]], [], sem=sB[gs])
                    P.dma(q, vc[rows, :].rearrange("(t p) c -> p t c", p=128), sV[gs][:, :, 768:1024], [sB[gs]], [], sem=sB[gs])
            P.barrier()

            P.rot = [4, 5, 6, 7]
            slots = []
            for i in range(2):
                slots.append((P.alloc([S], BF16), P.alloc([S], BF16), P.alloc([NT, 128], BF16), P.dbuf("kv%d" % i)))
            PT = [P.alloc([512], BF16) for _ in range(3)]
            PTB = [Buf("pt%d" % i) for i in range(3)]
            rden = [P.alloc([512], F32) for _ in range(2)]
            rdB = [Buf("rd%d" % i) for i in range(2)]
            ost = [P.alloc([512], BF16) for _ in range(2)]
            ostB = [P.dbuf("ost%d" % i) for i in range(2)]
            BB = P.alloc([4, 8, 512], BF16)
            bbB = Buf("BB")
            rstg = P.alloc([5, 128], F32)
            rsB = P.dbuf("rstg")
            P.memset("pool", BB, NEG, [bbB])
            for h in range(4):
                P.dma("sp", rstg, w["relT"][l, h], [], [rsB], sem=rsB)
                P.memset("pool", rstg[64:128, 0, 0:64], NEG, [rsB])
                P.memset("pool", rstg[0:64, 4, 64:128], NEG, [rsB])
                for j in range(8):
                    for qt in range(4):
                        dl = qt + 4 - j
                        if 0 <= dl <= 4:
                            P.cp("pool" if (j + qt) % 2 else "dve", BB[:, h, j, qt * 128:(qt + 1) * 128], rstg[:, dl, :], [rsB], [bbB])
            cnt = {"pt": 0, "o": 0, "hd": 0}

            def load_head(kT_src, qT_src, rows, v_src, vc0, vw):
                KT, QT, V, kvB = slots[cnt["hd"] % 2]
                cnt["hd"] += 1
                P.dma("sp", KT[0:rows, :], kT_src, [], [kvB], sem=kvB)
                P.dma("sp", QT[0:rows, :], qT_src, [], [kvB], sem=kvB)
                vr = v_src.rearrange("(t p) c -> p t c", p=128)
                for t0 in range(0, NT, 8):
                    t1 = min(NT, t0 + 8)
                    P.dma("sp", V[:, t0:t1, 0:vw], vr[:, t0:t1, vc0:vc0 + vw], [], [kvB], sem=kvB)
                return KT, QT, V, kvB

            def run_head(hd, rows, kind, scale, yrow0, bias_tiles, biasB):
                KT, QT, V, kvB = hd
                for G in range(NG):
                    if kind == "chk":
                        kts = [(4 * G - 4 + j, j) for j in range(8) if 4 * G - 4 + j >= 0]
                    else:
                        kts = [(kt, (kt - 4 * G) if kt >= 4 * G else None) for kt in range(4 * G + 4)]
                    o2 = cnt["o"] % 2
                    cnt["o"] += 1
                    ob, obB = P.bank_at(2 * o2)
                    db, dbB = P.bank_at(2 * o2 + 1)
                    q = QT[0:rows, G * 512:(G + 1) * 512]
                    n = len(kts)

                    def pv(i, ptb, ptB):
                        kt = kts[i][0]
                        P.mm(ob[:, :], V[:, kt, :], ptb, i == 0, i == n - 1, [kvB, ptB], [obB])
                        if kind == "mla":
                            P.mm(db[:, :], ones_bf, ptb, i == 0, i == n - 1, [cB, ptB], [dbB])

                    prev = None
                    for i, (kt, j) in enumerate(kts):
                        sb_, sbB = P.bank()
                        P.mm(sb_[:, :], KT[0:rows, kt * 128:(kt + 1) * 128], q, True, j is None, [kvB], [sbB])
                        if j is not None:
                            P.mm(sb_[:, :], ident, bias_tiles[j], False, True, [cB, biasB], [sbB])
                        sl = cnt["pt"] % 3
                        cnt["pt"] += 1
                        P.act(PT[sl], sb_[:, :], AF.Exp, [sbB], [PTB[sl]], scale=scale)
                        if prev is not None:
                            pv(*prev)
                        prev = (i, PT[sl], PTB[sl])
                    pv(*prev)
                    gsl = slice(G * 512, (G + 1) * 512)
                    if kind == "mla":
                        P.op("dve", lambda e, a=rden[o2], b=db: e.reciprocal(out=a, in_=b[:, :]), [dbB], [rdB[o2]])
                        P.tt("dve", ost[o2], ob[:, :], rden[o2], ALU.mult, [obB, rdB[o2]], [ostB[o2]])
                        P.dma("sp", ymixT[yrow0:yrow0 + 128, gsl], ost[o2], [ostB[o2]], [], sem=ostB[o2])
                    else:
                        P.op("dve", lambda e, a=rden[o2], b=ob: e.reciprocal(out=a[0:64, :], in_=b[64:128, :]), [obB], [rdB[o2]])
                        P.tt("dve", ost[o2][0:64, :], ob[0:64, :], rden[o2][0:64, :], ALU.mult, [obB, rdB[o2]], [ostB[o2]])
                        P.dma("sp", ymixT[yrow0:yrow0 + 64, gsl], ost[o2][0:64, :], [ostB[o2]], [], sem=ostB[o2])

            mla_scale = float(96 ** -0.5)
            for h in range(4):
                hd = load_head(kTm[h], qTm[h], 96, vm, h * 128, 128)
                run_head(hd, 96, "mla", mla_scale, h * 128, [mbm[:, j, :] for j in range(4)], cB)
            for i in range(2):
                P.memset("pool", slots[i][2][:, :, 64:128], 1.0, [slots[i][3]])
            for h in range(4):
                hd = load_head(kTf[h], qTf[h], 70, vf, h * 64, 64)
                run_head(hd, 70, "fox", 1.0, 512 + h * 64, [mbf[:, j, :] for j in range(4)], cB)
            for h in range(4):
                hd = load_head(kTc[h * 64:(h + 1) * 64, :], qTc[h * 64:(h + 1) * 64, :], 64, vc, h * 64, 64)
                run_head(hd, 64, "chk", 1.0, 768 + h * 64, [BB[:, h, j, :] for j in range(8)], bbB)
            P.barrier()

            P.rot = list(range(8))
            stg = [P.alloc([2048], F32) for _ in range(2)]
            stgB = [P.dbuf("stg%d" % i) for i in range(2)]
            gB = P.dbuf("gains")
            g_o = P.alloc([8], F32)
            g_c = P.alloc([8], F32)
            g_m = P.alloc([8], F32)
            P.dma("sp", g_o, w["out_norm"][l].rearrange("(c p) -> p c", p=128), [], [gB], sem=gB, slow=True)
            P.dma("sp", g_c, w["norm_cross"][l].rearrange("(c p) -> p c", p=128), [], [gB], sem=gB, slow=True)
            P.dma("sp", g_m, w["norm_mem"][l].rearrange("(c p) -> p c", p=128), [], [gB], sem=gB, slow=True)
            Wo = P.alloc([8, 1024], BF16)
            Wcq = P.alloc([8, 512], BF16)
            Wco = P.alloc([4, 1024], BF16)
            Wckv = P.alloc([8, 1024], BF16)
            wB = Buf("wO1")
            load_weight(Wo, w["w_o"][l], D, [(0, 1024, 0)], gain=g_o, stg=stg, stgB=stgB, wB=wB)
            load_weight(Wcq, w["w_cq"][l], D, [(0, 512, 0)], gain=g_c, stg=stg, stgB=stgB, wB=wB)
            load_weight(Wco, w["w_co"][l], 512, [(0, 1024, 0)], stg=stg, stgB=stgB, wB=wB)
            load_weight(Wckv, w["w_ckv"][l], D, [(0, 1024, 0)], gain=g_m, stg=stg, stgB=stgB, wB=wB)
            junk = P.alloc([D], BF16)
            junkB = Buf("junk")
            memT = P.alloc([8, 256], BF16)
            KcT = P.alloc([4, 256], BF16)
            Vc = P.alloc([2, 512], BF16)
            mB = Buf("mem")
            mx = P.alloc([D], F32)
            mxB = P.dbuf("mx")
            mst = P.alloc([2], F32)
            mhb = P.alloc([D], BF16)
            for mt in range(2):
                P.dma("sp", mx, mem_in[mt * 128:(mt + 1) * 128, :], [], [mxB], sem=mxB)
                P.act(junk, mx, AF.Square, [mxB], [junkB, mB], accum=mst[:, 0:1])
                rms_rstd(mst[:, 0:1], 1, D, mB)
                P.ts("dve", mhb, mx, mst[:, 0:1], None, ALU.mult, None, [mxB, mB], [mB])
                bt, bb = P.bank()
                bv = bf16_view(bt)
                for c in range(8):
                    P.tr(bv[:, c * 128:(c + 1) * 128], mhb[:, c * 128:(c + 1) * 128], ident, [mB], [bb])
                P.cp("dve", memT[:, :, mt * 128:(mt + 1) * 128], bv.rearrange("p (c t) -> p c t", c=8), [bb], [mB])
            for h in range(4):
                bt, bb = P.bank()
                for c in range(8):
                    P.mm(bt[:, 0:256], Wckv[:, c, h * 128:(h + 1) * 128], memT[:, c, :], c == 0, c == 7, [wB, mB], [bb])
                P.cp("act", KcT[:, h, :], bt[:, 0:256], [bb], [mB])
            for mt in range(2):
                bt, bb = P.bank()
                for c in range(8):
                    P.mm(bt[:, :], memT[:, c, mt * 128:(mt + 1) * 128], Wckv[:, c, 512:1024], c == 0, c == 7, [wB, mB], [bb])
                P.cp("act", Vc[:, mt, :], bt[:, :], [bb], [mB])

            YT = [P.alloc([8, 512], BF16) for _ in range(2)]
            YTB = [P.dbuf("yt%d" % i) for i in range(2)]
            xg = [P.alloc([4, D], F32) for _ in range(2)]
            xgB = [P.dbuf("xg%d" % i) for i in range(2)]
            sq = P.alloc([8, 512], BF16)
            sqB = Buf("sq")
            rr = P.alloc([3, 512], F32)
            rrB = Buf("rr")
            YN = P.alloc([8, 512], BF16)
            YNB = Buf("yn")
            st1 = P.alloc([4], F32)
            st1B = Buf("st1")
            hb1 = P.alloc([D], BF16)
            hb1B = Buf("hb1")
            hcT = P.alloc([8, 512], BF16)
            hcTB = Buf("hcT")
            qcT = P.alloc([4, 512], BF16)
            qcTB = Buf("qcT")
            PT = [P.alloc([512], BF16) for _ in range(3)]
            PTB = [Buf("pt%d" % i) for i in range(3)]
            ocT = P.alloc([4, 512], BF16)
            ocTB = Buf("ocT")
            rdn = P.alloc([512], F32)
            rdnB = Buf("rdn")
            pti = 0
            c_scale = float(128 ** -0.5)
            x_o1 = x_in if l == 0 else xres
            for G in range(NG):
                g2 = G % 2
                gsl = slice(G * 512, (G + 1) * 512)
                P.dma("sp", YT[g2], ymixT[:, gsl].rearrange("(c p) s -> p c s", p=128), [], [YTB[g2]], sem=YTB[g2])
                P.dma("sp", xg[g2], x_o1[gsl, :].rearrange("(t p) d -> p t d", p=128), [], [xgB[g2]], sem=xgB[g2])
                P.tt("pool", sq, YT[g2], YT[g2], ALU.mult, [YTB[g2]], [sqB])
                for gi, (c0, c1, wdt) in enumerate(((0, 4, 512), (4, 6, 256), (6, 8, 256))):
                    bt, bb = P.bank()
                    for c in range(c0, c1):
                        P.mm(bt[:, :], ones_bf, sq[:, c, :], c == c0, c == c1 - 1, [cB, sqB], [bb])
                    P.act(rr[:, gi, :], bt[:, :], AF.Ln, [bb], [rrB], bias=eps_t[:, 0:1], scale=1.0 / wdt)
                    P.act(rr[:, gi, :], rr[:, gi, :], AF.Exp, [rrB], [rrB], scale=-0.5)
                    for c in range(c0, c1):
                        P.tt("dve", YN[:, c, :], YT[g2][:, c, :], rr[:, gi, :], ALU.mult, [YTB[g2], rrB], [YNB])
                for t in range(4):
                    ts_ = slice(t * 128, (t + 1) * 128)
                    for half in range(2):
                        hs = slice(half * 512, (half + 1) * 512)
                        bt, bb = P.bank()
                        for c in range(8):
                            P.mm(bt[:, :], YN[:, c, ts_], Wo[:, c, hs], c == 0, c == 7, [YNB, wB], [bb])
                        P.tt("dve", xg[g2][:, t, hs], bt[:, :], xg[g2][:, t, hs], ALU.add, [bb, xgB[g2]], [xgB[g2]])
                    P.act(junk, xg[g2][:, t, :], AF.Square, [xgB[g2]], [junkB, st1B], accum=st1[:, t:t + 1])
                    rms_rstd(st1[:, t:t + 1], 1, D, st1B)
                    P.ts("dve", hb1, xg[g2][:, t, :], st1[:, t:t + 1], None, ALU.mult, None, [xgB[g2], st1B], [hb1B])
                    bt, bb = P.bank()
                    bv = bf16_view(bt)
                    for c in range(8):
                        P.tr(bv[:, c * 128:(c + 1) * 128], hb1[:, c * 128:(c + 1) * 128], ident, [hb1B], [bb])
                    P.cp("act", hcT[:, :, ts_], bv.rearrange("p (c t) -> p c t", c=8), [bb], [hcTB])
                for h in range(4):
                    bt, bb = P.bank()
                    for c in range(8):
                        P.mm(bt[:, :], Wcq[:, c, h * 128:(h + 1) * 128], hcT[:, c, :], c == 0, c == 7, [wB, hcTB], [bb])
                    P.cp("act", qcT[:, h, :], bt[:, :], [bb], [qcTB])
                for h in range(4):
                    ob, obB = P.bank()
                    db, dbB = P.bank()
                    pts = []
                    for mt in range(2):
                        sb_, sbB = P.bank()
                        P.mm(sb_[:, :], KcT[:, h, mt * 128:(mt + 1) * 128], qcT[:, h, :], True, True, [mB, qcTB], [sbB])
                        sl = pti % 3
                        pti += 1
                        P.act(PT[sl], sb_[:, :], AF.Exp, [sbB], [PTB[sl]], scale=c_scale)
                        pts.append(sl)
                    for mt in range(2):
                        sl = pts[mt]
                        P.mm(ob[:, :], Vc[:, mt, h * 128:(h + 1) * 128], PT[sl], mt == 0, mt == 1, [mB, PTB[sl]], [obB])
                        P.mm(db[:, :], ones_bf, PT[sl], mt == 0, mt == 1, [cB, PTB[sl]], [dbB])
                    P.op("dve", lambda e, a=rdn, b=db: e.reciprocal(out=a, in_=b[:, :]), [dbB], [rdnB])
                    P.tt("dve", ocT[:, h, :], ob[:, :], rdn, ALU.mult, [obB, rdnB], [ocTB])
                for t in range(4):
                    ts_ = slice(t * 128, (t + 1) * 128)
                    for half in range(2):
                        hs = slice(half * 512, (half + 1) * 512)
                        bt, bb = P.bank()
                        for h in range(4):
                            P.mm(bt[:, :], ocT[:, h, ts_], Wco[:, h, hs], h == 0, h == 3, [ocTB, wB], [bb])
                        P.tt("dve", xg[g2][:, t, hs], bt[:, :], xg[g2][:, t, hs], ALU.add, [bb, xgB[g2]], [xgB[g2]])
                P.dma("sp", xres[gsl, :].rearrange("(t p) d -> p t d", p=128), xg[g2], [xgB[g2]], [], sem=xgB[g2])
            P.barrier()

            stg = [P.alloc([1024], F32) for _ in range(2)]
            stgB = [P.dbuf("stg%d" % i) for i in range(2)]
            gB = P.dbuf("gains")
            g_f = P.alloc([8], F32)
            P.dma("sp", g_f, w["norm_ffn"][l].rearrange("(c p) -> p c", p=128), [], [gB], sem=gB, slow=True)
            last = (l == n_layers - 1)
            if last:
                gfin = P.alloc([D], F32)
                P.dma("sp", gfin, w["final_norm"].partition_broadcast(128), [], [gB], sem=gB)
            Wgu = P.alloc([8, 2 * FFN], BF16)
            Wd = P.alloc([22, 1024], BF16)
            wB = Buf("wO2")
            load_weight(Wgu, w["w_gu"][l], D, [(i * 1024, min(1024, 2 * FFN - i * 1024), i * 1024) for i in range(6)], gain=g_f, stg=stg, stgB=stgB, wB=wB)
            load_weight(Wd, w["w_down"][l], FFN, [(0, 1024, 0)], stg=stg, stgB=stgB, wB=wB)
            junk = P.alloc([D], BF16)
            junkB = Buf("junk")
            xg = P.alloc([2, D], F32)
            xgB = P.dbuf("xg")
            st2 = P.alloc([8], F32)
            st2B = Buf("st2")
            hb2 = P.alloc([D], BF16)
            hb2B = Buf("hb2")
            hfT = P.alloc([8, 256], BF16)
            hfTB = Buf("hfT")
            sg = [P.alloc([256], BF16) for _ in range(2)]
            sgB = [Buf("sg%d" % i) for i in range(2)]
            aT = P.alloc([22, 256], BF16)
            aTB = Buf("aT")
            for G in range(S // 256):
                gsl = slice(G * 256, (G + 1) * 256)
                P.dma("sp", xg, xres[gsl, :].rearrange("(t p) d -> p t d", p=128), [], [xgB], sem=xgB)
                for t in range(2):
                    ts_ = slice(t * 128, (t + 1) * 128)
                    P.act(junk, xg[:, t, :], AF.Square, [xgB], [junkB, st2B], accum=st2[:, t:t + 1])
                    rms_rstd(st2[:, t:t + 1], 1, D, st2B)
                    P.ts("dve", hb2, xg[:, t, :], st2[:, t:t + 1], None, ALU.mult, None, [xgB, st2B], [hb2B])
                    bt, bb = P.bank()
                    bv = bf16_view(bt)
                    for c in range(8):
                        P.tr(bv[:, c * 128:(c + 1) * 128], hb2[:, c * 128:(c + 1) * 128], ident, [hb2B], [bb])
                    P.cp("dve", hfT[:, :, ts_], bv.rearrange("p (c t) -> p c t", c=8), [bb], [hfTB])
                for c2 in range(22):
                    bg, bgB = P.bank()
                    bu, buB = P.bank()
                    for c in range(8):
                        P.mm(bg[:, 0:256], Wgu[:, c, c2 * 128:(c2 + 1) * 128], hfT[:, c, :], c == 0, c == 7, [wB, hfTB], [bgB])
                    for c in range(8):
                        P.mm(bu[:, 0:256], Wgu[:, c, FFN + c2 * 128:FFN + (c2 + 1) * 128], hfT[:, c, :], c == 0, c == 7, [wB, hfTB], [buB])
                    s2_ = c2 % 2
                    P.act(sg[s2_], bg[:, 0:256], AF.Silu, [bgB], [sgB[s2_]])
                    P.tt("dve", aT[:, c2, :], bu[:, 0:256], sg[s2_], ALU.mult, [buB, sgB[s2_]], [aTB])
                for t in range(2):
                    ts_ = slice(t * 128, (t + 1) * 128)
                    for half in range(2):
                        hs = slice(half * 512, (half + 1) * 512)
                        bt, bb = P.bank()
                        for c2 in range(22):
                            P.mm(bt[:, :], aT[:, c2, ts_], Wd[:, c2, hs], c2 == 0, c2 == 21, [aTB, wB], [bb])
                        P.tt("dve", xg[:, t, hs], bt[:, :], xg[:, t, hs], ALU.add, [bb, xgB], [xgB])
                    rows = slice(G * 256 + t * 128, G * 256 + (t + 1) * 128)
                    if last:
                        P.act(junk, xg[:, t, :], AF.Square, [xgB], [junkB, st2B], accum=st2[:, 4 + t:5 + t])
                        rms_rstd(st2[:, 4 + t:5 + t], 1, D, st2B)
                        P.op("dve", lambda e, b=xg[:, t, :], c=st2[:, 4 + t:5 + t]: e.scalar_tensor_tensor(
                            out=b, in0=b, scalar=c, in1=gfin, op0=ALU.mult, op1=ALU.mult), [xgB, st2B, gB], [xgB])
                        P.dma("sp", out_d[rows, :], xg[:, t, :], [xgB], [], sem=xgB)
                if not last:
                    P.dma("sp", xres[gsl, :].rearrange("(t p) d -> p t d", p=128), xg, [xgB], [], sem=xgB)
            P.barrier()

        P.barrier()
        print('NREC', Prog.nrec)
        with nc.Block() as block:
            P.emit(block)
    return nc


def make_relT(rel_bias):
    ki = np.arange(128)[:, None, None]
    dl = np.arange(5)[None, :, None]
    qi = np.arange(128)[None, None, :]
    idx = np.clip(dl * 128 + qi - ki, -63, 128) + 63
    return np.ascontiguousarray(rel_bias[:, :, idx])


def kernel(**inputs):
    S = inputs["x"].shape[1]
    B = inputs["x"].shape[0]
    nc = build_program(S)
    relT = make_relT(np.asarray(inputs["rel_bias"], dtype=np.float32))
    NT = S // 128
    pos = (np.arange(NT, dtype=np.float32)[None, :] * 128 + np.arange(128, dtype=np.float32)[:, None])
    in_maps = []
    for b in range(B):
        m = {"x": np.ascontiguousarray(inputs["x"][b], dtype=np.float32),
             "mem": np.ascontiguousarray(inputs["mem"][b], dtype=np.float32), "relT": relT}
        for k, v in inputs.items():
            if k in ("x", "mem", "rel_bias"):
                continue
            m[k] = np.ascontiguousarray(v, dtype=np.float32)
        in_maps.append(m)
    res = run_bass_kernel_spmd(nc, in_maps, core_ids=list(range(B)))
    return np.stack([np.asarray(r["out"], dtype=np.float32) for r in res.results], axis=0)
```

```python
import os
import numpy as np
from contextlib import ExitStack
import concourse.bass as bass
import concourse.mybir as mybir
from concourse.bass_utils import run_bass_kernel_spmd

F32 = mybir.dt.float32
BF16 = mybir.dt.bfloat16
I32 = mybir.dt.int32
AF = mybir.ActivationFunctionType
ALU = mybir.AluOpType

D = 1024
NL = 2
EPS = 1e-6
FFN = 2816
NEG = -30000.0
COMPUTE = ("pe", "act", "dve", "pool")
ENGS = ("pe", "act", "dve", "pool", "sp")


class Buf:
    __slots__ = ("name", "w", "r", "sem")

    def __init__(self, name=""):
        self.name = name
        self.w = None
        self.r = {}
        self.sem = None


class Prog:
    def __init__(self, nc, es, arena_words):
        self.nc = nc
        self.es = es
        self.ops = {e: [] for e in ENGS}
        self.esem = {e: es.enter_context(nc.semaphore("s_" + e)) for e in COMPUTE}
        self.dsems = []
        self.free_ds = []
        self.arena = es.enter_context(nc.sbuf_tensor("arena", [128, arena_words], F32))
        self.arena_words = arena_words
        self.persist_top = 0
        self.top = 0
        self.banks = []
        for i in range(8):
            t = es.enter_context(nc.psum_tensor("bank%d" % i, [128, 512], F32))
            self.banks.append((t, Buf("bank%d" % i)))
        self.bank_i = 0

    def alloc(self, shape, dtype, parts=128):
        n = 1
        for s in shape:
            n *= s
        nbytes = n * (2 if dtype == BF16 else 4)
        words = (nbytes + 3) // 4
        words = (words + 7) // 8 * 8
        off = self.top
        self.top += words
        assert self.top <= self.arena_words, ("SBUF arena overflow", self.top * 4)
        v = self.arena[0:parts, off:off + (nbytes + 3) // 4]
        if dtype != F32:
            v = v.bitcast(dtype)
        if len(shape) == 2:
            v = v.rearrange("p (a b) -> p a b", a=shape[0])
        elif len(shape) == 3:
            v = v.rearrange("p (a b c) -> p a b c", a=shape[0], b=shape[1])
        return v

    rot = list(range(8))

    def bank(self):
        i = self.rot[self.bank_i % len(self.rot)]
        self.bank_i += 1
        return self.banks[i]

    def bank_at(self, i):
        return self.banks[i]

    def new_sem(self, buf):
        if self.free_ds:
            buf.sem = self.free_ds.pop()
        else:
            h = self.es.enter_context(self.nc.semaphore("d%d" % len(self.dsems)))
            self.dsems.append([h, 0, False])
            buf.sem = len(self.dsems) - 1
        return buf

    def dbuf(self, name=""):
        return self.new_sem(Buf(name))

    def _deps(self, eng, reads, writes):
        deps = {}

        def add(tok):
            if tok is None:
                return
            k, v = tok
            if k == "pe" and eng == "pe":
                return
            if not isinstance(k, str):
                ds = self.dsems[k[1]]
                v = ds[1]
                ds[2] = True
            if deps.get(k, -1) < v:
                deps[k] = v

        for b in reads:
            add(b.w)
        for b in writes:
            add(b.w)
            for t in b.r.values():
                add(t)
        return deps

    nrec = 0
    maxops = int(os.environ.get('K_MAXOPS', '0'))

    def op(self, eng, fn, reads=(), writes=()):
        Prog.nrec += 1
        if Prog.maxops and Prog.nrec > Prog.maxops:
            return None
        deps = self._deps(eng, reads, writes)
        tok = (eng, len(self.ops[eng]))
        self.ops[eng].append([fn, deps, False, None, 0])
        for b in reads:
            b.r[eng] = tok
        for b in writes:
            b.w = tok
            b.r = {}
        return tok

    def dma(self, q, out, in_, reads=(), writes=(), sem=None, slow=False):
        Prog.nrec += 1
        if Prog.maxops and Prog.nrec > Prog.maxops:
            return None
        deps = self._deps(q, reads, writes)
        s = sem.sem
        ds = self.dsems[s]
        key = ("d", s)
        if ds[2]:
            if deps.get(key, -1) < ds[1]:
                deps[key] = ds[1]
            ds[2] = False
        ds[1] += 16
        tok = (key, ds[1])
        if slow:
            fn = lambda e: e.dma_start(out=out, in_=in_, allow_slow_non_contiguous=True)
        else:
            fn = lambda e: e.dma_start(out=out, in_=in_)
        self.ops[q].append([fn, deps, False, s, 0])
        for b in reads:
            b.r[key] = tok
        for b in writes:
            b.w = tok
            b.r = {}
        return tok

    def barrier(self):
        deps = {}
        for e in COMPUTE:
            i = len(self.ops[e]) - 1
            while i >= 0 and self.ops[e][i][0] is None:
                i -= 1
            if i >= 0:
                deps[e] = i
        for i, ds in enumerate(self.dsems):
            if ds[1] > 0:
                deps[("d", i)] = ds[1]
                ds[2] = False
        for e in ENGS:
            d = dict(deps)
            if e == "pe":
                d.pop("pe", None)
            self.ops[e].append([None, d, False, None, 0])
        self.top = self.persist_top
        self.free_ds = list(range(len(self.dsems)))
        for t, b in self.banks:
            b.w = None
            b.r = {}

    def emit(self, block):
        for e in ENGS:
            w = {}
            for o in self.ops[e]:
                nd = []
                for k, v in o[1].items():
                    if w.get(k, -1) >= v:
                        continue
                    w[k] = v
                    nd.append((k, v))
                    if isinstance(k, str):
                        self.ops[k][v][2] = True
                o[1] = nd
        for e in COMPUTE:
            c = 0
            for o in self.ops[e]:
                if o[2]:
                    c += 1
                o[4] = c
        prog = self
        if os.environ.get('K_DUMP'):
            for e in ENGS:
                for i, o in enumerate(self.ops[e]):
                    ws = [(k, (self.ops[k][v][4] if isinstance(k, str) else v)) for k, v in o[1]]
                    print(e, i, 'waits', ws, 'fn' if o[0] else 'nofn', 'inc' if o[2] else '', 'dma%s' % o[3] if o[3] is not None else '', 'val', o[4])

        def body(e):
            def f(h):
                for o in prog.ops[e]:
                    for k, v in o[1]:
                        if isinstance(k, str):
                            h.wait_ge(prog.esem[k], prog.ops[k][v][4])
                        else:
                            h.wait_ge(prog.dsems[k[1]][0], v)
                    if o[0] is None:
                        continue
                    ins = o[0](h)
                    if o[3] is not None:
                        ins.then_inc(prog.dsems[o[3]][0], 16)
                    elif o[2]:
                        ins.then_inc(prog.esem[e], 1)
            return f

        block.sync(body("sp"))
        block.tensor(body("pe"))
        block.scalar(body("act"))
        block.vector(body("dve"))
        block.gpsimd(body("pool"))

    def mm(self, out, lhsT, rhs, start, stop, reads, writes):
        return self.op("pe", lambda e: e.matmul(out, lhsT=lhsT, rhs=rhs, start=start, stop=stop), reads, writes)

    def tr(self, out, in_, ident, reads, writes):
        return self.op("pe", lambda e: e.transpose(out=out, in_=in_, identity=ident), reads, writes)

    def act(self, out, in_, func, reads, writes, bias=None, scale=1.0, accum=None):
        kw = {}
        if bias is not None:
            kw["bias"] = bias
        if accum is not None:
            kw["accum_out"] = accum
        return self.op("act", lambda e: e.activation(out=out, in_=in_, func=func, scale=scale, **kw), reads, writes)

    def ts(self, eng, out, in0, s1, s2, op0, op1, reads, writes):
        if op1 is None:
            return self.op(eng, lambda e: e.tensor_scalar(out=out, in0=in0, scalar1=s1, scalar2=None, op0=op0), reads, writes)
        return self.op(eng, lambda e: e.tensor_scalar(out=out, in0=in0, scalar1=s1, scalar2=s2, op0=op0, op1=op1), reads, writes)

    def tt(self, eng, out, in0, in1, op, reads, writes):
        return self.op(eng, lambda e: e.tensor_tensor(out=out, in0=in0, in1=in1, op=op), reads, writes)

    def cp(self, eng, out, in_, reads, writes):
        if eng == "act":
            return self.op("act", lambda e: e.copy(out=out, in_=in_), reads, writes)
        return self.op(eng, lambda e: e.tensor_copy(out=out, in_=in_), reads, writes)

    def memset(self, eng, ap, val, writes):
        return self.op(eng, lambda e: e.memset(ap, val), (), writes)


def bf16_view(bank_t, parts=128):
    return bank_t[0:parts, :].bitcast(BF16)


def build_program(S, n_layers=NL, debug=False):
    NT = S // 128
    NG = S // 512
    nc = bass.Bass("TRN2", target_bir_lowering=False)

    def din(name, shape, dt=F32):
        return nc.dram_tensor(name, list(shape), dt, kind="ExternalInput").ap()

    x_in = din("x", [S, D])
    mem_in = din("mem", [256, D])
    w = {}
    for name, shape in [("norm_mix", [NL, D]), ("w_in", [NL, D, 1956]), ("q_norm", [NL, 256]),
                        ("w_uq", [NL, 256, 384]), ("kv_norm", [NL, 128]), ("w_ukv", [NL, 128, 768]),
                        ("f_bias", [NL, 4]), ("relT", [NL, 4, 128, 5, 128]), ("out_norm", [NL, D]),
                        ("w_o", [NL, D, D]), ("norm_cross", [NL, D]), ("norm_mem", [NL, D]),
                        ("w_cq", [NL, D, 512]), ("w_ckv", [NL, D, 1024]), ("w_co", [NL, 512, D]),
                        ("norm_ffn", [NL, D]), ("w_gu", [NL, D, 2 * FFN]), ("w_down", [NL, FFN, D]),
                        ("final_norm", [D])]:
        w[name] = din(name, shape)
    out_d = nc.dram_tensor("out", [S, D], F32, kind="ExternalOutput").ap()
    skind = "ExternalOutput" if debug else "Internal"

    def dscr(name, shape, dt):
        return nc.dram_tensor(name, list(shape), dt, kind=skind).ap()

    xres = dscr("xres", [S, D], F32)
    qTm = dscr("qTm", [4, 96, S], BF16)
    kTm = dscr("kTm", [4, 96, S], BF16)
    vm = dscr("vm", [S, 512], BF16)
    qTf = dscr("qTf", [4, 70, S], BF16)
    kTf = dscr("kTf", [4, 70, S], BF16)
    vf = dscr("vf", [S, 256], BF16)
    qTc = dscr("qTc", [256, S], BF16)
    kTc = dscr("kTc", [256, S], BF16)
    vc = dscr("vc", [S, 256], BF16)
    ymixT = dscr("ymixT", [D, S], BF16)
    ropeD = dscr("ropeD", [S, 128], F32)

    with ExitStack() as es:
        P = Prog(nc, es, arena_words=50 * 1024)
        ident = P.alloc([128], BF16)
        ones_bf = P.alloc([128], BF16)
        identf = P.alloc([128], F32)
        onesf = P.alloc([128], F32)
        triu = P.alloc([128], F32)
        triu_bf = P.alloc([128], BF16)
        eps_t = P.alloc([1], F32)
        one_t = P.alloc([1], F32)
        npi_t = P.alloc([1], F32)
        mbm = P.alloc([4, 512], BF16)
        mbf = P.alloc([4, 512], BF16)
        cB = Buf("consts")
        P.memset("pool", identf, 1.0, [cB])
        P.op("pool", lambda e: e.affine_select(out=identf, in_=identf, pattern=[[-1, 128]], compare_op=ALU.is_equal,
                                                fill=0.0, base=0, channel_multiplier=1), [cB], [cB])
        P.cp("pool", ident, identf, [cB], [cB])
        P.memset("pool", onesf, 1.0, [cB])
        P.memset("pool", ones_bf, 1.0, [cB])
        P.memset("pool", triu, 1.0, [cB])
        P.op("pool", lambda e: e.affine_select(out=triu, in_=triu, pattern=[[1, 128]], compare_op=ALU.is_ge,
                                                fill=0.0, base=0, channel_multiplier=-1), [cB], [cB])
        P.cp("pool", triu_bf, triu, [cB], [cB])
        P.memset("pool", eps_t, EPS, [cB])
        P.memset("pool", one_t, 1.0, [cB])
        P.memset("pool", npi_t, -np.pi, [cB])
        P.memset("pool", mbm, 0.0, [cB])
        P.memset("pool", mbf, 0.0, [cB])
        for j in range(4):
            if j > 0:
                P.memset("pool", mbm[:, j, 0:j * 128], NEG, [cB])
                P.memset("pool", mbf[:, j, 0:j * 128], NEG, [cB])
            P.memset("pool", mbm[64:128, j, j * 128:j * 128 + 64], NEG, [cB])
            blk = mbf[:, j, j * 128:(j + 1) * 128]
            P.op("pool", (lambda blk: lambda e: e.affine_select(out=blk, in_=blk, pattern=[[1, 128]], compare_op=ALU.is_ge,
                                                                 fill=NEG, base=0, channel_multiplier=-1))(blk), [cB], [cB])
        P.persist_top = P.top
        if os.environ.get('K_STOP', '') != 'c1':
          if True:
            rope = P.alloc([NT, 128], F32)
            posi = P.alloc([NT], I32)
            posf = P.alloc([NT], F32)
            invf = P.alloc([16], F32)
            ang = P.alloc([NT, 16], F32)
            ang2 = P.alloc([NT, 16], F32)
            sn = P.alloc([NT, 16], F32)
            csn = P.alloc([NT, 16], F32)
            P.op("pool", lambda e: e.iota(posi, pattern=[[128, NT]], base=0, channel_multiplier=1), [], [cB])
            P.cp("pool", posf, posi, [cB], [cB])
            for i in range(16):
                P.memset("pool", invf[:, i:i + 1], float(np.float32(10000.0) ** np.float32(-(2.0 * i) / 32.0)), [cB])
            for t in range(NT):
                P.ts("dve", ang[:, t, :], invf, posf[:, t:t + 1], None, ALU.mult, None, [cB], [cB])
            kint = P.alloc([NT, 16], I32)
            kf = P.alloc([NT, 16], F32)
            TWO_PI = float(2 * np.pi)

            def sin_of(dst, shift):
                if shift:
                    P.ts("dve", ang2, ang, float(shift), None, ALU.add, None, [cB], [cB])
                    src = ang2
                else:
                    src = ang
                P.ts("dve", kf, src, 1.0 / TWO_PI, None, ALU.mult, None, [cB], [cB])
                P.cp("dve", kint, kf, [cB], [cB])
                P.cp("dve", kf, kint, [cB], [cB])
                P.op("dve", lambda e: e.scalar_tensor_tensor(out=kf, in0=kf, scalar=-TWO_PI, in1=src, op0=ALU.mult, op1=ALU.add), [cB], [cB])
                P.ts("dve", dst, kf, float(np.pi), TWO_PI, ALU.is_gt, ALU.mult, [cB], [cB])
                P.tt("dve", kf, kf, dst, ALU.subtract, [cB], [cB])
                P.ts("dve", dst, kf, -float(np.pi), TWO_PI, ALU.is_lt, ALU.mult, [cB], [cB])
                P.tt("dve", kf, kf, dst, ALU.add, [cB], [cB])
                P.act(dst, kf, AF.Sin, [cB], [cB])

            sin_of(sn, 0.0)
            sin_of(csn, np.pi / 2)
            for hh in range(4):
                P.cp("pool", rope[:, :, hh * 16:(hh + 1) * 16], csn, [cB], [cB])
                P.cp("pool", rope[:, :, 64 + hh * 16:64 + (hh + 1) * 16], sn, [cB], [cB])
            rpB = P.dbuf("ropeout")
            P.dma("sp", ropeD.rearrange("(t p) c -> p t c", p=128), rope, [cB], [], sem=rpB)
        P.barrier()
        STOP = os.environ.get('K_STOP', '')

        def load_weight(dst, src, K, segs, gain=None, stg=None, stgB=None, wB=None, engs=("dve", "act")):
            i = 0
            for c in range(K // 128):
                for (s0, n, d0) in segs:
                    slot = i % len(stg)
                    P.dma("sp", stg[slot][:, 0:n], src[c * 128:(c + 1) * 128, s0:s0 + n], [], [stgB[slot]], sem=stgB[slot])
                    eng = engs[i % len(engs)]
                    o = dst[:, c, d0:d0 + n]
                    src_t = stg[slot][:, 0:n]
                    if gain is not None:
                        if eng == "act":
                            P.act(o, src_t, AF.Copy, [stgB[slot]], [wB], scale=gain[:, c:c + 1])
                        else:
                            P.ts("dve", o, src_t, gain[:, c:c + 1], None, ALU.mult, None, [stgB[slot]], [wB])
                    else:
                        P.cp(eng, o, src_t, [stgB[slot]], [wB])
                    i += 1

        def load_gain(dst, src_vec, K, gB):
            if K >= 128:
                P.dma("sp", dst, src_vec.rearrange("(c p) -> p c", p=128), [], [gB], sem=gB)
            return dst

        def rms_rstd(ss, n, dim, rB):
            P.act(ss, ss, AF.Ln, [rB], [rB], bias=eps_t[:, 0:1], scale=1.0 / dim)
            P.act(ss, ss, AF.Exp, [rB], [rB], scale=-0.5)

        for l in range(n_layers if STOP not in ('consts', 'c1') else 0):
            x_src = x_in if l == 0 else xres
            stg = [P.alloc([2048], F32) for _ in range(2)]
            stgB = [P.dbuf("stg%d" % i) for i in range(2)]
            gB = P.dbuf("gains")
            g_mix = P.alloc([8], F32)
            g_q = P.alloc([2], F32)
            g_kv = P.alloc([1], F32)
            fb = P.alloc([4], F32)
            P.dma("sp", g_mix, w["norm_mix"][l].rearrange("(c p) -> p c", p=128), [], [gB], sem=gB, slow=True)
            P.dma("sp", g_q, w["q_norm"][l].rearrange("(c p) -> p c", p=128), [], [gB], sem=gB, slow=True)
            P.dma("sp", g_kv, w["kv_norm"][l].rearrange("(c p) -> p c", p=128), [], [gB], sem=gB, slow=True)
            P.dma("sp", fb, w["f_bias"][l].partition_broadcast(128), [], [gB], sem=gB)
            Win = P.alloc([8, 1956], BF16)
            Wuq = P.alloc([2, 384], BF16)
            Wukv = P.alloc([1, 768], BF16)
            wB = Buf("wP")
            segs_in = [(0, 416, 0), (1184, 4, 416), (416, 512, 420), (928, 256, 932), (1700, 256, 1188),
                       (1188, 512, 1444)]
            load_weight(Win, w["w_in"][l], D, segs_in, gain=g_mix, stg=stg, stgB=stgB, wB=wB)
            segs_uq = []
            for h in range(4):
                segs_uq += [(h * 96, 64, h * 64), (h * 96 + 64, 16, 256 + h * 16), (h * 96 + 80, 16, 320 + h * 16)]
            load_weight(Wuq, w["w_uq"][l], 256, segs_uq, gain=g_q, stg=stg, stgB=stgB, wB=wB)
            segs_ukv = []
            for h in range(4):
                segs_ukv += [(h * 192, 64, h * 64), (h * 192 + 64, 128, 256 + h * 128)]
            load_weight(Wukv, w["w_ukv"][l], 128, segs_ukv, gain=g_kv, stg=stg, stgB=stgB, wB=wB)

            if STOP == 'weights':
                break
            NX = 3
            xs = [P.alloc([D], F32) for _ in range(NX)]
            xsB = [P.dbuf("xs%d" % i) for i in range(NX)]
            rps = [P.alloc([128], F32) for _ in range(NX)]
            junk = P.alloc([D], BF16)
            junkB = Buf("junk")
            st = [P.alloc([8], F32) for _ in range(2)]
            stB = [Buf("st%d" % i) for i in range(2)]
            hb = [P.alloc([D], BF16) for _ in range(2)]
            hbB = [Buf("hb%d" % i) for i in range(2)]
            hT = [P.alloc([D], BF16) for _ in range(2)]
            hTB = [Buf("hT%d" % i) for i in range(2)]
            cn = [P.alloc([384], BF16) for _ in range(2)]
            cnB = [Buf("cn%d" % i) for i in range(2)]
            cT = [P.alloc([384], BF16) for _ in range(2)]
            cTB = [Buf("cT%d" % i) for i in range(2)]
            tmp = [P.alloc([6, 64], F32) for _ in range(2)]
            tmpB = [Buf("tmp%d" % i) for i in range(2)]
            Qm = [P.alloc([4, 96], BF16) for _ in range(2)]
            Km = [P.alloc([4, 96], BF16) for _ in range(2)]
            Qf = [P.alloc([4, 70], BF16) for _ in range(2)]
            Kf = [P.alloc([4, 70], BF16) for _ in range(2)]
            Qc = [P.alloc([256], BF16) for _ in range(2)]
            Kc = [P.alloc([256], BF16) for _ in range(2)]
            tmB = [Buf("tm%d" % i) for i in range(2)]
            fx = [P.alloc([24], F32) for _ in range(2)]
            fxh = [P.alloc([3, 4], BF16) for _ in range(2)]
            fxs = [P.alloc([3, 4], BF16) for _ in range(2)]
            fxB = [Buf("fx%d" % i) for i in range(2)]
            tot = P.alloc([4], F32)
            totB = Buf("tot")
            P.memset("pool", tot, 0.0, [totB])
            for i in range(2):
                P.memset("pool", Qf[i][:, :, 67:70], 1.0, [tmB[i]])
                P.memset("pool", Kf[i][:, :, 64:67], 1.0, [tmB[i]])
            sQm = [P.alloc([4, 512], BF16) for _ in range(2)]
            sKm = [P.alloc([4, 512], BF16) for _ in range(2)]
            sQf = [P.alloc([4, 512], BF16) for _ in range(2)]
            sKf = [P.alloc([4, 512], BF16) for _ in range(2)]
            sQc = [P.alloc([2, 512], BF16) for _ in range(2)]
            sKc = [P.alloc([2, 512], BF16) for _ in range(2)]
            sV = [P.alloc([4, 1024], BF16) for _ in range(2)]
            sB = [P.dbuf("stage%d" % i) for i in range(2)]

            for t in range(NT):
                G, tt_ = divmod(t, 4)
                gs = G % 2
                i2 = t % 2
                xi = t % NX
                P.dma("sp", xs[xi], x_src[t * 128:(t + 1) * 128, :], [], [xsB[xi]], sem=xsB[xi])
                P.dma("sp", rps[xi], ropeD[t * 128:(t + 1) * 128, :], [], [xsB[xi]], sem=xsB[xi])
                P.act(junk, xs[xi], AF.Square, [xsB[xi]], [junkB, stB[i2]], accum=st[i2][:, 0:1])
                rms_rstd(st[i2][:, 0:1], 1, D, stB[i2])
                P.ts("dve", hb[i2], xs[xi], st[i2][:, 0:1], None, ALU.mult, None, [xsB[xi], stB[i2]], [hbB[i2]])
                bt, bb = P.bank()
                bv = bf16_view(bt)
                for c in range(8):
                    P.tr(bv[:, c * 128:(c + 1) * 128], hb[i2][:, c * 128:(c + 1) * 128], ident, [hbB[i2]], [bb])
                P.cp("dve", hT[i2], bv, [bb], [hTB[i2]])
                chunks = [(0, 420), (420, 512), (932, 512), (1444, 512)]
                pb = []
                for (c0, n) in chunks:
                    bt2, bb2 = P.bank()
                    for c in range(8):
                        P.mm(bt2[:, 0:n], hT[i2][:, c * 128:(c + 1) * 128], Win[:, c, c0:c0 + n], c == 0, c == 7,
                             [hTB[i2], wB], [bb2])
                    pb.append((bt2, bb2))
                (b0, b0B), (b1, b1B), (b2, b2B), (b3, b3B) = pb
                s2 = st[i2]
                P.act(junk[:, 0:256], b0[:, 0:256], AF.Square, [b0B], [junkB, stB[i2]], accum=s2[:, 1:2])
                P.act(junk[:, 0:128], b0[:, 256:384], AF.Square, [b0B], [junkB, stB[i2]], accum=s2[:, 2:3])
                P.act(s2[:, 1:2], s2[:, 1:2], AF.Ln, [stB[i2]], [stB[i2]], bias=eps_t[:, 0:1], scale=1.0 / 256)
                P.act(s2[:, 2:3], s2[:, 2:3], AF.Ln, [stB[i2]], [stB[i2]], bias=eps_t[:, 0:1], scale=1.0 / 128)
                P.act(s2[:, 1:3], s2[:, 1:3], AF.Exp, [stB[i2]], [stB[i2]], scale=-0.5)
                P.ts("dve", cn[i2][:, 0:256], b0[:, 0:256], s2[:, 1:2], None, ALU.mult, None, [b0B, stB[i2]], [cnB[i2]])
                P.ts("dve", cn[i2][:, 256:384], b0[:, 256:384], s2[:, 2:3], None, ALU.mult, None, [b0B, stB[i2]], [cnB[i2]])
                bt3, bb3 = P.bank()
                bv3 = bf16_view(bt3)
                for c in range(3):
                    P.tr(bv3[:, c * 128:(c + 1) * 128], cn[i2][:, c * 128:(c + 1) * 128], ident, [cnB[i2]], [bb3])
                P.cp("dve", cT[i2], bv3[:, 0:384], [bb3], [cTB[i2]])
                bq, bqB = P.bank()
                for c in range(2):
                    P.mm(bq[:, 0:384], cT[i2][:, c * 128:(c + 1) * 128], Wuq[:, c, :], c == 0, c == 1, [cTB[i2], wB], [bqB])
                bk, bkB = P.bank()
                P.mm(bk[:, 0:256], cT[i2][:, 256:384], Wukv[:, 0, 0:256], True, True, [cTB[i2], wB], [bkB])
                bvv, bvB = P.bank()
                P.mm(bvv[:, 0:512], cT[i2][:, 256:384], Wukv[:, 0, 256:768], True, True, [cTB[i2], wB], [bvB])
                cos4 = rps[xi][:, 0:64]
                sin4 = rps[xi][:, 64:128]
                tm_ = tmp[i2]
                qv = Qm[i2]
                kv_ = Km[i2]
                P.tt("dve", tm_[:, 0, :], bq[:, 256:320], cos4, ALU.mult, [bqB, xsB[xi]], [tmpB[i2]])
                P.tt("dve", tm_[:, 1, :], bq[:, 320:384], sin4, ALU.mult, [bqB, xsB[xi]], [tmpB[i2]])
                P.tt("dve", tm_[:, 2, :], bq[:, 256:320], sin4, ALU.mult, [bqB, xsB[xi]], [tmpB[i2]])
                P.tt("dve", tm_[:, 3, :], bq[:, 320:384], cos4, ALU.mult, [bqB, xsB[xi]], [tmpB[i2]])
                P.tt("dve", qv[:, :, 64:80], tm_[:, 0, :].rearrange("p (h d) -> p h d", h=4),
                     tm_[:, 1, :].rearrange("p (h d) -> p h d", h=4), ALU.subtract, [tmpB[i2]], [tmB[i2]])
                P.tt("dve", qv[:, :, 80:96], tm_[:, 2, :].rearrange("p (h d) -> p h d", h=4),
                     tm_[:, 3, :].rearrange("p (h d) -> p h d", h=4), ALU.add, [tmpB[i2]], [tmB[i2]])
                P.cp("act", qv[:, :, 0:64], bq[:, 0:256].rearrange("p (h d) -> p h d", h=4), [bqB], [tmB[i2]])
                P.tt("dve", tm_[:, 4, 0:16], b0[:, 384:400], cos4[:, 0:16], ALU.mult, [b0B, xsB[xi]], [tmpB[i2]])
                P.tt("dve", tm_[:, 4, 16:32], b0[:, 400:416], sin4[:, 0:16], ALU.mult, [b0B, xsB[xi]], [tmpB[i2]])
                P.tt("dve", tm_[:, 4, 32:48], b0[:, 384:400], sin4[:, 0:16], ALU.mult, [b0B, xsB[xi]], [tmpB[i2]])
                P.tt("dve", tm_[:, 4, 48:64], b0[:, 400:416], cos4[:, 0:16], ALU.mult, [b0B, xsB[xi]], [tmpB[i2]])
                P.tt("dve", tm_[:, 5, 0:16], tm_[:, 4, 0:16], tm_[:, 4, 16:32], ALU.subtract, [tmpB[i2]], [tmpB[i2]])
                P.tt("dve", tm_[:, 5, 16:32], tm_[:, 4, 32:48], tm_[:, 4, 48:64], ALU.add, [tmpB[i2]], [tmpB[i2]])
                for h in range(4):
                    P.cp("pool", kv_[:, h, 64:96], tm_[:, 5, 0:32], [tmpB[i2]], [tmB[i2]])
                P.cp("act", kv_[:, :, 0:64], bk[:, 0:256].rearrange("p (h d) -> p h d", h=4), [bkB], [tmB[i2]])
                P.cp("act", sV[gs][:, tt_, 0:512], bvv[:, 0:512], [bvB], [sB[gs]])
                f = fx[i2]
                P.tt("dve", f[:, 0:4], b0[:, 416:420], fb, ALU.add, [b0B, gB], [fxB[i2]])
                P.act(f[:, 0:4], f[:, 0:4], AF.Exp, [fxB[i2]], [fxB[i2]], scale=-1.0)
                P.act(f[:, 4:8], f[:, 0:4], AF.Ln, [fxB[i2]], [fxB[i2]], bias=one_t[:, 0:1], scale=1.0)
                spb = fxs[i2]
                P.cp("dve", spb[:, 0, :], f[:, 4:8], [fxB[i2]], [fxB[i2]])
                P.tt("dve", f[:, 12:16], f[:, 4:8], spb[:, 0, :], ALU.subtract, [fxB[i2]], [fxB[i2]])
                P.cp("dve", spb[:, 1, :], f[:, 12:16], [fxB[i2]], [fxB[i2]])
                P.tt("dve", f[:, 16:20], f[:, 12:16], spb[:, 1, :], ALU.subtract, [fxB[i2]], [fxB[i2]])
                P.cp("dve", spb[:, 2, :], f[:, 16:20], [fxB[i2]], [fxB[i2]])
                bc, bcB = P.bank()
                for pc in range(3):
                    P.mm(bc[:, 0:4], triu_bf, spb[:, pc, :], pc == 0, pc == 2, [fxB[i2], cB], [bcB])
                for pc in range(3):
                    P.mm(bc[:, 8:12], ones_bf, spb[:, pc, :], pc == 0, pc == 2, [fxB[i2], cB], [bcB])
                P.tt("dve", f[:, 8:12], bc[:, 0:4], tot, ALU.add, [bcB, totB], [fxB[i2]])
                P.tt("dve", tot, bc[:, 8:12], tot, ALU.add, [bcB, totB], [totB])
                ph = fxh[i2]
                P.cp("dve", ph[:, 0, :], f[:, 8:12], [fxB[i2]], [fxB[i2]])
                P.tt("dve", f[:, 12:16], f[:, 8:12], ph[:, 0, :], ALU.subtract, [fxB[i2]], [fxB[i2]])
                P.cp("dve", ph[:, 1, :], f[:, 12:16], [fxB[i2]], [fxB[i2]])
                P.tt("dve", f[:, 16:20], f[:, 12:16], ph[:, 1, :], ALU.subtract, [fxB[i2]], [fxB[i2]])
                P.cp("dve", ph[:, 2, :], f[:, 16:20], [fxB[i2]], [fxB[i2]])
                for pc in range(3):
                    P.cp("pool", Kf[i2][:, :, 67 + pc:68 + pc], ph[:, pc, :].rearrange("p (h o) -> p h o", o=1),
                         [fxB[i2]], [tmB[i2]])
                    P.ts("dve", Qf[i2][:, :, 64 + pc:65 + pc], ph[:, pc, :].rearrange("p (h o) -> p h o", o=1),
                         -1.0, None, ALU.mult, None, [fxB[i2]], [tmB[i2]])
                P.ts("dve", Qf[i2][:, :, 0:64], b1[:, 0:256].rearrange("p (h d) -> p h d", h=4), 0.125, None, ALU.mult, None,
                     [b1B], [tmB[i2]])
                P.cp("act", Kf[i2][:, :, 0:64], b1[:, 256:512].rearrange("p (h d) -> p h d", h=4), [b1B], [tmB[i2]])
                P.cp("act", sV[gs][:, tt_, 512:1024], b2[:, 0:512], [b2B], [sB[gs]])
                P.ts("dve", Qc[i2], b3[:, 0:256], 0.125, None, ALU.mult, None, [b3B], [tmB[i2]])
                P.cp("act", Kc[i2], b3[:, 256:512], [b3B], [tmB[i2]])
                cs = slice(tt_ * 128, (tt_ + 1) * 128)
                for (src, dst, rows) in ((Qm[i2], sQm[gs], 96), (Km[i2], sKm[gs], 96), (Qf[i2], sQf[gs], 70), (Kf[i2], sKf[gs], 70)):
                    btx, bbx = P.bank()
                    bvx = bf16_view(btx)
                    for h in range(4):
                        P.tr(bvx[0:rows, h * 128:(h + 1) * 128], src[:, h, :], ident, [tmB[i2]], [bbx])
                    P.cp("dve" if rows == 96 else "act", dst[0:rows, :, cs], bvx[0:rows, 0:512].rearrange("p (h t) -> p h t", h=4),
                         [bbx], [sB[gs]])
                btx, bbx = P.bank()
                bvx = bf16_view(btx)
                for c in range(2):
                    P.tr(bvx[:, c * 128:(c + 1) * 128], Qc[i2][:, c * 128:(c + 1) * 128], ident, [tmB[i2]], [bbx])
                    P.tr(bvx[:, (2 + c) * 128:(3 + c) * 128], Kc[i2][:, c * 128:(c + 1) * 128], ident, [tmB[i2]], [bbx])
                P.cp("dve", sQc[gs][:, :, cs], bvx[:, 0:256].rearrange("p (h t) -> p h t", h=2), [bbx], [sB[gs]])
                P.cp("act", sKc[gs][:, :, cs], bvx[:, 256:512].rearrange("p (h t) -> p h t", h=2), [bbx], [sB[gs]])
                if tt_ == 3:
                    gsl = slice(G * 512, (G + 1) * 512)
                    q = "sp"
                    for h in range(4):
                        P.dma(q, qTm[h, :, gsl], sQm[gs][0:96, h, :], [sB[gs]], [], sem=sB[gs])
                        P.dma(q, kTm[h, :, gsl], sKm[gs][0:96, h, :], [sB[gs]], [], sem=sB[gs])
                        P.dma(q, qTf[h, :, gsl], sQf[gs][0:70, h, :], [sB[gs]], [], sem=sB[gs])
                        P.dma(q, kTf[h, :, gsl], sKf[gs][0:70, h, :], [sB[gs]], [], sem=sB[gs])
                    for c in range(2):
                        P.dma(q, qTc[c * 128:(c + 1) * 128, gsl], sQc[gs][:, c, :], [sB[gs]], [], sem=sB[gs])
                        P.dma(q, kTc[c * 128:(c + 1) * 128, gsl], sKc[gs][:, c, :], [sB[gs]], [], sem=sB[gs])
                    rows = slice(G * 512, (G + 1) * 512)
                    P.dma(q, vm[rows, :].rearrange("(t p) c -> p t c", p=128), sV[gs][:, :, 0:512], [sB[gs]], [], sem=sB[gs])
                    P.dma(q, vf[rows, :].rearrange("(t p) c -> p t c", p=128), sV[gs][:, :, 512:768], [sB[gs]], [], sem=sB[gs])
                    P.dma(q, vc[rows, :].rearrange("(t p) c -> p t c", p=128), sV[gs][:, :, 768:1024], [sB[gs]], [], sem=sB[gs])
            P.barrier()

            P.rot = [4, 5, 6, 7]
            slots = []
            for i in range(2):
                slots.append((P.alloc([S], BF16), P.alloc([S], BF16), P.alloc([NT, 128], BF16), P.dbuf("kv%d" % i)))
            PT = [P.alloc([512], BF16) for _ in range(3)]
            PTB = [Buf("pt%d" % i) for i in range(3)]
            rden = [P.alloc([512], F32) for _ in range(2)]
            rdB = [Buf("rd%d" % i) for i in range(2)]
            ost = [P.alloc([512], BF16) for _ in range(2)]
            ostB = [P.dbuf("ost%d" % i) for i in range(2)]
            BB = P.alloc([4, 8, 512], BF16)
            bbB = Buf("BB")
            rstg = P.alloc([5, 128], F32)
            rsB = P.dbuf("rstg")
            P.memset("pool", BB, NEG, [bbB])
            for h in range(4):
                P.dma("sp", rstg, w["relT"][l, h], [], [rsB], sem=rsB)
                P.memset("pool", rstg[64:128, 0, 0:64], NEG, [rsB])
                P.memset("pool", rstg[0:64, 4, 64:128], NEG, [rsB])
                for j in range(8):
                    for qt in range(4):
                        dl = qt + 4 - j
                        if 0 <= dl <= 4:
                            P.cp("pool" if (j + qt) % 2 else "dve", BB[:, h, j, qt * 128:(qt + 1) * 128], rstg[:, dl, :], [rsB], [bbB])
            cnt = {"pt": 0, "o": 0, "hd": 0}

            def load_head(kT_src, qT_src, rows, v_src, vc0, vw):
                KT, QT, V, kvB = slots[cnt["hd"] % 2]
                cnt["hd"] += 1
                P.dma("sp", KT[0:rows, :], kT_src, [], [kvB], sem=kvB)
                P.dma("sp", QT[0:rows, :], qT_src, [], [kvB], sem=kvB)
                vr = v_src.rearrange("(t p) c -> p t c", p=128)
                for t0 in range(0, NT, 8):
                    t1 = min(NT, t0 + 8)
                    P.dma("sp", V[:, t0:t1, 0:vw], vr[:, t0:t1, vc0:vc0 + vw], [], [kvB], sem=kvB)
                return KT, QT, V, kvB

            def run_head(hd, rows, kind, scale, yrow0, bias_tiles, biasB):
                KT, QT, V, kvB = hd
                for G in range(NG):
                    if kind == "chk":
                        kts = [(4 * G - 4 + j, j) for j in range(8) if 4 * G - 4 + j >= 0]
                    else:
                        kts = [(kt, (kt - 4 * G) if kt >= 4 * G else None) for kt in range(4 * G + 4)]
                    o2 = cnt["o"] % 2
                    cnt["o"] += 1
                    ob, obB = P.bank_at(2 * o2)
                    db, dbB = P.bank_at(2 * o2 + 1)
                    q = QT[0:rows, G * 512:(G + 1) * 512]
                    n = len(kts)

                    def pv(i, ptb, ptB):
                        kt = kts[i][0]
                        P.mm(ob[:, :], V[:, kt, :], ptb, i == 0, i == n - 1, [kvB, ptB], [obB])
                        if kind == "mla":
                            P.mm(db[:, :], ones_bf, ptb, i == 0, i == n - 1, [cB, ptB], [dbB])

                    prev = None
                    for i, (kt, j) in enumerate(kts):
                        sb_, sbB = P.bank()
                        P.mm(sb_[:, :], KT[0:rows, kt * 128:(kt + 1) * 128], q, True, j is None, [kvB], [sbB])
                        if j is not None:
                            P.mm(sb_[:, :], ident, bias_tiles[j], False, True, [cB, biasB], [sbB])
                        sl = cnt["pt"] % 3
                        cnt["pt"] += 1
                        P.act(PT[sl], sb_[:, :], AF.Exp, [sbB], [PTB[sl]], scale=scale)
                        if prev is not None:
                            pv(*prev)
                        prev = (i, PT[sl], PTB[sl])
                    pv(*prev)
                    gsl = slice(G * 512, (G + 1) * 512)
                    if kind == "mla":
                        P.op("dve", lambda e, a=rden[o2], b=db: e.reciprocal(out=a, in_=b[:, :]), [dbB], [rdB[o2]])
                        P.tt("dve", ost[o2], ob[:, :], rden[o2], ALU.mult, [obB, rdB[o2]], [ostB[o2]])
                        P.dma("sp", ymixT[yrow0:yrow0 + 128, gsl], ost[o2], [ostB[o2]], [], sem=ostB[o2])
                    else:
                        P.op("dve", lambda e, a=rden[o2], b=ob: e.reciprocal(out=a[0:64, :], in_=b[64:128, :]), [obB], [rdB[o2]])
                        P.tt("dve", ost[o2][0:64, :], ob[0:64, :], rden[o2][0:64, :], ALU.mult, [obB, rdB[o2]], [ostB[o2]])
                        P.dma("sp", ymixT[yrow0:yrow0 + 64, gsl], ost[o2][0:64, :], [ostB[o2]], [], sem=ostB[o2])

            mla_scale = float(96 ** -0.5)
            specs = []
            for h in range(4):
                specs.append(((kTm[h], qTm[h], 96, vm, h * 128, 128), (96, "mla", mla_scale, h * 128, [mbm[:, j, :] for j in range(4)], cB)))
            for h in range(4):
                specs.append(((kTf[h], qTf[h], 70, vf, h * 64, 64), (70, "fox", 1.0, 512 + h * 64, [mbf[:, j, :] for j in range(4)], cB)))
            for h in range(4):
                specs.append(((kTc[h * 64:(h + 1) * 64, :], qTc[h * 64:(h + 1) * 64, :], 64, vc, h * 64, 64),
                              (64, "chk", 1.0, 768 + h * 64, [BB[:, h, j, :] for j in range(8)], bbB)))

            def prefetch(i):
                if i in (4, 5):
                    P.memset("pool", slots[i % 2][2][:, :, 64:128], 1.0, [slots[i % 2][3]])
                return load_head(*specs[i][0])

            hds = {0: prefetch(0)}
            for i in range(12):
                if i + 1 < 12:
                    hds[i + 1] = prefetch(i + 1)
                run_head(hds.pop(i), *specs[i][1])
            P.barrier()

            P.rot = list(range(8))
            stg = [P.alloc([2048], F32) for _ in range(2)]
            stgB = [P.dbuf("stg%d" % i) for i in range(2)]
            gB = P.dbuf("gains")
            g_o = P.alloc([8], F32)
            g_c = P.alloc([8], F32)
            g_m = P.alloc([8], F32)
            P.dma("sp", g_o, w["out_norm"][l].rearrange("(c p) -> p c", p=128), [], [gB], sem=gB, slow=True)
            P.dma("sp", g_c, w["norm_cross"][l].rearrange("(c p) -> p c", p=128), [], [gB], sem=gB, slow=True)
            P.dma("sp", g_m, w["norm_mem"][l].rearrange("(c p) -> p c", p=128), [], [gB], sem=gB, slow=True)
            Wo = P.alloc([8, 1024], BF16)
            Wcq = P.alloc([8, 512], BF16)
            Wco = P.alloc([4, 1024], BF16)
            Wckv = P.alloc([8, 1024], BF16)
            wB = Buf("wO1")
            load_weight(Wo, w["w_o"][l], D, [(0, 1024, 0)], gain=g_o, stg=stg, stgB=stgB, wB=wB)
            load_weight(Wcq, w["w_cq"][l], D, [(0, 512, 0)], gain=g_c, stg=stg, stgB=stgB, wB=wB)
            load_weight(Wco, w["w_co"][l], 512, [(0, 1024, 0)], stg=stg, stgB=stgB, wB=wB)
            load_weight(Wckv, w["w_ckv"][l], D, [(0, 1024, 0)], gain=g_m, stg=stg, stgB=stgB, wB=wB)
            junk = P.alloc([D], BF16)
            junkB = Buf("junk")
            memT = P.alloc([8, 256], BF16)
            KcT = P.alloc([4, 256], BF16)
            Vc = P.alloc([2, 512], BF16)
            mB = Buf("mem")
            mx = P.alloc([D], F32)
            mxB = P.dbuf("mx")
            mst = P.alloc([2], F32)
            mhb = P.alloc([D], BF16)
            for mt in range(2):
                P.dma("sp", mx, mem_in[mt * 128:(mt + 1) * 128, :], [], [mxB], sem=mxB)
                P.act(junk, mx, AF.Square, [mxB], [junkB, mB], accum=mst[:, 0:1])
                rms_rstd(mst[:, 0:1], 1, D, mB)
                P.ts("dve", mhb, mx, mst[:, 0:1], None, ALU.mult, None, [mxB, mB], [mB])
                bt, bb = P.bank()
                bv = bf16_view(bt)
                for c in range(8):
                    P.tr(bv[:, c * 128:(c + 1) * 128], mhb[:, c * 128:(c + 1) * 128], ident, [mB], [bb])
                P.cp("dve", memT[:, :, mt * 128:(mt + 1) * 128], bv.rearrange("p (c t) -> p c t", c=8), [bb], [mB])
            for h in range(4):
                bt, bb = P.bank()
                for c in range(8):
                    P.mm(bt[:, 0:256], Wckv[:, c, h * 128:(h + 1) * 128], memT[:, c, :], c == 0, c == 7, [wB, mB], [bb])
                P.cp("act", KcT[:, h, :], bt[:, 0:256], [bb], [mB])
            for mt in range(2):
                bt, bb = P.bank()
                for c in range(8):
                    P.mm(bt[:, :], memT[:, c, mt * 128:(mt + 1) * 128], Wckv[:, c, 512:1024], c == 0, c == 7, [wB, mB], [bb])
                P.cp("act", Vc[:, mt, :], bt[:, :], [bb], [mB])

            YT = [P.alloc([8, 512], BF16) for _ in range(2)]
            YTB = [P.dbuf("yt%d" % i) for i in range(2)]
            xg = [P.alloc([4, D], F32) for _ in range(2)]
            xgB = [P.dbuf("xg%d" % i) for i in range(2)]
            sq = P.alloc([8, 512], BF16)
            sqB = Buf("sq")
            rr = P.alloc([3, 512], F32)
            rrB = Buf("rr")
            YN = P.alloc([8, 512], BF16)
            YNB = Buf("yn")
            st1 = P.alloc([4], F32)
            st1B = Buf("st1")
            hb1 = P.alloc([D], BF16)
            hb1B = Buf("hb1")
            hcT = P.alloc([8, 512], BF16)
            hcTB = Buf("hcT")
            qcT = P.alloc([4, 512], BF16)
            qcTB = Buf("qcT")
            PT = [P.alloc([512], BF16) for _ in range(3)]
            PTB = [Buf("pt%d" % i) for i in range(3)]
            ocT = P.alloc([4, 512], BF16)
            ocTB = Buf("ocT")
            rdn = P.alloc([512], F32)
            rdnB = Buf("rdn")
            pti = 0
            c_scale = float(128 ** -0.5)
            x_o1 = x_in if l == 0 else xres
            for G in range(NG):
                g2 = G % 2
                gsl = slice(G * 512, (G + 1) * 512)
                P.dma("sp", YT[g2], ymixT[:, gsl].rearrange("(c p) s -> p c s", p=128), [], [YTB[g2]], sem=YTB[g2])
                P.dma("sp", xg[g2], x_o1[gsl, :].rearrange("(t p) d -> p t d", p=128), [], [xgB[g2]], sem=xgB[g2])
                P.tt("pool", sq, YT[g2], YT[g2], ALU.mult, [YTB[g2]], [sqB])
                for gi, (c0, c1, wdt) in enumerate(((0, 4, 512), (4, 6, 256), (6, 8, 256))):
                    bt, bb = P.bank()
                    for c in range(c0, c1):
                        P.mm(bt[:, :], ones_bf, sq[:, c, :], c == c0, c == c1 - 1, [cB, sqB], [bb])
                    P.act(rr[:, gi, :], bt[:, :], AF.Ln, [bb], [rrB], bias=eps_t[:, 0:1], scale=1.0 / wdt)
                    P.act(rr[:, gi, :], rr[:, gi, :], AF.Exp, [rrB], [rrB], scale=-0.5)
                    for c in range(c0, c1):
                        P.tt("dve", YN[:, c, :], YT[g2][:, c, :], rr[:, gi, :], ALU.mult, [YTB[g2], rrB], [YNB])
                for t in range(4):
                    ts_ = slice(t * 128, (t + 1) * 128)
                    for half in range(2):
                        hs = slice(half * 512, (half + 1) * 512)
                        bt, bb = P.bank()
                        for c in range(8):
                            P.mm(bt[:, :], YN[:, c, ts_], Wo[:, c, hs], c == 0, c == 7, [YNB, wB], [bb])
                        P.tt("dve", xg[g2][:, t, hs], bt[:, :], xg[g2][:, t, hs], ALU.add, [bb, xgB[g2]], [xgB[g2]])
                    P.act(junk, xg[g2][:, t, :], AF.Square, [xgB[g2]], [junkB, st1B], accum=st1[:, t:t + 1])
                    rms_rstd(st1[:, t:t + 1], 1, D, st1B)
                    P.ts("dve", hb1, xg[g2][:, t, :], st1[:, t:t + 1], None, ALU.mult, None, [xgB[g2], st1B], [hb1B])
                    bt, bb = P.bank()
                    bv = bf16_view(bt)
                    for c in range(8):
                        P.tr(bv[:, c * 128:(c + 1) * 128], hb1[:, c * 128:(c + 1) * 128], ident, [hb1B], [bb])
                    P.cp("act", hcT[:, :, ts_], bv.rearrange("p (c t) -> p c t", c=8), [bb], [hcTB])
                for h in range(4):
                    bt, bb = P.bank()
                    for c in range(8):
                        P.mm(bt[:, :], Wcq[:, c, h * 128:(h + 1) * 128], hcT[:, c, :], c == 0, c == 7, [wB, hcTB], [bb])
                    P.cp("act", qcT[:, h, :], bt[:, :], [bb], [qcTB])
                for h in range(4):
                    ob, obB = P.bank()
                    db, dbB = P.bank()
                    pts = []
                    for mt in range(2):
                        sb_, sbB = P.bank()
                        P.mm(sb_[:, :], KcT[:, h, mt * 128:(mt + 1) * 128], qcT[:, h, :], True, True, [mB, qcTB], [sbB])
                        sl = pti % 3
                        pti += 1
                        P.act(PT[sl], sb_[:, :], AF.Exp, [sbB], [PTB[sl]], scale=c_scale)
                        pts.append(sl)
                    for mt in range(2):
                        sl = pts[mt]
                        P.mm(ob[:, :], Vc[:, mt, h * 128:(h + 1) * 128], PT[sl], mt == 0, mt == 1, [mB, PTB[sl]], [obB])
                        P.mm(db[:, :], ones_bf, PT[sl], mt == 0, mt == 1, [cB, PTB[sl]], [dbB])
                    P.op("dve", lambda e, a=rdn, b=db: e.reciprocal(out=a, in_=b[:, :]), [dbB], [rdnB])
                    P.tt("dve", ocT[:, h, :], ob[:, :], rdn, ALU.mult, [obB, rdnB], [ocTB])
                for t in range(4):
                    ts_ = slice(t * 128, (t + 1) * 128)
                    for half in range(2):
                        hs = slice(half * 512, (half + 1) * 512)
                        bt, bb = P.bank()
                        for h in range(4):
                            P.mm(bt[:, :], ocT[:, h, ts_], Wco[:, h, hs], h == 0, h == 3, [ocTB, wB], [bb])
                        P.tt("dve", xg[g2][:, t, hs], bt[:, :], xg[g2][:, t, hs], ALU.add, [bb, xgB[g2]], [xgB[g2]])
                P.dma("sp", xres[gsl, :].rearrange("(t p) d -> p t d", p=128), xg[g2], [xgB[g2]], [], sem=xgB[g2])
            P.barrier()

            stg = [P.alloc([1024], F32) for _ in range(2)]
            stgB = [P.dbuf("stg%d" % i) for i in range(2)]
            gB = P.dbuf("gains")
            g_f = P.alloc([8], F32)
            P.dma("sp", g_f, w["norm_ffn"][l].rearrange("(c p) -> p c", p=128), [], [gB], sem=gB, slow=True)
            last = (l == n_layers - 1)
            if last:
                gfin = P.alloc([D], F32)
                P.dma("sp", gfin, w["final_norm"].partition_broadcast(128), [], [gB], sem=gB)
            Wgu = P.alloc([8, 2 * FFN], BF16)
            Wd = P.alloc([22, 1024], BF16)
            wB = Buf("wO2")
            load_weight(Wgu, w["w_gu"][l], D, [(i * 1024, min(1024, 2 * FFN - i * 1024), i * 1024) for i in range(6)], gain=g_f, stg=stg, stgB=stgB, wB=wB)
            load_weight(Wd, w["w_down"][l], FFN, [(0, 1024, 0)], stg=stg, stgB=stgB, wB=wB)
            junk = P.alloc([D], BF16)
            junkB = Buf("junk")
            xg = P.alloc([2, D], F32)
            xgB = P.dbuf("xg")
            st2 = P.alloc([8], F32)
            st2B = Buf("st2")
            hb2 = P.alloc([D], BF16)
            hb2B = Buf("hb2")
            hfT = P.alloc([8, 256], BF16)
            hfTB = Buf("hfT")
            sg = [P.alloc([256], BF16) for _ in range(2)]
            sgB = [Buf("sg%d" % i) for i in range(2)]
            aT = P.alloc([22, 256], BF16)
            aTB = Buf("aT")
            for G in range(S // 256):
                gsl = slice(G * 256, (G + 1) * 256)
                P.dma("sp", xg, xres[gsl, :].rearrange("(t p) d -> p t d", p=128), [], [xgB], sem=xgB)
                for t in range(2):
                    ts_ = slice(t * 128, (t + 1) * 128)
                    P.act(junk, xg[:, t, :], AF.Square, [xgB], [junkB, st2B], accum=st2[:, t:t + 1])
                    rms_rstd(st2[:, t:t + 1], 1, D, st2B)
                    P.ts("dve", hb2, xg[:, t, :], st2[:, t:t + 1], None, ALU.mult, None, [xgB, st2B], [hb2B])
                    bt, bb = P.bank()
                    bv = bf16_view(bt)
                    for c in range(8):
                        P.tr(bv[:, c * 128:(c + 1) * 128], hb2[:, c * 128:(c + 1) * 128], ident, [hb2B], [bb])
                    P.cp("dve", hfT[:, :, ts_], bv.rearrange("p (c t) -> p c t", c=8), [bb], [hfTB])
                for c2 in range(22):
                    bg, bgB = P.bank()
                    bu, buB = P.bank()
                    for c in range(8):
                        P.mm(bg[:, 0:256], Wgu[:, c, c2 * 128:(c2 + 1) * 128], hfT[:, c, :], c == 0, c == 7, [wB, hfTB], [bgB])
                    for c in range(8):
                        P.mm(bu[:, 0:256], Wgu[:, c, FFN + c2 * 128:FFN + (c2 + 1) * 128], hfT[:, c, :], c == 0, c == 7, [wB, hfTB], [buB])
                    s2_ = c2 % 2
                    P.act(sg[s2_], bg[:, 0:256], AF.Silu, [bgB], [sgB[s2_]])
                    P.tt("dve", aT[:, c2, :], bu[:, 0:256], sg[s2_], ALU.mult, [buB, sgB[s2_]], [aTB])
                for t in range(2):
                    ts_ = slice(t * 128, (t + 1) * 128)
                    for half in range(2):
                        hs = slice(half * 512, (half + 1) * 512)
                        bt, bb = P.bank()
                        for c2 in range(22):
                            P.mm(bt[:, :], aT[:, c2, ts_], Wd[:, c2, hs], c2 == 0, c2 == 21, [aTB, wB], [bb])
                        P.tt("dve", xg[:, t, hs], bt[:, :], xg[:, t, hs], ALU.add, [bb, xgB], [xgB])
                    rows = slice(G * 256 + t * 128, G * 256 + (t + 1) * 128)
                    if last:
                        P.act(junk, xg[:, t, :], AF.Square, [xgB], [junkB, st2B], accum=st2[:, 4 + t:5 + t])
                        rms_rstd(st2[:, 4 + t:5 + t], 1, D, st2B)
                        P.op("dve", lambda e, b=xg[:, t, :], c=st2[:, 4 + t:5 + t]: e.scalar_tensor_tensor(
                            out=b, in0=b, scalar=c, in1=gfin, op0=ALU.mult, op1=ALU.mult), [xgB, st2B, gB], [xgB])
                        P.dma("sp", out_d[rows, :], xg[:, t, :], [xgB], [], sem=xgB)
                if not last:
                    P.dma("sp", xres[gsl, :].rearrange("(t p) d -> p t d", p=128), xg, [xgB], [], sem=xgB)
            P.barrier()

        P.barrier()
        print('NREC', Prog.nrec)
        with nc.Block() as block:
            P.emit(block)
    return nc


def make_relT(rel_bias):
    ki = np.arange(128)[:, None, None]
    dl = np.arange(5)[None, :, None]
    qi = np.arange(128)[None, None, :]
    idx = np.clip(dl * 128 + qi - ki, -63, 128) + 63
    return np.ascontiguousarray(rel_bias[:, :, idx])


def kernel(**inputs):
    S = inputs["x"].shape[1]
    B = inputs["x"].shape[0]
    nc = build_program(S)
    relT = make_relT(np.asarray(inputs["rel_bias"], dtype=np.float32))
    NT = S // 128
    pos = (np.arange(NT, dtype=np.float32)[None, :] * 128 + np.arange(128, dtype=np.float32)[:, None])
    in_maps = []
    for b in range(B):
        m = {"x": np.ascontiguousarray(inputs["x"][b], dtype=np.float32),
             "mem": np.ascontiguousarray(inputs["mem"][b], dtype=np.float32), "relT": relT}
        for k, v in inputs.items():
            if k in ("x", "mem", "rel_bias"):
                continue
            m[k] = np.ascontiguousarray(v, dtype=np.float32)
        in_maps.append(m)
    res = run_bass_kernel_spmd(nc, in_maps, core_ids=list(range(B)))
    return np.stack([np.asarray(r["out"], dtype=np.float32) for r in res.results], axis=0)
```

```python
import os
import numpy as np
from contextlib import ExitStack
import concourse.bass as bass
import concourse.mybir as mybir
from concourse.bass_utils import run_bass_kernel_spmd

F32 = mybir.dt.float32
BF16 = mybir.dt.bfloat16
I32 = mybir.dt.int32
AF = mybir.ActivationFunctionType
ALU = mybir.AluOpType

D = 1024
NL = 2
EPS = 1e-6
FFN = 2816
NEG = -30000.0
COMPUTE = ("pe", "act", "dve", "pool")
ENGS = ("pe", "act", "dve", "pool", "sp")


class Buf:
    __slots__ = ("name", "w", "r", "sem")

    def __init__(self, name=""):
        self.name = name
        self.w = None
        self.r = {}
        self.sem = None


class Prog:
    def __init__(self, nc, es, arena_words):
        self.nc = nc
        self.es = es
        self.ops = {e: [] for e in ENGS}
        self.esem = {e: es.enter_context(nc.semaphore("s_" + e)) for e in COMPUTE}
        self.dsems = []
        self.free_ds = []
        self.dsems.append([es.enter_context(nc.semaphore("cc")), 0, False])
        self.cc_index = 0
        self.arena = es.enter_context(nc.sbuf_tensor("arena", [128, arena_words], F32))
        self.arena_words = arena_words
        self.persist_top = 0
        self.top = 0
        self.banks = []
        for i in range(8):
            t = es.enter_context(nc.psum_tensor("bank%d" % i, [128, 512], F32))
            self.banks.append((t, Buf("bank%d" % i)))
        self.bank_i = 0

    def alloc(self, shape, dtype, parts=128):
        n = 1
        for s in shape:
            n *= s
        nbytes = n * (2 if dtype == BF16 else 4)
        words = (nbytes + 3) // 4
        words = (words + 7) // 8 * 8
        off = self.top
        self.top += words
        assert self.top <= self.arena_words, ("SBUF arena overflow", self.top * 4)
        v = self.arena[0:parts, off:off + (nbytes + 3) // 4]
        if dtype != F32:
            v = v.bitcast(dtype)
        if len(shape) == 2:
            v = v.rearrange("p (a b) -> p a b", a=shape[0])
        elif len(shape) == 3:
            v = v.rearrange("p (a b c) -> p a b c", a=shape[0], b=shape[1])
        return v

    rot = list(range(8))

    def bank(self):
        i = self.rot[self.bank_i % len(self.rot)]
        self.bank_i += 1
        return self.banks[i]

    def bank_at(self, i):
        return self.banks[i]

    def new_sem(self, buf):
        if self.free_ds:
            buf.sem = self.free_ds.pop()
        else:
            h = self.es.enter_context(self.nc.semaphore("d%d" % len(self.dsems)))
            self.dsems.append([h, 0, False])
            buf.sem = len(self.dsems) - 1
        return buf

    def dbuf(self, name=""):
        return self.new_sem(Buf(name))

    def _deps(self, eng, reads, writes):
        deps = {}

        def add(tok):
            if tok is None:
                return
            k, v = tok
            if k == "pe" and eng == "pe":
                return
            if not isinstance(k, str):
                ds = self.dsems[k[1]]
                v = ds[1]
                ds[2] = True
            if deps.get(k, -1) < v:
                deps[k] = v

        for b in reads:
            add(b.w)
        for b in writes:
            add(b.w)
            for t in b.r.values():
                add(t)
        return deps

    nrec = 0
    mute = False
    maxops = int(os.environ.get('K_MAXOPS', '0'))

    def op(self, eng, fn, reads=(), writes=()):
        Prog.nrec += 1
        if Prog.mute or (Prog.maxops and Prog.nrec > Prog.maxops):
            return None
        deps = self._deps(eng, reads, writes)
        tok = (eng, len(self.ops[eng]))
        self.ops[eng].append([fn, deps, False, None, 0])
        for b in reads:
            b.r[eng] = tok
        for b in writes:
            b.w = tok
            b.r = {}
        return tok

    def dma(self, q, out, in_, reads=(), writes=(), sem=None, slow=False, fn=None, inc=16):
        Prog.nrec += 1
        if Prog.mute or (Prog.maxops and Prog.nrec > Prog.maxops):
            return None
        deps = self._deps(q, reads, writes)
        s = sem.sem
        ds = self.dsems[s]
        key = ("d", s)
        if ds[2]:
            if deps.get(key, -1) < ds[1]:
                deps[key] = ds[1]
            ds[2] = False
        ds[1] += inc
        tok = (key, ds[1])
        if fn is not None:
            pass
        elif slow:
            fn = lambda e: e.dma_start(out=out, in_=in_, allow_slow_non_contiguous=True)
        else:
            fn = lambda e: e.dma_start(out=out, in_=in_)
        self.ops[q].append([fn, deps, False, (s, inc), 0])
        for b in reads:
            b.r[key] = tok
        for b in writes:
            b.w = tok
            b.r = {}
        return tok

    def barrier(self):
        deps = {}
        for e in COMPUTE:
            i = len(self.ops[e]) - 1
            while i >= 0 and (self.ops[e][i][0] is None or self.ops[e][i][3] is not None):
                i -= 1
            if i >= 0:
                deps[e] = i
        for i, ds in enumerate(self.dsems):
            if ds[1] > 0:
                deps[("d", i)] = ds[1]
                ds[2] = False
        for e in ENGS:
            d = dict(deps)
            if e == "pe":
                d.pop("pe", None)
            self.ops[e].append([None, d, False, None, 0])
        self.top = self.persist_top
        self.free_ds = [i for i in range(len(self.dsems)) if i != self.cc_index]
        for t, b in self.banks:
            b.w = None
            b.r = {}

    def emit(self, block):
        for e in ENGS:
            w = {}
            for o in self.ops[e]:
                nd = []
                for k, v in o[1].items():
                    if w.get(k, -1) >= v:
                        continue
                    w[k] = v
                    nd.append((k, v))
                    if isinstance(k, str):
                        self.ops[k][v][2] = True
                o[1] = nd
        for e in COMPUTE:
            c = 0
            for o in self.ops[e]:
                if o[2]:
                    c += 1
                o[4] = c
        prog = self
        if os.environ.get('K_DUMP'):
            for e in ENGS:
                for i, o in enumerate(self.ops[e]):
                    ws = [(k, (self.ops[k][v][4] if isinstance(k, str) else v)) for k, v in o[1]]
                    print(e, i, 'waits', ws, 'fn' if o[0] else 'nofn', 'inc' if o[2] else '', 'dma%s' % (o[3],) if o[3] is not None else '', 'val', o[4])

        def body(e):
            def f(h):
                for o in prog.ops[e]:
                    for k, v in o[1]:
                        if isinstance(k, str):
                            h.wait_ge(prog.esem[k], prog.ops[k][v][4])
                        else:
                            h.wait_ge(prog.dsems[k[1]][0], v)
                    if o[0] is None:
                        continue
                    ins = o[0](h)
                    if o[3] is not None:
                        ins.then_inc(prog.dsems[o[3][0]][0], o[3][1])
                    elif o[2]:
                        ins.then_inc(prog.esem[e], 1)
            return f

        block.sync(body("sp"))
        block.tensor(body("pe"))
        block.scalar(body("act"))
        block.vector(body("dve"))
        block.gpsimd(body("pool"))

    def mm(self, out, lhsT, rhs, start, stop, reads, writes):
        return self.op("pe", lambda e: e.matmul(out, lhsT=lhsT, rhs=rhs, start=start, stop=stop), reads, writes)

    def tr(self, out, in_, ident, reads, writes):
        return self.op("pe", lambda e: e.transpose(out=out, in_=in_, identity=ident), reads, writes)

    def act(self, out, in_, func, reads, writes, bias=None, scale=1.0, accum=None):
        kw = {}
        if bias is not None:
            kw["bias"] = bias
        if accum is not None:
            kw["accum_out"] = accum
        return self.op("act", lambda e: e.activation(out=out, in_=in_, func=func, scale=scale, **kw), reads, writes)

    def ts(self, eng, out, in0, s1, s2, op0, op1, reads, writes):
        if op1 is None:
            return self.op(eng, lambda e: e.tensor_scalar(out=out, in0=in0, scalar1=s1, scalar2=None, op0=op0), reads, writes)
        return self.op(eng, lambda e: e.tensor_scalar(out=out, in0=in0, scalar1=s1, scalar2=s2, op0=op0, op1=op1), reads, writes)

    def tt(self, eng, out, in0, in1, op, reads, writes):
        return self.op(eng, lambda e: e.tensor_tensor(out=out, in0=in0, in1=in1, op=op), reads, writes)

    def cp(self, eng, out, in_, reads, writes):
        if eng == "act":
            return self.op("act", lambda e: e.copy(out=out, in_=in_), reads, writes)
        return self.op(eng, lambda e: e.tensor_copy(out=out, in_=in_), reads, writes)

    def memset(self, eng, ap, val, writes):
        return self.op(eng, lambda e: e.memset(ap, val), (), writes)


def bf16_view(bank_t, parts=128):
    return bank_t[0:parts, :].bitcast(BF16)


def build_program(S, n_layers=NL, debug=False):
    SL = S // 2
    NT = SL // 128
    NG = SL // 512
    NTG = S // 128
    KROWS = 920
    nc = bass.Bass("TRN2", target_bir_lowering=False)

    def din(name, shape, dt=F32):
        return nc.dram_tensor(name, list(shape), dt, kind="ExternalInput").ap()

    x_in = din("x", [SL, D])
    rank_in = din("rank", [128, 2])
    mem_in = din("mem", [256, D])
    w = {}
    for name, shape in [("norm_mix", [NL, D]), ("w_in", [NL, D, 1956]), ("q_norm", [NL, 256]),
                        ("w_uq", [NL, 256, 384]), ("kv_norm", [NL, 128]), ("w_ukv", [NL, 128, 768]),
                        ("f_bias", [NL, 4]), ("relT", [NL, 4, 128, 5, 128]), ("out_norm", [NL, D]),
                        ("w_o", [NL, D, D]), ("norm_cross", [NL, D]), ("norm_mem", [NL, D]),
                        ("w_cq", [NL, D, 512]), ("w_ckv", [NL, D, 1024]), ("w_co", [NL, 512, D]),
                        ("norm_ffn", [NL, D]), ("w_gu", [NL, D, 2 * FFN]), ("w_down", [NL, FFN, D]),
                        ("final_norm", [D])]:
        w[name] = din(name, shape)
    out_d = nc.dram_tensor("out", [SL, D], F32, kind="ExternalOutput").ap()
    skind = "ExternalOutput" if debug else "Internal"

    def dscr(name, shape, dt):
        return nc.dram_tensor(name, list(shape), dt, kind=skind).ap()

    xres = dscr("xres", [SL, D], F32)
    qTm = dscr("qTm", [4, 96, SL], BF16)
    qTf = dscr("qTf", [4, 70, SL], BF16)
    qTc = dscr("qTc", [256, SL], BF16)
    kloc = dscr("kloc", [KROWS, SL], BF16)
    vloc = dscr("vloc", [SL, 1024], BF16)
    totloc = dscr("totloc", [128, 256], F32)
    KCH = [(0, 192), (192, 192), (384, 140), (524, 140), (664, 128), (792, 128)]
    NVC = max(1, SL // 1024)
    VR = SL // NVC
    kgat = [[nc.dram_tensor("kgat%d_%d" % (i, c), [2 * n, SL], BF16, kind="Internal").ap() for c, (st_, n) in enumerate(KCH)]
            for i in range(n_layers)]
    vgat = [[nc.dram_tensor("vgat%d_%d" % (i, c), [2 * VR, 1024], BF16, kind="Internal").ap() for c in range(NVC)]
            for i in range(n_layers)]
    tgat = [nc.dram_tensor("tgat%d" % i, [256, 256], F32, kind="Internal").ap() for i in range(n_layers)]
    ymixT = dscr("ymixT", [D, SL], BF16)
    ropeD = dscr("ropeD", [SL, 128], F32)
    RG = [[0, 1], [2, 3], [4, 5], [6, 7]]

    with ExitStack() as es:
        P = Prog(nc, es, arena_words=50 * 1024)
        ident = P.alloc([128], BF16)
        ones_bf = P.alloc([128], BF16)
        identf = P.alloc([128], F32)
        onesf = P.alloc([128], F32)
        triu = P.alloc([128], F32)
        triu_bf = P.alloc([128], BF16)
        eps_t = P.alloc([1], F32)
        one_t = P.alloc([1], F32)
        npi_t = P.alloc([1], F32)
        mbm = P.alloc([4, 512], BF16)
        mbf = P.alloc([4, 512], BF16)
        mbm8 = P.alloc([8, 512], BF16)
        mbf8 = P.alloc([8, 512], BF16)
        rk = P.alloc([8], F32)
        sut = P.alloc([64], F32)
        cB = Buf("consts")
        P.memset("pool", identf, 1.0, [cB])
        P.op("pool", lambda e: e.affine_select(out=identf, in_=identf, pattern=[[-1, 128]], compare_op=ALU.is_equal,
                                                fill=0.0, base=0, channel_multiplier=1), [cB], [cB])
        P.cp("pool", ident, identf, [cB], [cB])
        P.memset("pool", onesf, 1.0, [cB])
        P.memset("pool", ones_bf, 1.0, [cB])
        P.memset("pool", triu, 1.0, [cB])
        P.op("pool", lambda e: e.affine_select(out=triu, in_=triu, pattern=[[1, 128]], compare_op=ALU.is_ge,
                                                fill=0.0, base=0, channel_multiplier=-1), [cB], [cB])
        P.cp("pool", triu_bf, triu, [cB], [cB])
        P.memset("pool", eps_t, EPS, [cB])
        P.memset("pool", one_t, 1.0, [cB])
        P.memset("pool", npi_t, -np.pi, [cB])
        P.memset("pool", mbm, 0.0, [cB])
        P.memset("pool", mbf, 0.0, [cB])
        for j in range(4):
            if j > 0:
                P.memset("pool", mbm[:, j, 0:j * 128], NEG, [cB])
                P.memset("pool", mbf[:, j, 0:j * 128], NEG, [cB])
            P.memset("pool", mbm[64:128, j, j * 128:j * 128 + 64], NEG, [cB])
            blk = mbf[:, j, j * 128:(j + 1) * 128]
            P.op("pool", (lambda blk: lambda e: e.affine_select(out=blk, in_=blk, pattern=[[1, 128]], compare_op=ALU.is_ge,
                                                                 fill=NEG, base=0, channel_multiplier=-1))(blk), [cB], [cB])
        rkB = P.dbuf("rank")
        P.dma("sp", rk[:, 0:2], rank_in, [], [rkB], sem=rkB)
        P.ts("dve", rk[:, 2:3], rk[:, 1:2], NEG, None, ALU.mult, None, [rkB], [cB])
        P.ts("dve", rk[:, 3:4], rk[:, 0:1], NEG, None, ALU.mult, None, [rkB], [cB])
        P.ts("dve", rk[:, 4:5], rk[:, 0:1], 512.0, None, ALU.mult, None, [rkB], [cB])
        for m in range(8):
            for (src, dst) in ((mbm, mbm8), (mbf, mbf8)):
                if m < 4:
                    P.ts("dve", dst[:, m, :], src[:, m, :], rk[:, 1:2], None, ALU.mult, None, [cB], [cB])
                else:
                    P.ts("dve", dst[:, m, :], src[:, m - 4, :], rk[:, 0:1], rk[:, 2:3], ALU.mult, ALU.add, [cB], [cB])
        P.memset("pool", sut[0:64, :], 1.0, [cB])
        P.op("pool", lambda e: e.affine_select(out=sut[0:64, :], in_=sut[0:64, :], pattern=[[1, 64]], compare_op=ALU.is_gt,
                                                fill=0.0, base=0, channel_multiplier=-1), [cB], [cB])
        P.persist_top = P.top
        if os.environ.get('K_STOP', '') != 'c1':
          if True:
            rope = P.alloc([NT, 128], F32)
            posi = P.alloc([NT], I32)
            posf = P.alloc([NT], F32)
            invf = P.alloc([16], F32)
            ang = P.alloc([NT, 16], F32)
            ang2 = P.alloc([NT, 16], F32)
            sn = P.alloc([NT, 16], F32)
            csn = P.alloc([NT, 16], F32)
            P.op("pool", lambda e: e.iota(posi.rearrange("p (j t) -> p j t", t=4), pattern=[[1024, NT // 4], [128, 4]], base=0,
                                           channel_multiplier=1), [], [cB])
            P.cp("pool", posf, posi, [cB], [cB])
            P.op("dve", lambda e: e.scalar_tensor_tensor(out=posf, in0=onesf[:, 0:NT], scalar=rk[:, 4:5], in1=posf,
                                                          op0=ALU.mult, op1=ALU.add), [cB], [cB])
            for i in range(16):
                P.memset("pool", invf[:, i:i + 1], float(np.float32(10000.0) ** np.float32(-(2.0 * i) / 32.0)), [cB])
            for t in range(NT):
                P.ts("dve", ang[:, t, :], invf, posf[:, t:t + 1], None, ALU.mult, None, [cB], [cB])
            kint = P.alloc([NT, 16], I32)
            kf = P.alloc([NT, 16], F32)
            TWO_PI = float(2 * np.pi)

            def sin_of(dst, shift):
                if shift:
                    P.ts("dve", ang2, ang, float(shift), None, ALU.add, None, [cB], [cB])
                    src = ang2
                else:
                    src = ang
                P.ts("dve", kf, src, 1.0 / TWO_PI, None, ALU.mult, None, [cB], [cB])
                P.cp("dve", kint, kf, [cB], [cB])
                P.cp("dve", kf, kint, [cB], [cB])
                P.op("dve", lambda e: e.scalar_tensor_tensor(out=kf, in0=kf, scalar=-TWO_PI, in1=src, op0=ALU.mult, op1=ALU.add), [cB], [cB])
                P.ts("dve", dst, kf, float(np.pi), TWO_PI, ALU.is_gt, ALU.mult, [cB], [cB])
                P.tt("dve", kf, kf, dst, ALU.subtract, [cB], [cB])
                P.ts("dve", dst, kf, -float(np.pi), TWO_PI, ALU.is_lt, ALU.mult, [cB], [cB])
                P.tt("dve", kf, kf, dst, ALU.add, [cB], [cB])
                P.act(dst, kf, AF.Sin, [cB], [cB])

            sin_of(sn, 0.0)
            sin_of(csn, np.pi / 2)
            for hh in range(4):
                P.cp("pool", rope[:, :, hh * 16:(hh + 1) * 16], csn, [cB], [cB])
                P.cp("pool", rope[:, :, 64 + hh * 16:64 + (hh + 1) * 16], sn, [cB], [cB])
            rpB = P.dbuf("ropeout")
            P.dma("sp", ropeD.rearrange("(t p) c -> p t c", p=128), rope, [cB], [], sem=rpB)
        P.barrier()
        STOP = os.environ.get('K_STOP', '')

        def load_weight(dst, src, K, segs, gain=None, stg=None, stgB=None, wB=None, engs=("dve", "act")):
            i = 0
            for c in range(K // 128):
                for (s0, n, d0) in segs:
                    slot = i % len(stg)
                    P.dma("sp", stg[slot][:, 0:n], src[c * 128:(c + 1) * 128, s0:s0 + n], [], [stgB[slot]], sem=stgB[slot])
                    eng = engs[i % len(engs)]
                    o = dst[:, c, d0:d0 + n]
                    src_t = stg[slot][:, 0:n]
                    if gain is not None:
                        if eng == "act":
                            P.act(o, src_t, AF.Copy, [stgB[slot]], [wB], scale=gain[:, c:c + 1])
                        else:
                            P.ts("dve", o, src_t, gain[:, c:c + 1], None, ALU.mult, None, [stgB[slot]], [wB])
                    else:
                        P.cp(eng, o, src_t, [stgB[slot]], [wB])
                    i += 1

        def load_gain(dst, src_vec, K, gB):
            if K >= 128:
                P.dma("sp", dst, src_vec.rearrange("(c p) -> p c", p=128), [], [gB], sem=gB)
            return dst

        def rms_rstd(ss, n, dim, rB):
            P.act(ss, ss, AF.Ln, [rB], [rB], bias=eps_t[:, 0:1], scale=1.0 / dim)
            P.act(ss, ss, AF.Exp, [rB], [rB], scale=-0.5)

        for l in range(n_layers if STOP not in ('consts', 'c1') else 0):
            x_src = x_in if l == 0 else xres
            Prog.mute = os.environ.get('K_MUTE', '') == 'P'
            stg = [P.alloc([2048], F32) for _ in range(2)]
            stgB = [P.dbuf("stg%d" % i) for i in range(2)]
            gB = P.dbuf("gains")
            g_mix = P.alloc([8], F32)
            g_q = P.alloc([2], F32)
            g_kv = P.alloc([1], F32)
            fb = P.alloc([4], F32)
            P.dma("sp", g_mix, w["norm_mix"][l].rearrange("(c p) -> p c", p=128), [], [gB], sem=gB, slow=True)
            P.dma("sp", g_q, w["q_norm"][l].rearrange("(c p) -> p c", p=128), [], [gB], sem=gB, slow=True)
            P.dma("sp", g_kv, w["kv_norm"][l].rearrange("(c p) -> p c", p=128), [], [gB], sem=gB, slow=True)
            P.dma("sp", fb, w["f_bias"][l].partition_broadcast(128), [], [gB], sem=gB)
            Win = P.alloc([8, 1956], BF16)
            Wuq = P.alloc([2, 384], BF16)
            Wukv = P.alloc([1, 768], BF16)
            wB = Buf("wP")
            segs_in = [(0, 416, 0), (1184, 4, 416), (416, 512, 420), (928, 256, 932), (1700, 256, 1188),
                       (1188, 512, 1444)]
            load_weight(Win, w["w_in"][l], D, segs_in, gain=g_mix, stg=stg, stgB=stgB, wB=wB)
            segs_uq = []
            for h in range(4):
                segs_uq += [(h * 96, 64, h * 64), (h * 96 + 64, 16, 256 + h * 16), (h * 96 + 80, 16, 320 + h * 16)]
            load_weight(Wuq, w["w_uq"][l], 256, segs_uq, gain=g_q, stg=stg, stgB=stgB, wB=wB)
            segs_ukv = []
            for h in range(4):
                segs_ukv += [(h * 192, 64, h * 64), (h * 192 + 64, 128, 256 + h * 128)]
            load_weight(Wukv, w["w_ukv"][l], 128, segs_ukv, gain=g_kv, stg=stg, stgB=stgB, wB=wB)

            if STOP == 'weights':
                break
            NX = 3
            xs = [P.alloc([D], F32) for _ in range(NX)]
            xsB = [P.dbuf("xs%d" % i) for i in range(NX)]
            rps = [P.alloc([128], F32) for _ in range(NX)]
            junk = P.alloc([D], BF16)
            junkB = Buf("junk")
            st = [P.alloc([8], F32) for _ in range(2)]
            stB = [Buf("st%d" % i) for i in range(2)]
            hb = [P.alloc([D], BF16) for _ in range(2)]
            hbB = [Buf("hb%d" % i) for i in range(2)]
            hT = [P.alloc([D], BF16) for _ in range(2)]
            hTB = [Buf("hT%d" % i) for i in range(2)]
            cn = [P.alloc([384], BF16) for _ in range(2)]
            cnB = [Buf("cn%d" % i) for i in range(2)]
            cT = [P.alloc([384], BF16) for _ in range(2)]
            cTB = [Buf("cT%d" % i) for i in range(2)]
            tmp = [P.alloc([6, 64], F32) for _ in range(2)]
            tmpB = [Buf("tmp%d" % i) for i in range(2)]
            Qm = [P.alloc([4, 96], BF16) for _ in range(2)]
            Km = [P.alloc([4, 96], BF16) for _ in range(2)]
            Qf = [P.alloc([4, 70], BF16) for _ in range(2)]
            Kf = [P.alloc([4, 70], BF16) for _ in range(2)]
            Qc = [P.alloc([256], BF16) for _ in range(2)]
            Kc = [P.alloc([256], BF16) for _ in range(2)]
            tmB = [Buf("tm%d" % i) for i in range(2)]
            fx = [P.alloc([24], F32) for _ in range(2)]
            fxh = [P.alloc([3, 4], BF16) for _ in range(2)]
            fxs = [P.alloc([3, 4], BF16) for _ in range(2)]
            fxk = [P.alloc([3, 4], BF16) for _ in range(2)]
            ttsb = P.alloc([256], F32)
            ttB = P.dbuf("ttsb")
            P.memset("pool", ttsb, 0.0, [ttB])
            fxB = [Buf("fx%d" % i) for i in range(2)]
            tot = P.alloc([4], F32)
            totB = Buf("tot")
            P.memset("pool", tot, 0.0, [totB])
            for i in range(2):
                P.memset("pool", Qf[i][:, :, 67:70], 1.0, [tmB[i]])
                P.memset("pool", Kf[i][:, :, 64:67], 1.0, [tmB[i]])
            sQm = [P.alloc([4, 512], BF16) for _ in range(2)]
            sKm = [P.alloc([4, 512], BF16) for _ in range(2)]
            sQf = [P.alloc([4, 512], BF16) for _ in range(2)]
            sKf = [P.alloc([4, 512], BF16) for _ in range(2)]
            sQc = [P.alloc([2, 512], BF16) for _ in range(2)]
            sKc = [P.alloc([2, 512], BF16) for _ in range(2)]
            sV = [P.alloc([4, 1024], BF16) for _ in range(2)]
            sB = [P.dbuf("stage%d" % i) for i in range(2)]

            for t in range(NT):
                G, tt_ = divmod(t, 4)
                gs = G % 2
                i2 = t % 2
                xi = t % NX
                P.dma("sp", xs[xi], x_src[t * 128:(t + 1) * 128, :], [], [xsB[xi]], sem=xsB[xi])
                P.dma("sp", rps[xi], ropeD[t * 128:(t + 1) * 128, :], [], [xsB[xi]], sem=xsB[xi])
                P.act(junk, xs[xi], AF.Square, [xsB[xi]], [junkB, stB[i2]], accum=st[i2][:, 0:1])
                rms_rstd(st[i2][:, 0:1], 1, D, stB[i2])
                P.ts("dve", hb[i2], xs[xi], st[i2][:, 0:1], None, ALU.mult, None, [xsB[xi], stB[i2]], [hbB[i2]])
                bt, bb = P.bank()
                bv = bf16_view(bt)
                for c in range(8):
                    P.tr(bv[:, c * 128:(c + 1) * 128], hb[i2][:, c * 128:(c + 1) * 128], ident, [hbB[i2]], [bb])
                P.cp("dve", hT[i2], bv, [bb], [hTB[i2]])
                chunks = [(0, 420), (420, 512), (932, 512), (1444, 512)]
                pb = []
                for (c0, n) in chunks:
                    bt2, bb2 = P.bank()
                    for c in range(8):
                        P.mm(bt2[:, 0:n], hT[i2][:, c * 128:(c + 1) * 128], Win[:, c, c0:c0 + n], c == 0, c == 7,
                             [hTB[i2], wB], [bb2])
                    pb.append((bt2, bb2))
                (b0, b0B), (b1, b1B), (b2, b2B), (b3, b3B) = pb
                s2 = st[i2]
                P.act(junk[:, 0:256], b0[:, 0:256], AF.Square, [b0B], [junkB, stB[i2]], accum=s2[:, 1:2])
                P.act(junk[:, 0:128], b0[:, 256:384], AF.Square, [b0B], [junkB, stB[i2]], accum=s2[:, 2:3])
                P.act(s2[:, 1:2], s2[:, 1:2], AF.Ln, [stB[i2]], [stB[i2]], bias=eps_t[:, 0:1], scale=1.0 / 256)
                P.act(s2[:, 2:3], s2[:, 2:3], AF.Ln, [stB[i2]], [stB[i2]], bias=eps_t[:, 0:1], scale=1.0 / 128)
                P.act(s2[:, 1:3], s2[:, 1:3], AF.Exp, [stB[i2]], [stB[i2]], scale=-0.5)
                P.ts("dve", cn[i2][:, 0:256], b0[:, 0:256], s2[:, 1:2], None, ALU.mult, None, [b0B, stB[i2]], [cnB[i2]])
                P.ts("dve", cn[i2][:, 256:384], b0[:, 256:384], s2[:, 2:3], None, ALU.mult, None, [b0B, stB[i2]], [cnB[i2]])
                bt3, bb3 = P.bank()
                bv3 = bf16_view(bt3)
                for c in range(3):
                    P.tr(bv3[:, c * 128:(c + 1) * 128], cn[i2][:, c * 128:(c + 1) * 128], ident, [cnB[i2]], [bb3])
                P.cp("dve", cT[i2], bv3[:, 0:384], [bb3], [cTB[i2]])
                bq, bqB = P.bank()
                for c in range(2):
                    P.mm(bq[:, 0:384], cT[i2][:, c * 128:(c + 1) * 128], Wuq[:, c, :], c == 0, c == 1, [cTB[i2], wB], [bqB])
                bk, bkB = P.bank()
                P.mm(bk[:, 0:256], cT[i2][:, 256:384], Wukv[:, 0, 0:256], True, True, [cTB[i2], wB], [bkB])
                bvv, bvB = P.bank()
                P.mm(bvv[:, 0:512], cT[i2][:, 256:384], Wukv[:, 0, 256:768], True, True, [cTB[i2], wB], [bvB])
                cos4 = rps[xi][:, 0:64]
                sin4 = rps[xi][:, 64:128]
                tm_ = tmp[i2]
                qv = Qm[i2]
                kv_ = Km[i2]
                P.tt("dve", tm_[:, 0, :], bq[:, 256:320], cos4, ALU.mult, [bqB, xsB[xi]], [tmpB[i2]])
                P.tt("dve", tm_[:, 1, :], bq[:, 320:384], sin4, ALU.mult, [bqB, xsB[xi]], [tmpB[i2]])
                P.tt("dve", tm_[:, 2, :], bq[:, 256:320], sin4, ALU.mult, [bqB, xsB[xi]], [tmpB[i2]])
                P.tt("dve", tm_[:, 3, :], bq[:, 320:384], cos4, ALU.mult, [bqB, xsB[xi]], [tmpB[i2]])
                P.tt("dve", qv[:, :, 64:80], tm_[:, 0, :].rearrange("p (h d) -> p h d", h=4),
                     tm_[:, 1, :].rearrange("p (h d) -> p h d", h=4), ALU.subtract, [tmpB[i2]], [tmB[i2]])
                P.tt("dve", qv[:, :, 80:96], tm_[:, 2, :].rearrange("p (h d) -> p h d", h=4),
                     tm_[:, 3, :].rearrange("p (h d) -> p h d", h=4), ALU.add, [tmpB[i2]], [tmB[i2]])
                P.cp("act", qv[:, :, 0:64], bq[:, 0:256].rearrange("p (h d) -> p h d", h=4), [bqB], [tmB[i2]])
                P.tt("dve", tm_[:, 4, 0:16], b0[:, 384:400], cos4[:, 0:16], ALU.mult, [b0B, xsB[xi]], [tmpB[i2]])
                P.tt("dve", tm_[:, 4, 16:32], b0[:, 400:416], sin4[:, 0:16], ALU.mult, [b0B, xsB[xi]], [tmpB[i2]])
                P.tt("dve", tm_[:, 4, 32:48], b0[:, 384:400], sin4[:, 0:16], ALU.mult, [b0B, xsB[xi]], [tmpB[i2]])
                P.tt("dve", tm_[:, 4, 48:64], b0[:, 400:416], cos4[:, 0:16], ALU.mult, [b0B, xsB[xi]], [tmpB[i2]])
                P.tt("dve", tm_[:, 5, 0:16], tm_[:, 4, 0:16], tm_[:, 4, 16:32], ALU.subtract, [tmpB[i2]], [tmpB[i2]])
                P.tt("dve", tm_[:, 5, 16:32], tm_[:, 4, 32:48], tm_[:, 4, 48:64], ALU.add, [tmpB[i2]], [tmpB[i2]])
                for h in range(4):
                    P.cp("pool", kv_[:, h, 64:96], tm_[:, 5, 0:32], [tmpB[i2]], [tmB[i2]])
                P.cp("act", kv_[:, :, 0:64], bk[:, 0:256].rearrange("p (h d) -> p h d", h=4), [bkB], [tmB[i2]])
                P.cp("act", sV[gs][:, tt_, 0:512], bvv[:, 0:512], [bvB], [sB[gs]])
                f = fx[i2]
                P.tt("dve", f[:, 0:4], b0[:, 416:420], fb, ALU.add, [b0B, gB], [fxB[i2]])
                P.act(f[:, 0:4], f[:, 0:4], AF.Exp, [fxB[i2]], [fxB[i2]], scale=-1.0)
                P.act(f[:, 4:8], f[:, 0:4], AF.Ln, [fxB[i2]], [fxB[i2]], bias=one_t[:, 0:1], scale=1.0)
                spb = fxs[i2]
                P.cp("dve", spb[:, 0, :], f[:, 4:8], [fxB[i2]], [fxB[i2]])
                P.tt("dve", f[:, 12:16], f[:, 4:8], spb[:, 0, :], ALU.subtract, [fxB[i2]], [fxB[i2]])
                P.cp("dve", spb[:, 1, :], f[:, 12:16], [fxB[i2]], [fxB[i2]])
                P.tt("dve", f[:, 16:20], f[:, 12:16], spb[:, 1, :], ALU.subtract, [fxB[i2]], [fxB[i2]])
                P.cp("dve", spb[:, 2, :], f[:, 16:20], [fxB[i2]], [fxB[i2]])
                bc, bcB = P.bank()
                for pc in range(3):
                    P.mm(bc[:, 0:4], triu_bf, spb[:, pc, :], pc == 0, pc == 2, [fxB[i2], cB], [bcB])
                for pc in range(3):
                    P.mm(bc[:, 8:12], ones_bf, spb[:, pc, :], pc == 0, pc == 2, [fxB[i2], cB], [bcB])
                P.cp("dve", f[:, 20:24], bc[:, 0:4], [bcB], [fxB[i2]])
                P.cp("dve", ttsb[:, t * 4:(t + 1) * 4], bc[:, 8:12], [bcB], [ttB])
                if tt_ == 0:
                    P.cp("dve", f[:, 8:12], bc[:, 0:4], [bcB], [fxB[i2]])
                    P.cp("dve", tot, bc[:, 8:12], [bcB], [totB])
                else:
                    P.tt("dve", f[:, 8:12], bc[:, 0:4], tot, ALU.add, [bcB, totB], [fxB[i2]])
                    P.tt("dve", tot, bc[:, 8:12], tot, ALU.add, [bcB, totB], [totB])
                for (srcc, ph) in ((f[:, 8:12], fxh[i2]), (f[:, 20:24], fxk[i2])):
                    P.cp("dve", ph[:, 0, :], srcc, [fxB[i2]], [fxB[i2]])
                    P.tt("dve", f[:, 12:16], srcc, ph[:, 0, :], ALU.subtract, [fxB[i2]], [fxB[i2]])
                    P.cp("dve", ph[:, 1, :], f[:, 12:16], [fxB[i2]], [fxB[i2]])
                    P.tt("dve", f[:, 16:20], f[:, 12:16], ph[:, 1, :], ALU.subtract, [fxB[i2]], [fxB[i2]])
                    P.cp("dve", ph[:, 2, :], f[:, 16:20], [fxB[i2]], [fxB[i2]])
                for pc in range(3):
                    P.cp("pool", Kf[i2][:, :, 67 + pc:68 + pc], fxk[i2][:, pc, :].rearrange("p (h o) -> p h o", o=1),
                         [fxB[i2]], [tmB[i2]])
                    P.ts("dve", Qf[i2][:, :, 64 + pc:65 + pc], fxh[i2][:, pc, :].rearrange("p (h o) -> p h o", o=1),
                         -1.0, None, ALU.mult, None, [fxB[i2]], [tmB[i2]])
                P.ts("dve", Qf[i2][:, :, 0:64], b1[:, 0:256].rearrange("p (h d) -> p h d", h=4), 0.125, None, ALU.mult, None,
                     [b1B], [tmB[i2]])
                P.cp("act", Kf[i2][:, :, 0:64], b1[:, 256:512].rearrange("p (h d) -> p h d", h=4), [b1B], [tmB[i2]])
                P.cp("act", sV[gs][:, tt_, 512:1024], b2[:, 0:512], [b2B], [sB[gs]])
                P.ts("dve", Qc[i2], b3[:, 0:256], 0.125, None, ALU.mult, None, [b3B], [tmB[i2]])
                P.cp("act", Kc[i2], b3[:, 256:512], [b3B], [tmB[i2]])
                cs = slice(tt_ * 128, (tt_ + 1) * 128)
                for (src, dst, rows) in ((Qm[i2], sQm[gs], 96), (Km[i2], sKm[gs], 96), (Qf[i2], sQf[gs], 70), (Kf[i2], sKf[gs], 70)):
                    btx, bbx = P.bank()
                    bvx = bf16_view(btx)
                    for h in range(4):
                        P.tr(bvx[0:rows, h * 128:(h + 1) * 128], src[:, h, :], ident, [tmB[i2]], [bbx])
                    P.cp("dve" if rows == 96 else "act", dst[0:rows, :, cs], bvx[0:rows, 0:512].rearrange("p (h t) -> p h t", h=4),
                         [bbx], [sB[gs]])
                btx, bbx = P.bank()
                bvx = bf16_view(btx)
                for c in range(2):
                    P.tr(bvx[:, c * 128:(c + 1) * 128], Qc[i2][:, c * 128:(c + 1) * 128], ident, [tmB[i2]], [bbx])
                    P.tr(bvx[:, (2 + c) * 128:(3 + c) * 128], Kc[i2][:, c * 128:(c + 1) * 128], ident, [tmB[i2]], [bbx])
                P.cp("dve", sQc[gs][:, :, cs], bvx[:, 0:256].rearrange("p (h t) -> p h t", h=2), [bbx], [sB[gs]])
                P.cp("act", sKc[gs][:, :, cs], bvx[:, 256:512].rearrange("p (h t) -> p h t", h=2), [bbx], [sB[gs]])
                if tt_ == 3:
                    gsl = slice(G * 512, (G + 1) * 512)
                    q = "sp"
                    for h in range(4):
                        P.dma(q, qTm[h, :, gsl], sQm[gs][0:96, h, :], [sB[gs]], [], sem=sB[gs])
                        P.dma(q, kloc[h * 96:(h + 1) * 96, gsl], sKm[gs][0:96, h, :], [sB[gs]], [], sem=sB[gs])
                        P.dma(q, qTf[h, :, gsl], sQf[gs][0:70, h, :], [sB[gs]], [], sem=sB[gs])
                        P.dma(q, kloc[384 + h * 70:384 + (h + 1) * 70, gsl], sKf[gs][0:70, h, :], [sB[gs]], [], sem=sB[gs])
                    for c in range(2):
                        P.dma(q, qTc[c * 128:(c + 1) * 128, gsl], sQc[gs][:, c, :], [sB[gs]], [], sem=sB[gs])
                        P.dma(q, kloc[664 + c * 128:664 + (c + 1) * 128, gsl], sKc[gs][:, c, :], [sB[gs]], [], sem=sB[gs])
                    rows = slice(G * 512, (G + 1) * 512)
                    P.dma(q, vloc[rows, :].rearrange("(t p) c -> p t c", p=128), sV[gs], [sB[gs]], [], sem=sB[gs])
            P.dma("sp", totloc[:, :], ttsb, [ttB], [], sem=ttB)
            P.barrier()
            Prog.mute = False
            if STOP == 'p':
                break
            ccB = Buf("cc")
            ccB.sem = P.cc_index
            gatB = Buf("gathered")
            pieces = [(kloc[st_:st_ + n, :], kgat[l][c]) for c, (st_, n) in enumerate(KCH)]
            pieces += [(vloc[c * VR:(c + 1) * VR, :], vgat[l][c]) for c in range(NVC)]
            pieces.append((totloc, tgat[l]))
            for (src_ap, dst_ap) in pieces:
                P.dma("pool", None, None, [], [gatB], sem=ccB, inc=1,
                      fn=(lambda a, b: lambda e: e.collective_compute("AllGather", ALU.bypass, replica_groups=RG,
                                                                      ins=[a.opt()], outs=[b.opt()]))(src_ap, dst_ap))
            P.barrier()

            if STOP == 'cc':
                break
            P.rot = [4, 5, 6, 7]
            slots = []
            for i in range(2):
                slots.append((P.alloc([S], BF16), P.alloc([SL], BF16), P.alloc([NTG, 128], BF16), P.dbuf("kv%d" % i)))
            PT = [P.alloc([512], BF16) for _ in range(3)]
            PTB = [Buf("pt%d" % i) for i in range(3)]
            rden = [P.alloc([512], F32) for _ in range(2)]
            rdB = [Buf("rd%d" % i) for i in range(2)]
            ost = [P.alloc([512], BF16) for _ in range(2)]
            ostB = [P.dbuf("ost%d" % i) for i in range(2)]
            BBh = P.alloc([8, 512], BF16)
            bbhB = Buf("BBh")
            BBc = P.alloc([4, 12, 512], BF16)
            bbB = Buf("BBc")
            rstg = P.alloc([5, 128], F32)
            rsB = P.dbuf("rstg")
            for h in range(4):
                P.dma("sp", rstg, w["relT"][l, h], [], [rsB], sem=rsB)
                P.memset("pool", rstg[64:128, 0, 0:64], NEG, [rsB])
                P.memset("pool", rstg[0:64, 4, 64:128], NEG, [rsB])
                P.memset("pool", BBh, NEG, [bbhB])
                for j in range(8):
                    for qt in range(4):
                        dl = qt + 4 - j
                        if 0 <= dl <= 4:
                            P.cp("pool" if (j + qt) % 2 else "dve", BBh[:, j, qt * 128:(qt + 1) * 128], rstg[:, dl, :], [rsB], [bbhB])
                for m in range(12):
                    dst = BBc[:, h, m, :]
                    if m < 4:
                        P.ts("dve", dst, BBh[:, m, :], rk[:, 1:2], rk[:, 3:4], ALU.mult, ALU.add, [bbhB, cB], [bbB])
                    elif m < 8:
                        P.ts("dve", dst, BBh[:, m, :], rk[:, 1:2], None, ALU.mult, None, [bbhB, cB], [bbB])
                        P.op("dve", (lambda d, a: lambda e: e.scalar_tensor_tensor(out=d, in0=a, scalar=rk[:, 0:1], in1=d,
                                                                                     op0=ALU.mult, op1=ALU.add))(dst, BBh[:, m - 4, :]),
                             [bbhB, cB, bbB], [bbB])
                    else:
                        P.ts("dve", dst, BBh[:, m - 4, :], rk[:, 0:1], rk[:, 2:3], ALU.mult, ALU.add, [bbhB, cB], [bbB])
            Tld = P.alloc([2, NT * 4], F32)
            TldB = P.dbuf("Tld")
            for rho in range(2):
                P.dma("sp", Tld[:, rho, :], tgat[l][rho * 128, 0:NT * 4].partition_broadcast(128), [gatB], [TldB], sem=TldB)
            Trow = P.alloc([NTG, 4], F32)
            scA = P.alloc([NTG, 4], F32)
            scB = P.alloc([NTG, 4], F32)
            offb = P.alloc([4, NG, NTG], F32)
            colt = P.alloc([1], F32)
            offB = Buf("offb")
            T5 = Trow.rearrange("p (j r t) h -> p j r (t h)", r=2, t=4)
            for rho in range(2):
                P.cp("dve", T5[:, :, rho, :], Tld[:, rho, :].rearrange("p (j x) -> p j x", x=16), [TldB], [offB])
            P.cp("dve", scA, Trow, [offB], [offB])
            cur, nxt = scA, scB
            d = 1
            while d < NTG:
                P.cp("dve", nxt[:, 0:d, :], cur[:, 0:d, :], [offB], [offB])
                P.tt("dve", nxt[:, d:NTG, :], cur[:, d:NTG, :], cur[:, 0:NTG - d, :], ALU.add, [offB], [offB])
                cur, nxt = nxt, cur
                d *= 2
            P.tt("dve", nxt, cur, Trow, ALU.subtract, [offB], [offB])
            off = nxt
            for h in range(4):
                for j in range(NG):
                    P.ts("dve", colt, off[:, 8 * j, h:h + 1], rk[:, 1:2], None, ALU.mult, None, [cB, offB], [offB])
                    P.op("dve", (lambda a: lambda e: e.scalar_tensor_tensor(out=colt, in0=a, scalar=rk[:, 0:1], in1=colt,
                                                                              op0=ALU.mult, op1=ALU.add))(off[:, 8 * j + 4, h:h + 1]),
                         [cB, offB], [offB])
                    P.ts("dve", offb[:, h, j, :], off[:, :, h], colt[:, 0:1], None, ALU.subtract, None, [offB], [offB])
            if STOP == 'a0':
                break
            cnt = {"pt": 0, "o": 0, "hd": 0}

            def load_head(krow0, rows, qT_src, vc0, vw):
                KT, QT, V, kvB = slots[cnt["hd"] % 2]
                cnt["hd"] += 1
                KT4 = KT.rearrange("p (j r c) -> p j r c", r=2, c=512)
                V5 = V.rearrange("p (j r t) c -> p j r t c", r=2, t=4)
                kc = [c for c, (st_, n) in enumerate(KCH) if st_ <= krow0 < st_ + n][0]
                kst, kn = KCH[kc]
                for rho in range(2):
                    r0 = rho * kn + (krow0 - kst)
                    P.dma("sp", KT4[0:rows, :, rho, :], kgat[l][kc][r0:r0 + rows, :].rearrange("k (j c) -> k j c", c=512),
                          [gatB], [kvB], sem=kvB)
                P.dma("sp", QT[0:rows, :], qT_src, [], [kvB], sem=kvB)
                for rho in range(2):
                    for j in range(NG):
                        vcn, voff = divmod(j * 512, VR)
                        vr = vgat[l][vcn][rho * VR + voff:rho * VR + voff + 512, :].rearrange("(t p) c -> p t c", p=128)
                        P.dma("sp", V5[:, j, rho, :, 0:vw], vr[:, :, vc0:vc0 + vw], [gatB], [kvB], sem=kvB)
                return KT, QT, V, kvB

            def run_head(hd, rows, kind, scale, yrow0, bias_tiles, biasB, hh):
                KT, QT, V, kvB = hd
                for j in range(NG):
                    if kind == "chk":
                        kts = [(8 * j - 4 + m, m) for m in range(12) if 8 * j - 4 + m >= 0]
                    else:
                        kts = [(kt, (kt - 8 * j) if kt >= 8 * j else None) for kt in range(8 * j + 8)]
                    o2 = cnt["o"] % 2
                    cnt["o"] += 1
                    ob, obB = P.bank_at(2 * o2)
                    db, dbB = P.bank_at(2 * o2 + 1)
                    q = QT[0:rows, j * 512:(j + 1) * 512]
                    n = len(kts)

                    def pv(i, ptb, ptB):
                        kt = kts[i][0]
                        P.mm(ob[:, :], V[:, kt, :], ptb, i == 0, i == n - 1, [kvB, ptB], [obB])
                        if kind == "mla":
                            P.mm(db[:, :], ones_bf, ptb, i == 0, i == n - 1, [cB, ptB], [dbB])

                    prev = None
                    for i, (kt, m) in enumerate(kts):
                        sb_, sbB = P.bank()
                        P.mm(sb_[:, :], KT[0:rows, kt * 128:(kt + 1) * 128], q, True, m is None, [kvB], [sbB])
                        if m is not None:
                            P.mm(sb_[:, :], ident, bias_tiles[m], False, True, [cB, biasB], [sbB])
                        sl = cnt["pt"] % 3
                        cnt["pt"] += 1
                        if kind == "fox":
                            P.act(PT[sl], sb_[:, :], AF.Exp, [sbB, offB], [PTB[sl]], scale=scale, bias=offb[:, hh, j, kt:kt + 1])
                        else:
                            P.act(PT[sl], sb_[:, :], AF.Exp, [sbB], [PTB[sl]], scale=scale)
                        if prev is not None:
                            pv(*prev)
                        prev = (i, PT[sl], PTB[sl])
                    pv(*prev)
                    gsl = slice(j * 512, (j + 1) * 512)
                    if kind == "mla":
                        P.op("dve", lambda e, a=rden[o2], b=db: e.reciprocal(out=a, in_=b[:, :]), [dbB], [rdB[o2]])
                        P.tt("dve", ost[o2], ob[:, :], rden[o2], ALU.mult, [obB, rdB[o2]], [ostB[o2]])
                        P.dma("sp", ymixT[yrow0:yrow0 + 128, gsl], ost[o2], [ostB[o2]], [], sem=ostB[o2])
                    else:
                        P.op("dve", lambda e, a=rden[o2], b=ob: e.reciprocal(out=a[0:64, :], in_=b[64:128, :]), [obB], [rdB[o2]])
                        P.tt("dve", ost[o2][0:64, :], ob[0:64, :], rden[o2][0:64, :], ALU.mult, [obB, rdB[o2]], [ostB[o2]])
                        P.dma("sp", ymixT[yrow0:yrow0 + 64, gsl], ost[o2][0:64, :], [ostB[o2]], [], sem=ostB[o2])

            mla_scale = float(96 ** -0.5)
            specs = []
            for h in range(4):
                specs.append(((h * 96, 96, qTm[h], h * 128, 128),
                              (96, "mla", mla_scale, h * 128, [mbm8[:, m, :] for m in range(8)], cB, h)))
            for h in range(4):
                specs.append(((384 + h * 70, 70, qTf[h], 512 + h * 64, 64),
                              (70, "fox", 1.0, 512 + h * 64, [mbf8[:, m, :] for m in range(8)], cB, h)))
            for h in range(4):
                specs.append(((664 + h * 64, 64, qTc[h * 64:(h + 1) * 64, :], 768 + h * 64, 64),
                              (64, "chk", 1.0, 768 + h * 64, [BBc[:, h, m, :] for m in range(12)], bbB, h)))

            def prefetch(i):
                if i in (4, 5):
                    P.memset("pool", slots[i % 2][2][:, :, 64:128], 1.0, [slots[i % 2][3]])
                return load_head(*specs[i][0])

            hds = {0: prefetch(0)}
            for i in range(12):
                if i + 1 < 12:
                    hds[i + 1] = prefetch(i + 1)
                run_head(hds.pop(i), *specs[i][1])
            P.barrier()

            P.rot = list(range(8))
            stg = [P.alloc([2048], F32) for _ in range(2)]
            stgB = [P.dbuf("stg%d" % i) for i in range(2)]
            gB = P.dbuf("gains")
            g_o = P.alloc([8], F32)
            g_c = P.alloc([8], F32)
            g_m = P.alloc([8], F32)
            P.dma("sp", g_o, w["out_norm"][l].rearrange("(c p) -> p c", p=128), [], [gB], sem=gB, slow=True)
            P.dma("sp", g_c, w["norm_cross"][l].rearrange("(c p) -> p c", p=128), [], [gB], sem=gB, slow=True)
            P.dma("sp", g_m, w["norm_mem"][l].rearrange("(c p) -> p c", p=128), [], [gB], sem=gB, slow=True)
            Wo = P.alloc([8, 1024], BF16)
            Wcq = P.alloc([8, 512], BF16)
            Wco = P.alloc([4, 1024], BF16)
            Wckv = P.alloc([8, 1024], BF16)
            wB = Buf("wO1")
            load_weight(Wo, w["w_o"][l], D, [(0, 1024, 0)], gain=g_o, stg=stg, stgB=stgB, wB=wB)
            load_weight(Wcq, w["w_cq"][l], D, [(0, 512, 0)], gain=g_c, stg=stg, stgB=stgB, wB=wB)
            load_weight(Wco, w["w_co"][l], 512, [(0, 1024, 0)], stg=stg, stgB=stgB, wB=wB)
            load_weight(Wckv, w["w_ckv"][l], D, [(0, 1024, 0)], gain=g_m, stg=stg, stgB=stgB, wB=wB)
            junk = P.alloc([D], BF16)
            junkB = Buf("junk")
            memT = P.alloc([8, 256], BF16)
            KcT = P.alloc([4, 256], BF16)
            Vc = P.alloc([2, 512], BF16)
            mB = Buf("mem")
            mx = P.alloc([D], F32)
            mxB = P.dbuf("mx")
            mst = P.alloc([2], F32)
            mhb = P.alloc([D], BF16)
            for mt in range(2):
                P.dma("sp", mx, mem_in[mt * 128:(mt + 1) * 128, :], [], [mxB], sem=mxB)
                P.act(junk, mx, AF.Square, [mxB], [junkB, mB], accum=mst[:, 0:1])
                rms_rstd(mst[:, 0:1], 1, D, mB)
                P.ts("dve", mhb, mx, mst[:, 0:1], None, ALU.mult, None, [mxB, mB], [mB])
                bt, bb = P.bank()
                bv = bf16_view(bt)
                for c in range(8):
                    P.tr(bv[:, c * 128:(c + 1) * 128], mhb[:, c * 128:(c + 1) * 128], ident, [mB], [bb])
                P.cp("dve", memT[:, :, mt * 128:(mt + 1) * 128], bv.rearrange("p (c t) -> p c t", c=8), [bb], [mB])
            for h in range(4):
                bt, bb = P.bank()
                for c in range(8):
                    P.mm(bt[:, 0:256], Wckv[:, c, h * 128:(h + 1) * 128], memT[:, c, :], c == 0, c == 7, [wB, mB], [bb])
                P.cp("act", KcT[:, h, :], bt[:, 0:256], [bb], [mB])
            for mt in range(2):
                bt, bb = P.bank()
                for c in range(8):
                    P.mm(bt[:, :], memT[:, c, mt * 128:(mt + 1) * 128], Wckv[:, c, 512:1024], c == 0, c == 7, [wB, mB], [bb])
                P.cp("act", Vc[:, mt, :], bt[:, :], [bb], [mB])

            YT = [P.alloc([8, 512], BF16) for _ in range(2)]
            YTB = [P.dbuf("yt%d" % i) for i in range(2)]
            xg = [P.alloc([4, D], F32) for _ in range(2)]
            xgB = [P.dbuf("xg%d" % i) for i in range(2)]
            sq = P.alloc([8, 512], BF16)
            sqB = Buf("sq")
            rr = P.alloc([3, 512], F32)
            rrB = Buf("rr")
            YN = P.alloc([8, 512], BF16)
            YNB = Buf("yn")
            st1 = P.alloc([4], F32)
            st1B = Buf("st1")
            hb1 = P.alloc([D], BF16)
            hb1B = Buf("hb1")
            hcT = P.alloc([8, 512], BF16)
            hcTB = Buf("hcT")
            qcT = P.alloc([4, 512], BF16)
            qcTB = Buf("qcT")
            PT = [P.alloc([512], BF16) for _ in range(3)]
            PTB = [Buf("pt%d" % i) for i in range(3)]
            ocT = P.alloc([4, 512], BF16)
            ocTB = Buf("ocT")
            rdn = P.alloc([512], F32)
            rdnB = Buf("rdn")
            pti = 0
            c_scale = float(128 ** -0.5)
            x_o1 = x_in if l == 0 else xres
            for G in range(NG):
                g2 = G % 2
                gsl = slice(G * 512, (G + 1) * 512)
                P.dma("sp", YT[g2], ymixT[:, gsl].rearrange("(c p) s -> p c s", p=128), [], [YTB[g2]], sem=YTB[g2])
                P.dma("sp", xg[g2], x_o1[gsl, :].rearrange("(t p) d -> p t d", p=128), [], [xgB[g2]], sem=xgB[g2])
                P.tt("pool", sq, YT[g2], YT[g2], ALU.mult, [YTB[g2]], [sqB])
                for gi, (c0, c1, wdt) in enumerate(((0, 4, 512), (4, 6, 256), (6, 8, 256))):
                    bt, bb = P.bank()
                    for c in range(c0, c1):
                        P.mm(bt[:, :], ones_bf, sq[:, c, :], c == c0, c == c1 - 1, [cB, sqB], [bb])
                    P.act(rr[:, gi, :], bt[:, :], AF.Ln, [bb], [rrB], bias=eps_t[:, 0:1], scale=1.0 / wdt)
                    P.act(rr[:, gi, :], rr[:, gi, :], AF.Exp, [rrB], [rrB], scale=-0.5)
                    for c in range(c0, c1):
                        P.tt("dve", YN[:, c, :], YT[g2][:, c, :], rr[:, gi, :], ALU.mult, [YTB[g2], rrB], [YNB])
                for t in range(4):
                    ts_ = slice(t * 128, (t + 1) * 128)
                    for half in range(2):
                        hs = slice(half * 512, (half + 1) * 512)
                        bt, bb = P.bank()
                        for c in range(8):
                            P.mm(bt[:, :], YN[:, c, ts_], Wo[:, c, hs], c == 0, c == 7, [YNB, wB], [bb])
                        P.tt("dve", xg[g2][:, t, hs], bt[:, :], xg[g2][:, t, hs], ALU.add, [bb, xgB[g2]], [xgB[g2]])
                    P.act(junk, xg[g2][:, t, :], AF.Square, [xgB[g2]], [junkB, st1B], accum=st1[:, t:t + 1])
                    rms_rstd(st1[:, t:t + 1], 1, D, st1B)
                    P.ts("dve", hb1, xg[g2][:, t, :], st1[:, t:t + 1], None, ALU.mult, None, [xgB[g2], st1B], [hb1B])
                    bt, bb = P.bank()
                    bv = bf16_view(bt)
                    for c in range(8):
                        P.tr(bv[:, c * 128:(c + 1) * 128], hb1[:, c * 128:(c + 1) * 128], ident, [hb1B], [bb])
                    P.cp("act", hcT[:, :, ts_], bv.rearrange("p (c t) -> p c t", c=8), [bb], [hcTB])
                for h in range(4):
                    bt, bb = P.bank()
                    for c in range(8):
                        P.mm(bt[:, :], Wcq[:, c, h * 128:(h + 1) * 128], hcT[:, c, :], c == 0, c == 7, [wB, hcTB], [bb])
                    P.cp("act", qcT[:, h, :], bt[:, :], [bb], [qcTB])
                for h in range(4):
                    ob, obB = P.bank()
                    db, dbB = P.bank()
                    pts = []
                    for mt in range(2):
                        sb_, sbB = P.bank()
                        P.mm(sb_[:, :], KcT[:, h, mt * 128:(mt + 1) * 128], qcT[:, h, :], True, True, [mB, qcTB], [sbB])
                        sl = pti % 3
                        pti += 1
                        P.act(PT[sl], sb_[:, :], AF.Exp, [sbB], [PTB[sl]], scale=c_scale)
                        pts.append(sl)
                    for mt in range(2):
                        sl = pts[mt]
                        P.mm(ob[:, :], Vc[:, mt, h * 128:(h + 1) * 128], PT[sl], mt == 0, mt == 1, [mB, PTB[sl]], [obB])
                        P.mm(db[:, :], ones_bf, PT[sl], mt == 0, mt == 1, [cB, PTB[sl]], [dbB])
                    P.op("dve", lambda e, a=rdn, b=db: e.reciprocal(out=a, in_=b[:, :]), [dbB], [rdnB])
                    P.tt("dve", ocT[:, h, :], ob[:, :], rdn, ALU.mult, [obB, rdnB], [ocTB])
                for t in range(4):
                    ts_ = slice(t * 128, (t + 1) * 128)
                    for half in range(2):
                        hs = slice(half * 512, (half + 1) * 512)
                        bt, bb = P.bank()
                        for h in range(4):
                            P.mm(bt[:, :], ocT[:, h, ts_], Wco[:, h, hs], h == 0, h == 3, [ocTB, wB], [bb])
                        P.tt("dve", xg[g2][:, t, hs], bt[:, :], xg[g2][:, t, hs], ALU.add, [bb, xgB[g2]], [xgB[g2]])
                P.dma("sp", xres[gsl, :].rearrange("(t p) d -> p t d", p=128), xg[g2], [xgB[g2]], [], sem=xgB[g2])
            P.barrier()

            stg = [P.alloc([1024], F32) for _ in range(2)]
            stgB = [P.dbuf("stg%d" % i) for i in range(2)]
            gB = P.dbuf("gains")
            g_f = P.alloc([8], F32)
            P.dma("sp", g_f, w["norm_ffn"][l].rearrange("(c p) -> p c", p=128), [], [gB], sem=gB, slow=True)
            last = (l == n_layers - 1)
            if last:
                gfin = P.alloc([D], F32)
                P.dma("sp", gfin, w["final_norm"].partition_broadcast(128), [], [gB], sem=gB)
            Wgu = P.alloc([8, 2 * FFN], BF16)
            Wd = P.alloc([22, 1024], BF16)
            wB = Buf("wO2")
            load_weight(Wgu, w["w_gu"][l], D, [(i * 1024, min(1024, 2 * FFN - i * 1024), i * 1024) for i in range(6)], gain=g_f, stg=stg, stgB=stgB, wB=wB)
            load_weight(Wd, w["w_down"][l], FFN, [(0, 1024, 0)], stg=stg, stgB=stgB, wB=wB)
            junk = P.alloc([D], BF16)
            junkB = Buf("junk")
            xg = P.alloc([2, D], F32)
            xgB = P.dbuf("xg")
            st2 = P.alloc([8], F32)
            st2B = Buf("st2")
            hb2 = P.alloc([D], BF16)
            hb2B = Buf("hb2")
            hfT = P.alloc([8, 256], BF16)
            hfTB = Buf("hfT")
            sg = [P.alloc([256], BF16) for _ in range(2)]
            sgB = [Buf("sg%d" % i) for i in range(2)]
            aT = P.alloc([22, 256], BF16)
            aTB = Buf("aT")
            for G in range(SL // 256):
                gsl = slice(G * 256, (G + 1) * 256)
                P.dma("sp", xg, xres[gsl, :].rearrange("(t p) d -> p t d", p=128), [], [xgB], sem=xgB)
                for t in range(2):
                    ts_ = slice(t * 128, (t + 1) * 128)
                    P.act(junk, xg[:, t, :], AF.Square, [xgB], [junkB, st2B], accum=st2[:, t:t + 1])
                    rms_rstd(st2[:, t:t + 1], 1, D, st2B)
                    P.ts("dve", hb2, xg[:, t, :], st2[:, t:t + 1], None, ALU.mult, None, [xgB, st2B], [hb2B])
                    bt, bb = P.bank()
                    bv = bf16_view(bt)
                    for c in range(8):
                        P.tr(bv[:, c * 128:(c + 1) * 128], hb2[:, c * 128:(c + 1) * 128], ident, [hb2B], [bb])
                    P.cp("dve", hfT[:, :, ts_], bv.rearrange("p (c t) -> p c t", c=8), [bb], [hfTB])
                for c2 in range(22):
                    bg, bgB = P.bank()
                    bu, buB = P.bank()
                    for c in range(8):
                        P.mm(bg[:, 0:256], Wgu[:, c, c2 * 128:(c2 + 1) * 128], hfT[:, c, :], c == 0, c == 7, [wB, hfTB], [bgB])
                    for c in range(8):
                        P.mm(bu[:, 0:256], Wgu[:, c, FFN + c2 * 128:FFN + (c2 + 1) * 128], hfT[:, c, :], c == 0, c == 7, [wB, hfTB], [buB])
                    s2_ = c2 % 2
                    P.act(sg[s2_], bg[:, 0:256], AF.Silu, [bgB], [sgB[s2_]])
                    P.tt("dve", aT[:, c2, :], bu[:, 0:256], sg[s2_], ALU.mult, [buB, sgB[s2_]], [aTB])
                for t in range(2):
                    ts_ = slice(t * 128, (t + 1) * 128)
                    for half in range(2):
                        hs = slice(half * 512, (half + 1) * 512)
                        bt, bb = P.bank()
                        for c2 in range(22):
                            P.mm(bt[:, :], aT[:, c2, ts_], Wd[:, c2, hs], c2 == 0, c2 == 21, [aTB, wB], [bb])
                        P.tt("dve", xg[:, t, hs], bt[:, :], xg[:, t, hs], ALU.add, [bb, xgB], [xgB])
                    rows = slice(G * 256 + t * 128, G * 256 + (t + 1) * 128)
                    if last:
                        P.act(junk, xg[:, t, :], AF.Square, [xgB], [junkB, st2B], accum=st2[:, 4 + t:5 + t])
                        rms_rstd(st2[:, 4 + t:5 + t], 1, D, st2B)
                        P.op("dve", lambda e, b=xg[:, t, :], c=st2[:, 4 + t:5 + t]: e.scalar_tensor_tensor(
                            out=b, in0=b, scalar=c, in1=gfin, op0=ALU.mult, op1=ALU.mult), [xgB, st2B, gB], [xgB])
                        P.dma("sp", out_d[rows, :], xg[:, t, :], [xgB], [], sem=xgB)
                if not last:
                    P.dma("sp", xres[gsl, :].rearrange("(t p) d -> p t d", p=128), xg, [xgB], [], sem=xgB)
            P.barrier()

        P.barrier()
        print('NREC', Prog.nrec)
        with nc.Block() as block:
            P.emit(block)
    return nc


def make_relT(rel_bias):
    ki = np.arange(128)[:, None, None]
    dl = np.arange(5)[None, :, None]
    qi = np.arange(128)[None, None, :]
    idx = np.clip(dl * 128 + qi - ki, -63, 128) + 63
    return np.ascontiguousarray(rel_bias[:, :, idx])


def kernel(**inputs):
    x = np.asarray(inputs["x"], dtype=np.float32)
    B, S, Dm = x.shape
    NGg = S // 512
    nc = build_program(S)
    relT = make_relT(np.asarray(inputs["rel_bias"], dtype=np.float32))
    in_maps = []
    for c in range(2 * B):
        b, r = divmod(c, 2)
        xc = np.ascontiguousarray(x[b].reshape(NGg, 512, Dm)[r::2].reshape(S // 2, Dm))
        m = {"x": xc, "mem": np.ascontiguousarray(inputs["mem"][b], dtype=np.float32), "relT": relT,
             "rank": np.ascontiguousarray(np.tile(np.array([[r, 1 - r]], dtype=np.float32), (128, 1)))}
        for k, v in inputs.items():
            if k in ("x", "mem", "rel_bias"):
                continue
            m[k] = np.ascontiguousarray(v, dtype=np.float32)
        in_maps.append(m)
    res = run_bass_kernel_spmd(nc, in_maps, core_ids=list(range(2 * B)))
    out = np.empty((B, S, Dm), dtype=np.float32)
    for c in range(2 * B):
        b, r = divmod(c, 2)
        out[b].reshape(NGg, 512, Dm)[r::2] = np.asarray(res.results[c]["out"], dtype=np.float32).reshape(NGg // 2, 512, Dm)
    return out
```

```python
import os
import numpy as np
from contextlib import ExitStack
import concourse.bass as bass
import concourse.mybir as mybir
from concourse.bass_utils import run_bass_kernel_spmd

F32 = mybir.dt.float32
BF16 = mybir.dt.bfloat16
I32 = mybir.dt.int32
AF = mybir.ActivationFunctionType
ALU = mybir.AluOpType

D = 1024
NL = 2
EPS = 1e-6
FFN = 2816
NEG = -30000.0
COMPUTE = ("pe", "act", "dve", "pool")
ENGS = ("pe", "act", "dve", "pool", "sp")


class Buf:
    __slots__ = ("name", "w", "r", "sem")

    def __init__(self, name=""):
        self.name = name
        self.w = None
        self.r = {}
        self.sem = None


class Prog:
    def __init__(self, nc, es, arena_words):
        self.nc = nc
        self.es = es
        self.ops = {e: [] for e in ENGS}
        self.esem = {e: es.enter_context(nc.semaphore("s_" + e)) for e in COMPUTE}
        self.dsems = []
        self.free_ds = []
        self.dsems.append([es.enter_context(nc.semaphore("cc")), 0, False])
        self.cc_index = 0
        self.arena = es.enter_context(nc.sbuf_tensor("arena", [128, arena_words], F32))
        self.arena_words = arena_words
        self.persist_top = 0
        self.top = 0
        self.banks = []
        for i in range(8):
            t = es.enter_context(nc.psum_tensor("bank%d" % i, [128, 512], F32))
            self.banks.append((t, Buf("bank%d" % i)))
        self.bank_i = 0

    def alloc(self, shape, dtype, parts=128):
        n = 1
        for s in shape:
            n *= s
        nbytes = n * (2 if dtype == BF16 else 4)
        words = (nbytes + 3) // 4
        words = (words + 7) // 8 * 8
        off = self.top
        self.top += words
        assert self.top <= self.arena_words, ("SBUF arena overflow", self.top * 4)
        v = self.arena[0:parts, off:off + (nbytes + 3) // 4]
        if dtype != F32:
            v = v.bitcast(dtype)
        if len(shape) == 2:
            v = v.rearrange("p (a b) -> p a b", a=shape[0])
        elif len(shape) == 3:
            v = v.rearrange("p (a b c) -> p a b c", a=shape[0], b=shape[1])
        return v

    rot = list(range(8))

    def bank(self):
        i = self.rot[self.bank_i % len(self.rot)]
        self.bank_i += 1
        return self.banks[i]

    def bank_at(self, i):
        return self.banks[i]

    def new_sem(self, buf):
        if self.free_ds:
            buf.sem = self.free_ds.pop()
        else:
            h = self.es.enter_context(self.nc.semaphore("d%d" % len(self.dsems)))
            self.dsems.append([h, 0, False])
            buf.sem = len(self.dsems) - 1
        return buf

    def dbuf(self, name=""):
        return self.new_sem(Buf(name))

    def _deps(self, eng, reads, writes):
        deps = {}

        def add(tok):
            if tok is None:
                return
            k, v = tok
            if k == "pe" and eng == "pe":
                return
            if not isinstance(k, str):
                ds = self.dsems[k[1]]
                v = ds[1]
                ds[2] = True
            if deps.get(k, -1) < v:
                deps[k] = v

        for b in reads:
            add(b.w)
        for b in writes:
            add(b.w)
            for t in b.r.values():
                add(t)
        return deps

    nrec = 0
    mute = False
    maxops = int(os.environ.get('K_MAXOPS', '0'))

    def op(self, eng, fn, reads=(), writes=()):
        Prog.nrec += 1
        if Prog.mute or (Prog.maxops and Prog.nrec > Prog.maxops):
            return None
        deps = self._deps(eng, reads, writes)
        tok = (eng, len(self.ops[eng]))
        self.ops[eng].append([fn, deps, False, None, 0])
        for b in reads:
            b.r[eng] = tok
        for b in writes:
            b.w = tok
            b.r = {}
        return tok

    def dma(self, q, out, in_, reads=(), writes=(), sem=None, slow=False, fn=None, inc=16):
        Prog.nrec += 1
        if Prog.mute or (Prog.maxops and Prog.nrec > Prog.maxops):
            return None
        deps = self._deps(q, reads, writes)
        s = sem.sem
        ds = self.dsems[s]
        key = ("d", s)
        if ds[2]:
            if deps.get(key, -1) < ds[1]:
                deps[key] = ds[1]
            ds[2] = False
        ds[1] += inc
        tok = (key, ds[1])
        if fn is not None:
            pass
        elif slow:
            fn = lambda e: e.dma_start(out=out, in_=in_, allow_slow_non_contiguous=True)
        else:
            fn = lambda e: e.dma_start(out=out, in_=in_)
        self.ops[q].append([fn, deps, False, (s, inc), 0])
        for b in reads:
            b.r[key] = tok
        for b in writes:
            b.w = tok
            b.r = {}
        return tok

    def barrier(self):
        deps = {}
        for e in COMPUTE:
            i = len(self.ops[e]) - 1
            while i >= 0 and (self.ops[e][i][0] is None or self.ops[e][i][3] is not None):
                i -= 1
            if i >= 0:
                deps[e] = i
        for i, ds in enumerate(self.dsems):
            if ds[1] > 0:
                deps[("d", i)] = ds[1]
                ds[2] = False
        for e in ENGS:
            d = dict(deps)
            if e == "pe":
                d.pop("pe", None)
            self.ops[e].append([None, d, False, None, 0])
        self.top = self.persist_top
        self.free_ds = [i for i in range(len(self.dsems)) if i != self.cc_index]
        for t, b in self.banks:
            b.w = None
            b.r = {}

    def emit(self, block):
        for e in ENGS:
            w = {}
            for o in self.ops[e]:
                nd = []
                for k, v in o[1].items():
                    if w.get(k, -1) >= v:
                        continue
                    w[k] = v
                    nd.append((k, v))
                    if isinstance(k, str):
                        self.ops[k][v][2] = True
                o[1] = nd
        for e in COMPUTE:
            c = 0
            for o in self.ops[e]:
                if o[2]:
                    c += 1
                o[4] = c
        prog = self
        def body(e):
            def f(h):
                for o in prog.ops[e]:
                    for k, v in o[1]:
                        if isinstance(k, str):
                            h.wait_ge(prog.esem[k], prog.ops[k][v][4])
                        else:
                            h.wait_ge(prog.dsems[k[1]][0], v)
                    if o[0] is None:
                        continue
                    ins = o[0](h)
                    if o[3] is not None:
                        ins.then_inc(prog.dsems[o[3][0]][0], o[3][1])
                    elif o[2]:
                        ins.then_inc(prog.esem[e], 1)
            return f

        block.sync(body("sp"))
        block.tensor(body("pe"))
        block.scalar(body("act"))
        block.vector(body("dve"))
        block.gpsimd(body("pool"))

    def mm(self, out, lhsT, rhs, start, stop, reads, writes):
        return self.op("pe", lambda e: e.matmul(out, lhsT=lhsT, rhs=rhs, start=start, stop=stop), reads, writes)

    def tr(self, out, in_, ident, reads, writes):
        return self.op("pe", lambda e: e.transpose(out=out, in_=in_, identity=ident), reads, writes)

    def act(self, out, in_, func, reads, writes, bias=None, scale=1.0, accum=None):
        kw = {}
        if bias is not None:
            kw["bias"] = bias
        if accum is not None:
            kw["accum_out"] = accum
        return self.op("act", lambda e: e.activation(out=out, in_=in_, func=func, scale=scale, **kw), reads, writes)

    def ts(self, eng, out, in0, s1, s2, op0, op1, reads, writes):
        if op1 is None:
            return self.op(eng, lambda e: e.tensor_scalar(out=out, in0=in0, scalar1=s1, scalar2=None, op0=op0), reads, writes)
        return self.op(eng, lambda e: e.tensor_scalar(out=out, in0=in0, scalar1=s1, scalar2=s2, op0=op0, op1=op1), reads, writes)

    def tt(self, eng, out, in0, in1, op, reads, writes):
        return self.op(eng, lambda e: e.tensor_tensor(out=out, in0=in0, in1=in1, op=op), reads, writes)

    def cp(self, eng, out, in_, reads, writes):
        if eng == "act":
            return self.op("act", lambda e: e.copy(out=out, in_=in_), reads, writes)
        return self.op(eng, lambda e: e.tensor_copy(out=out, in_=in_), reads, writes)

    def memset(self, eng, ap, val, writes):
        return self.op(eng, lambda e: e.memset(ap, val), (), writes)


def bf16_view(bank_t, parts=128):
    return bank_t[0:parts, :].bitcast(BF16)


def build_program(S, n_layers=NL, debug=False):
    SL = S // 2
    NT = SL // 128
    NG = SL // 512
    NTG = S // 128
    KROWS = 920
    nc = bass.Bass("TRN2", target_bir_lowering=False)

    def din(name, shape, dt=F32):
        return nc.dram_tensor(name, list(shape), dt, kind="ExternalInput").ap()

    x_in = din("x", [SL, D])
    rank_in = din("rank", [128, 2])
    mem_in = din("mem", [256, D])
    w = {}
    for name, shape in [("norm_mix", [NL, D]), ("w_in", [NL, D, 1956]), ("q_norm", [NL, 256]),
                        ("w_uq", [NL, 256, 384]), ("kv_norm", [NL, 128]), ("w_ukv", [NL, 128, 768]),
                        ("f_bias", [NL, 4]), ("relT", [NL, 4, 128, 5, 128]), ("out_norm", [NL, D]),
                        ("w_o", [NL, D, D]), ("norm_cross", [NL, D]), ("norm_mem", [NL, D]),
                        ("w_cq", [NL, D, 512]), ("w_ckv", [NL, D, 1024]), ("w_co", [NL, 512, D]),
                        ("norm_ffn", [NL, D]), ("w_gu", [NL, D, 2 * FFN]), ("w_down", [NL, FFN, D]),
                        ("final_norm", [D])]:
        w[name] = din(name, shape)
    out_d = nc.dram_tensor("out", [SL, D], F32, kind="ExternalOutput").ap()
    skind = "ExternalOutput" if debug else "Internal"

    def dscr(name, shape, dt):
        return nc.dram_tensor(name, list(shape), dt, kind=skind).ap()

    xres = dscr("xres", [SL, D], F32)
    qTm = dscr("qTm", [4, 96, SL], BF16)
    qTf = dscr("qTf", [4, 70, SL], BF16)
    qTc = dscr("qTc", [256, SL], BF16)
    kloc = dscr("kloc", [KROWS, SL], BF16)
    vloc = dscr("vloc", [SL, 1024], BF16)
    totloc = dscr("totloc", [128, 256], F32)
    KCH = [(0, 192), (192, 192), (384, 140), (524, 140), (664, 128), (792, 128)]
    NVC = max(1, SL // 1024)
    VR = SL // NVC
    kgat = [[nc.dram_tensor("kgat%d_%d" % (i, c), [2 * n, SL], BF16, kind="Internal").ap() for c, (st_, n) in enumerate(KCH)]
            for i in range(n_layers)]
    vgat = [[nc.dram_tensor("vgat%d_%d" % (i, c), [2 * VR, 1024], BF16, kind="Internal").ap() for c in range(NVC)]
            for i in range(n_layers)]
    tgat = [nc.dram_tensor("tgat%d" % i, [256, 256], F32, kind="Internal").ap() for i in range(n_layers)]
    ymixT = dscr("ymixT", [D, SL], BF16)
    ropeD = dscr("ropeD", [SL, 128], F32)
    RG = [[0, 1], [2, 3], [4, 5], [6, 7]]

    with ExitStack() as es:
        P = Prog(nc, es, arena_words=50 * 1024)
        ident = P.alloc([128], BF16)
        ones_bf = P.alloc([128], BF16)
        identf = P.alloc([128], F32)
        onesf = P.alloc([128], F32)
        triu = P.alloc([128], F32)
        triu_bf = P.alloc([128], BF16)
        eps_t = P.alloc([1], F32)
        one_t = P.alloc([1], F32)
        npi_t = P.alloc([1], F32)
        mbm = P.alloc([4, 512], BF16)
        mbf = P.alloc([4, 512], BF16)
        mbm8 = P.alloc([8, 512], BF16)
        mbf8 = P.alloc([8, 512], BF16)
        rk = P.alloc([8], F32)
        sut = P.alloc([64], F32)
        cB = Buf("consts")
        P.memset("pool", identf, 1.0, [cB])
        P.op("pool", lambda e: e.affine_select(out=identf, in_=identf, pattern=[[-1, 128]], compare_op=ALU.is_equal,
                                                fill=0.0, base=0, channel_multiplier=1), [cB], [cB])
        P.cp("pool", ident, identf, [cB], [cB])
        P.memset("pool", onesf, 1.0, [cB])
        P.memset("pool", ones_bf, 1.0, [cB])
        P.memset("pool", triu, 1.0, [cB])
        P.op("pool", lambda e: e.affine_select(out=triu, in_=triu, pattern=[[1, 128]], compare_op=ALU.is_ge,
                                                fill=0.0, base=0, channel_multiplier=-1), [cB], [cB])
        P.cp("pool", triu_bf, triu, [cB], [cB])
        P.memset("pool", eps_t, EPS, [cB])
        P.memset("pool", one_t, 1.0, [cB])
        P.memset("pool", npi_t, -np.pi, [cB])
        P.memset("pool", mbm, 0.0, [cB])
        P.memset("pool", mbf, 0.0, [cB])
        for j in range(4):
            if j > 0:
                P.memset("pool", mbm[:, j, 0:j * 128], NEG, [cB])
                P.memset("pool", mbf[:, j, 0:j * 128], NEG, [cB])
            P.memset("pool", mbm[64:128, j, j * 128:j * 128 + 64], NEG, [cB])
            blk = mbf[:, j, j * 128:(j + 1) * 128]
            P.op("pool", (lambda blk: lambda e: e.affine_select(out=blk, in_=blk, pattern=[[1, 128]], compare_op=ALU.is_ge,
                                                                 fill=NEG, base=0, channel_multiplier=-1))(blk), [cB], [cB])
        rkB = P.dbuf("rank")
        P.dma("sp", rk[:, 0:2], rank_in, [], [rkB], sem=rkB)
        P.ts("dve", rk[:, 2:3], rk[:, 1:2], NEG, None, ALU.mult, None, [rkB], [cB])
        P.ts("dve", rk[:, 3:4], rk[:, 0:1], NEG, None, ALU.mult, None, [rkB], [cB])
        P.ts("dve", rk[:, 4:5], rk[:, 0:1], 512.0, None, ALU.mult, None, [rkB], [cB])
        for m in range(8):
            for (src, dst) in ((mbm, mbm8), (mbf, mbf8)):
                if m < 4:
                    P.ts("dve", dst[:, m, :], src[:, m, :], rk[:, 1:2], None, ALU.mult, None, [cB], [cB])
                else:
                    P.ts("dve", dst[:, m, :], src[:, m - 4, :], rk[:, 0:1], rk[:, 2:3], ALU.mult, ALU.add, [cB], [cB])
        P.memset("pool", sut[0:64, :], 1.0, [cB])
        P.op("pool", lambda e: e.affine_select(out=sut[0:64, :], in_=sut[0:64, :], pattern=[[1, 64]], compare_op=ALU.is_gt,
                                                fill=0.0, base=0, channel_multiplier=-1), [cB], [cB])
        P.persist_top = P.top
        if os.environ.get('K_STOP', '') != 'c1':
          if True:
            rope = P.alloc([NT, 128], F32)
            posi = P.alloc([NT], I32)
            posf = P.alloc([NT], F32)
            invf = P.alloc([16], F32)
            ang = P.alloc([NT, 16], F32)
            ang2 = P.alloc([NT, 16], F32)
            sn = P.alloc([NT, 16], F32)
            csn = P.alloc([NT, 16], F32)
            P.op("pool", lambda e: e.iota(posi.rearrange("p (j t) -> p j t", t=4), pattern=[[1024, NT // 4], [128, 4]], base=0,
                                           channel_multiplier=1), [], [cB])
            P.cp("pool", posf, posi, [cB], [cB])
            P.op("dve", lambda e: e.scalar_tensor_tensor(out=posf, in0=onesf[:, 0:NT], scalar=rk[:, 4:5], in1=posf,
                                                          op0=ALU.mult, op1=ALU.add), [cB], [cB])
            for i in range(16):
                P.memset("pool", invf[:, i:i + 1], float(np.float32(10000.0) ** np.float32(-(2.0 * i) / 32.0)), [cB])
            for t in range(NT):
                P.ts("dve", ang[:, t, :], invf, posf[:, t:t + 1], None, ALU.mult, None, [cB], [cB])
            kint = P.alloc([NT, 16], I32)
            kf = P.alloc([NT, 16], F32)
            TWO_PI = float(2 * np.pi)

            def sin_of(dst, shift):
                if shift:
                    P.ts("dve", ang2, ang, float(shift), None, ALU.add, None, [cB], [cB])
                    src = ang2
                else:
                    src = ang
                P.ts("dve", kf, src, 1.0 / TWO_PI, None, ALU.mult, None, [cB], [cB])
                P.cp("dve", kint, kf, [cB], [cB])
                P.cp("dve", kf, kint, [cB], [cB])
                P.op("dve", lambda e: e.scalar_tensor_tensor(out=kf, in0=kf, scalar=-TWO_PI, in1=src, op0=ALU.mult, op1=ALU.add), [cB], [cB])
                P.ts("dve", dst, kf, float(np.pi), TWO_PI, ALU.is_gt, ALU.mult, [cB], [cB])
                P.tt("dve", kf, kf, dst, ALU.subtract, [cB], [cB])
                P.ts("dve", dst, kf, -float(np.pi), TWO_PI, ALU.is_lt, ALU.mult, [cB], [cB])
                P.tt("dve", kf, kf, dst, ALU.add, [cB], [cB])
                P.act(dst, kf, AF.Sin, [cB], [cB])

            sin_of(sn, 0.0)
            sin_of(csn, np.pi / 2)
            for hh in range(4):
                P.cp("pool", rope[:, :, hh * 16:(hh + 1) * 16], csn, [cB], [cB])
                P.cp("pool", rope[:, :, 64 + hh * 16:64 + (hh + 1) * 16], sn, [cB], [cB])
            rpB = P.dbuf("ropeout")
            P.dma("sp", ropeD.rearrange("(t p) c -> p t c", p=128), rope, [cB], [], sem=rpB)
        P.barrier()
        STOP = os.environ.get('K_STOP', '')

        def load_weight(dst, src, K, segs, gain=None, stg=None, stgB=None, wB=None, engs=("dve", "act")):
            i = 0
            for c in range(K // 128):
                for (s0, n, d0) in segs:
                    slot = i % len(stg)
                    P.dma("sp", stg[slot][:, 0:n], src[c * 128:(c + 1) * 128, s0:s0 + n], [], [stgB[slot]], sem=stgB[slot])
                    eng = engs[i % len(engs)]
                    o = dst[:, c, d0:d0 + n]
                    src_t = stg[slot][:, 0:n]
                    if gain is not None:
                        if eng == "act":
                            P.act(o, src_t, AF.Copy, [stgB[slot]], [wB], scale=gain[:, c:c + 1])
                        else:
                            P.ts("dve", o, src_t, gain[:, c:c + 1], None, ALU.mult, None, [stgB[slot]], [wB])
                    else:
                        P.cp(eng, o, src_t, [stgB[slot]], [wB])
                    i += 1

        def load_gain(dst, src_vec, K, gB):
            if K >= 128:
                P.dma("sp", dst, src_vec.rearrange("(c p) -> p c", p=128), [], [gB], sem=gB)
            return dst

        def rms_rstd(ss, n, dim, rB):
            P.act(ss, ss, AF.Ln, [rB], [rB], bias=eps_t[:, 0:1], scale=1.0 / dim)
            P.act(ss, ss, AF.Exp, [rB], [rB], scale=-0.5)

        for l in range(n_layers if STOP not in ('consts', 'c1') else 0):
            x_src = x_in if l == 0 else xres
            Prog.mute = os.environ.get('K_MUTE', '') == 'P'
            stg = [P.alloc([2048], F32) for _ in range(2)]
            stgB = [P.dbuf("stg%d" % i) for i in range(2)]
            gB = P.dbuf("gains")
            g_mix = P.alloc([8], F32)
            g_q = P.alloc([2], F32)
            g_kv = P.alloc([1], F32)
            fb = P.alloc([4], F32)
            P.dma("sp", g_mix, w["norm_mix"][l].rearrange("(c p) -> p c", p=128), [], [gB], sem=gB, slow=True)
            P.dma("sp", g_q, w["q_norm"][l].rearrange("(c p) -> p c", p=128), [], [gB], sem=gB, slow=True)
            P.dma("sp", g_kv, w["kv_norm"][l].rearrange("(c p) -> p c", p=128), [], [gB], sem=gB, slow=True)
            P.dma("sp", fb, w["f_bias"][l].partition_broadcast(128), [], [gB], sem=gB)
            Win = P.alloc([8, 1956], BF16)
            Wuq = P.alloc([2, 384], BF16)
            Wukv = P.alloc([1, 768], BF16)
            wB = Buf("wP")
            segs_in = [(0, 416, 0), (1184, 4, 416), (416, 512, 420), (928, 256, 932), (1700, 256, 1188),
                       (1188, 512, 1444)]
            load_weight(Win, w["w_in"][l], D, segs_in, gain=g_mix, stg=stg, stgB=stgB, wB=wB)
            segs_uq = []
            for h in range(4):
                segs_uq += [(h * 96, 64, h * 64), (h * 96 + 64, 16, 256 + h * 16), (h * 96 + 80, 16, 320 + h * 16)]
            load_weight(Wuq, w["w_uq"][l], 256, segs_uq, gain=g_q, stg=stg, stgB=stgB, wB=wB)
            segs_ukv = []
            for h in range(4):
                segs_ukv += [(h * 192, 64, h * 64), (h * 192 + 64, 128, 256 + h * 128)]
            load_weight(Wukv, w["w_ukv"][l], 128, segs_ukv, gain=g_kv, stg=stg, stgB=stgB, wB=wB)

            if STOP == 'weights':
                break
            NX = 3
            xs = [P.alloc([D], F32) for _ in range(NX)]
            xsB = [P.dbuf("xs%d" % i) for i in range(NX)]
            rps = [P.alloc([128], F32) for _ in range(NX)]
            junk = P.alloc([D], BF16)
            junkB = Buf("junk")
            st = [P.alloc([8], F32) for _ in range(2)]
            stB = [Buf("st%d" % i) for i in range(2)]
            hb = [P.alloc([D], BF16) for _ in range(2)]
            hbB = [Buf("hb%d" % i) for i in range(2)]
            hT = [P.alloc([D], BF16) for _ in range(2)]
            hTB = [Buf("hT%d" % i) for i in range(2)]
            cn = [P.alloc([384], BF16) for _ in range(2)]
            cnB = [Buf("cn%d" % i) for i in range(2)]
            cT = [P.alloc([384], BF16) for _ in range(2)]
            cTB = [Buf("cT%d" % i) for i in range(2)]
            tmp = [P.alloc([6, 64], F32) for _ in range(2)]
            tmpB = [Buf("tmp%d" % i) for i in range(2)]
            Qm = [P.alloc([4, 96], BF16) for _ in range(2)]
            Km = [P.alloc([4, 96], BF16) for _ in range(2)]
            Qf = [P.alloc([4, 70], BF16) for _ in range(2)]
            Kf = [P.alloc([4, 70], BF16) for _ in range(2)]
            Qc = [P.alloc([256], BF16) for _ in range(2)]
            Kc = [P.alloc([256], BF16) for _ in range(2)]
            tmB = [Buf("tm%d" % i) for i in range(2)]
            fx = [P.alloc([24], F32) for _ in range(2)]
            fxh = [P.alloc([3, 4], BF16) for _ in range(2)]
            fxs = [P.alloc([3, 4], BF16) for _ in range(2)]
            fxk = [P.alloc([3, 4], BF16) for _ in range(2)]
            ttsb = P.alloc([256], F32)
            ttB = P.dbuf("ttsb")
            P.memset("pool", ttsb, 0.0, [ttB])
            fxB = [Buf("fx%d" % i) for i in range(2)]
            tot = P.alloc([4], F32)
            totB = Buf("tot")
            P.memset("pool", tot, 0.0, [totB])
            for i in range(2):
                P.memset("pool", Qf[i][:, :, 67:70], 1.0, [tmB[i]])
                P.memset("pool", Kf[i][:, :, 64:67], 1.0, [tmB[i]])
            sQm = [P.alloc([4, 512], BF16) for _ in range(2)]
            sKm = [P.alloc([4, 512], BF16) for _ in range(2)]
            sQf = [P.alloc([4, 512], BF16) for _ in range(2)]
            sKf = [P.alloc([4, 512], BF16) for _ in range(2)]
            sQc = [P.alloc([2, 512], BF16) for _ in range(2)]
            sKc = [P.alloc([2, 512], BF16) for _ in range(2)]
            sV = [P.alloc([4, 1024], BF16) for _ in range(2)]
            sB = [P.dbuf("stage%d" % i) for i in range(2)]

            for t in range(NT):
                G, tt_ = divmod(t, 4)
                gs = G % 2
                i2 = t % 2
                xi = t % NX
                P.dma("sp", xs[xi], x_src[t * 128:(t + 1) * 128, :], [], [xsB[xi]], sem=xsB[xi])
                P.dma("sp", rps[xi], ropeD[t * 128:(t + 1) * 128, :], [], [xsB[xi]], sem=xsB[xi])
                P.act(junk, xs[xi], AF.Square, [xsB[xi]], [junkB, stB[i2]], accum=st[i2][:, 0:1])
                rms_rstd(st[i2][:, 0:1], 1, D, stB[i2])
                P.ts("dve", hb[i2], xs[xi], st[i2][:, 0:1], None, ALU.mult, None, [xsB[xi], stB[i2]], [hbB[i2]])
                bt, bb = P.bank()
                bv = bf16_view(bt)
                for c in range(8):
                    P.tr(bv[:, c * 128:(c + 1) * 128], hb[i2][:, c * 128:(c + 1) * 128], ident, [hbB[i2]], [bb])
                P.cp("dve", hT[i2], bv, [bb], [hTB[i2]])
                chunks = [(0, 420), (420, 512), (932, 512), (1444, 512)]
                pb = []
                for (c0, n) in chunks:
                    bt2, bb2 = P.bank()
                    for c in range(8):
                        P.mm(bt2[:, 0:n], hT[i2][:, c * 128:(c + 1) * 128], Win[:, c, c0:c0 + n], c == 0, c == 7,
                             [hTB[i2], wB], [bb2])
                    pb.append((bt2, bb2))
                (b0, b0B), (b1, b1B), (b2, b2B), (b3, b3B) = pb
                s2 = st[i2]
                P.act(junk[:, 0:256], b0[:, 0:256], AF.Square, [b0B], [junkB, stB[i2]], accum=s2[:, 1:2])
                P.act(junk[:, 0:128], b0[:, 256:384], AF.Square, [b0B], [junkB, stB[i2]], accum=s2[:, 2:3])
                P.act(s2[:, 1:2], s2[:, 1:2], AF.Ln, [stB[i2]], [stB[i2]], bias=eps_t[:, 0:1], scale=1.0 / 256)
                P.act(s2[:, 2:3], s2[:, 2:3], AF.Ln, [stB[i2]], [stB[i2]], bias=eps_t[:, 0:1], scale=1.0 / 128)
                P.act(s2[:, 1:3], s2[:, 1:3], AF.Exp, [stB[i2]], [stB[i2]], scale=-0.5)
                P.ts("dve", cn[i2][:, 0:256], b0[:, 0:256], s2[:, 1:2], None, ALU.mult, None, [b0B, stB[i2]], [cnB[i2]])
                P.ts("dve", cn[i2][:, 256:384], b0[:, 256:384], s2[:, 2:3], None, ALU.mult, None, [b0B, stB[i2]], [cnB[i2]])
                bt3, bb3 = P.bank()
                bv3 = bf16_view(bt3)
                for c in range(3):
                    P.tr(bv3[:, c * 128:(c + 1) * 128], cn[i2][:, c * 128:(c + 1) * 128], ident, [cnB[i2]], [bb3])
                P.cp("dve", cT[i2], bv3[:, 0:384], [bb3], [cTB[i2]])
                bq, bqB = P.bank()
                for c in range(2):
                    P.mm(bq[:, 0:384], cT[i2][:, c * 128:(c + 1) * 128], Wuq[:, c, :], c == 0, c == 1, [cTB[i2], wB], [bqB])
                bk, bkB = P.bank()
                P.mm(bk[:, 0:256], cT[i2][:, 256:384], Wukv[:, 0, 0:256], True, True, [cTB[i2], wB], [bkB])
                bvv, bvB = P.bank()
                P.mm(bvv[:, 0:512], cT[i2][:, 256:384], Wukv[:, 0, 256:768], True, True, [cTB[i2], wB], [bvB])
                cos4 = rps[xi][:, 0:64]
                sin4 = rps[xi][:, 64:128]
                tm_ = tmp[i2]
                qv = Qm[i2]
                kv_ = Km[i2]
                P.tt("dve", tm_[:, 0, :], bq[:, 256:320], cos4, ALU.mult, [bqB, xsB[xi]], [tmpB[i2]])
                P.tt("dve", tm_[:, 1, :], bq[:, 320:384], sin4, ALU.mult, [bqB, xsB[xi]], [tmpB[i2]])
                P.tt("dve", tm_[:, 2, :], bq[:, 256:320], sin4, ALU.mult, [bqB, xsB[xi]], [tmpB[i2]])
                P.tt("dve", tm_[:, 3, :], bq[:, 320:384], cos4, ALU.mult, [bqB, xsB[xi]], [tmpB[i2]])
                P.tt("dve", qv[:, :, 64:80], tm_[:, 0, :].rearrange("p (h d) -> p h d", h=4),
                     tm_[:, 1, :].rearrange("p (h d) -> p h d", h=4), ALU.subtract, [tmpB[i2]], [tmB[i2]])
                P.tt("dve", qv[:, :, 80:96], tm_[:, 2, :].rearrange("p (h d) -> p h d", h=4),
                     tm_[:, 3, :].rearrange("p (h d) -> p h d", h=4), ALU.add, [tmpB[i2]], [tmB[i2]])
                P.cp("act", qv[:, :, 0:64], bq[:, 0:256].rearrange("p (h d) -> p h d", h=4), [bqB], [tmB[i2]])
                P.tt("dve", tm_[:, 4, 0:16], b0[:, 384:400], cos4[:, 0:16], ALU.mult, [b0B, xsB[xi]], [tmpB[i2]])
                P.tt("dve", tm_[:, 4, 16:32], b0[:, 400:416], sin4[:, 0:16], ALU.mult, [b0B, xsB[xi]], [tmpB[i2]])
                P.tt("dve", tm_[:, 4, 32:48], b0[:, 384:400], sin4[:, 0:16], ALU.mult, [b0B, xsB[xi]], [tmpB[i2]])
                P.tt("dve", tm_[:, 4, 48:64], b0[:, 400:416], cos4[:, 0:16], ALU.mult, [b0B, xsB[xi]], [tmpB[i2]])
                P.tt("dve", tm_[:, 5, 0:16], tm_[:, 4, 0:16], tm_[:, 4, 16:32], ALU.subtract, [tmpB[i2]], [tmpB[i2]])
                P.tt("dve", tm_[:, 5, 16:32], tm_[:, 4, 32:48], tm_[:, 4, 48:64], ALU.add, [tmpB[i2]], [tmpB[i2]])
                for h in range(4):
                    P.cp("pool", kv_[:, h, 64:96], tm_[:, 5, 0:32], [tmpB[i2]], [tmB[i2]])
                P.cp("act", kv_[:, :, 0:64], bk[:, 0:256].rearrange("p (h d) -> p h d", h=4), [bkB], [tmB[i2]])
                P.cp("act", sV[gs][:, tt_, 0:512], bvv[:, 0:512], [bvB], [sB[gs]])
                f = fx[i2]
                P.tt("dve", f[:, 0:4], b0[:, 416:420], fb, ALU.add, [b0B, gB], [fxB[i2]])
                P.act(f[:, 0:4], f[:, 0:4], AF.Exp, [fxB[i2]], [fxB[i2]], scale=-1.0)
                P.act(f[:, 4:8], f[:, 0:4], AF.Ln, [fxB[i2]], [fxB[i2]], bias=one_t[:, 0:1], scale=1.0)
                spb = fxs[i2]
                P.cp("dve", spb[:, 0, :], f[:, 4:8], [fxB[i2]], [fxB[i2]])
                P.tt("dve", f[:, 12:16], f[:, 4:8], spb[:, 0, :], ALU.subtract, [fxB[i2]], [fxB[i2]])
                P.cp("dve", spb[:, 1, :], f[:, 12:16], [fxB[i2]], [fxB[i2]])
                P.tt("dve", f[:, 16:20], f[:, 12:16], spb[:, 1, :], ALU.subtract, [fxB[i2]], [fxB[i2]])
                P.cp("dve", spb[:, 2, :], f[:, 16:20], [fxB[i2]], [fxB[i2]])
                bc, bcB = P.bank()
                for pc in range(3):
                    P.mm(bc[:, 0:4], triu_bf, spb[:, pc, :], pc == 0, pc == 2, [fxB[i2], cB], [bcB])
                for pc in range(3):
                    P.mm(bc[:, 8:12], ones_bf, spb[:, pc, :], pc == 0, pc == 2, [fxB[i2], cB], [bcB])
                P.cp("dve", f[:, 20:24], bc[:, 0:4], [bcB], [fxB[i2]])
                P.cp("dve", ttsb[:, t * 4:(t + 1) * 4], bc[:, 8:12], [bcB], [ttB])
                if tt_ == 0:
                    P.cp("dve", f[:, 8:12], bc[:, 0:4], [bcB], [fxB[i2]])
                    P.cp("dve", tot, bc[:, 8:12], [bcB], [totB])
                else:
                    P.tt("dve", f[:, 8:12], bc[:, 0:4], tot, ALU.add, [bcB, totB], [fxB[i2]])
                    P.tt("dve", tot, bc[:, 8:12], tot, ALU.add, [bcB, totB], [totB])
                for (srcc, ph) in ((f[:, 8:12], fxh[i2]), (f[:, 20:24], fxk[i2])):
                    P.cp("dve", ph[:, 0, :], srcc, [fxB[i2]], [fxB[i2]])
                    P.tt("dve", f[:, 12:16], srcc, ph[:, 0, :], ALU.subtract, [fxB[i2]], [fxB[i2]])
                    P.cp("dve", ph[:, 1, :], f[:, 12:16], [fxB[i2]], [fxB[i2]])
                    P.tt("dve", f[:, 16:20], f[:, 12:16], ph[:, 1, :], ALU.subtract, [fxB[i2]], [fxB[i2]])
                    P.cp("dve", ph[:, 2, :], f[:, 16:20], [fxB[i2]], [fxB[i2]])
                for pc in range(3):
                    P.cp("pool", Kf[i2][:, :, 67 + pc:68 + pc], fxk[i2][:, pc, :].rearrange("p (h o) -> p h o", o=1),
                         [fxB[i2]], [tmB[i2]])
                    P.ts("dve", Qf[i2][:, :, 64 + pc:65 + pc], fxh[i2][:, pc, :].rearrange("p (h o) -> p h o", o=1),
                         -1.0, None, ALU.mult, None, [fxB[i2]], [tmB[i2]])
                P.ts("dve", Qf[i2][:, :, 0:64], b1[:, 0:256].rearrange("p (h d) -> p h d", h=4), 0.125, None, ALU.mult, None,
                     [b1B], [tmB[i2]])
                P.cp("act", Kf[i2][:, :, 0:64], b1[:, 256:512].rearrange("p (h d) -> p h d", h=4), [b1B], [tmB[i2]])
                P.cp("act", sV[gs][:, tt_, 512:1024], b2[:, 0:512], [b2B], [sB[gs]])
                P.ts("dve", Qc[i2], b3[:, 0:256], 0.125, None, ALU.mult, None, [b3B], [tmB[i2]])
                P.cp("act", Kc[i2], b3[:, 256:512], [b3B], [tmB[i2]])
                cs = slice(tt_ * 128, (tt_ + 1) * 128)
                for (src, dst, rows) in ((Qm[i2], sQm[gs], 96), (Km[i2], sKm[gs], 96), (Qf[i2], sQf[gs], 70), (Kf[i2], sKf[gs], 70)):
                    btx, bbx = P.bank()
                    bvx = bf16_view(btx)
                    for h in range(4):
                        P.tr(bvx[0:rows, h * 128:(h + 1) * 128], src[:, h, :], ident, [tmB[i2]], [bbx])
                    P.cp("dve" if rows == 96 else "act", dst[0:rows, :, cs], bvx[0:rows, 0:512].rearrange("p (h t) -> p h t", h=4),
                         [bbx], [sB[gs]])
                btx, bbx = P.bank()
                bvx = bf16_view(btx)
                for c in range(2):
                    P.tr(bvx[:, c * 128:(c + 1) * 128], Qc[i2][:, c * 128:(c + 1) * 128], ident, [tmB[i2]], [bbx])
                    P.tr(bvx[:, (2 + c) * 128:(3 + c) * 128], Kc[i2][:, c * 128:(c + 1) * 128], ident, [tmB[i2]], [bbx])
                P.cp("dve", sQc[gs][:, :, cs], bvx[:, 0:256].rearrange("p (h t) -> p h t", h=2), [bbx], [sB[gs]])
                P.cp("act", sKc[gs][:, :, cs], bvx[:, 256:512].rearrange("p (h t) -> p h t", h=2), [bbx], [sB[gs]])
                if tt_ == 3:
                    gsl = slice(G * 512, (G + 1) * 512)
                    q = "sp"
                    for h in range(4):
                        P.dma(q, qTm[h, :, gsl], sQm[gs][0:96, h, :], [sB[gs]], [], sem=sB[gs])
                        P.dma(q, kloc[h * 96:(h + 1) * 96, gsl], sKm[gs][0:96, h, :], [sB[gs]], [], sem=sB[gs])
                        P.dma(q, qTf[h, :, gsl], sQf[gs][0:70, h, :], [sB[gs]], [], sem=sB[gs])
                        P.dma(q, kloc[384 + h * 70:384 + (h + 1) * 70, gsl], sKf[gs][0:70, h, :], [sB[gs]], [], sem=sB[gs])
                    for c in range(2):
                        P.dma(q, qTc[c * 128:(c + 1) * 128, gsl], sQc[gs][:, c, :], [sB[gs]], [], sem=sB[gs])
                        P.dma(q, kloc[664 + c * 128:664 + (c + 1) * 128, gsl], sKc[gs][:, c, :], [sB[gs]], [], sem=sB[gs])
                    rows = slice(G * 512, (G + 1) * 512)
                    P.dma(q, vloc[rows, :].rearrange("(t p) c -> p t c", p=128), sV[gs], [sB[gs]], [], sem=sB[gs])
            P.dma("sp", totloc[:, :], ttsb, [ttB], [], sem=ttB)
            P.barrier()
            Prog.mute = False
            if STOP == 'p':
                break
            ccB = Buf("cc")
            ccB.sem = P.cc_index
            gatB = Buf("gathered")
            pieces = [(kloc[st_:st_ + n, :], kgat[l][c]) for c, (st_, n) in enumerate(KCH)]
            pieces += [(vloc[c * VR:(c + 1) * VR, :], vgat[l][c]) for c in range(NVC)]
            pieces.append((totloc, tgat[l]))
            for (src_ap, dst_ap) in pieces:
                P.dma("pool", None, None, [], [gatB], sem=ccB, inc=1,
                      fn=(lambda a, b: lambda e: e.collective_compute("AllGather", ALU.bypass, replica_groups=RG,
                                                                      ins=[a.opt()], outs=[b.opt()]))(src_ap, dst_ap))
            P.barrier()

            if STOP == 'cc':
                break
            P.rot = [4, 5, 6, 7]
            slots = []
            for i in range(2):
                slots.append((P.alloc([S], BF16), P.alloc([SL], BF16), P.alloc([NTG, 128], BF16), P.dbuf("kv%d" % i)))
            PT = [P.alloc([512], BF16) for _ in range(3)]
            PTB = [Buf("pt%d" % i) for i in range(3)]
            rden = [P.alloc([512], F32) for _ in range(2)]
            rdB = [Buf("rd%d" % i) for i in range(2)]
            ost = [P.alloc([512], BF16) for _ in range(2)]
            ostB = [P.dbuf("ost%d" % i) for i in range(2)]
            BBh = P.alloc([8, 512], BF16)
            bbhB = Buf("BBh")
            BBc = P.alloc([4, 12, 512], BF16)
            bbB = Buf("BBc")
            rstg = P.alloc([5, 128], F32)
            rsB = P.dbuf("rstg")
            for h in range(4):
                P.dma("sp", rstg, w["relT"][l, h], [], [rsB], sem=rsB)
                P.memset("pool", rstg[64:128, 0, 0:64], NEG, [rsB])
                P.memset("pool", rstg[0:64, 4, 64:128], NEG, [rsB])
                P.memset("pool", BBh, NEG, [bbhB])
                for j in range(8):
                    for qt in range(4):
                        dl = qt + 4 - j
                        if 0 <= dl <= 4:
                            P.cp("pool" if (j + qt) % 2 else "dve", BBh[:, j, qt * 128:(qt + 1) * 128], rstg[:, dl, :], [rsB], [bbhB])
                for m in range(12):
                    dst = BBc[:, h, m, :]
                    if m < 4:
                        P.ts("dve", dst, BBh[:, m, :], rk[:, 1:2], rk[:, 3:4], ALU.mult, ALU.add, [bbhB, cB], [bbB])
                    elif m < 8:
                        P.ts("dve", dst, BBh[:, m, :], rk[:, 1:2], None, ALU.mult, None, [bbhB, cB], [bbB])
                        P.op("dve", (lambda d, a: lambda e: e.scalar_tensor_tensor(out=d, in0=a, scalar=rk[:, 0:1], in1=d,
                                                                                     op0=ALU.mult, op1=ALU.add))(dst, BBh[:, m - 4, :]),
                             [bbhB, cB, bbB], [bbB])
                    else:
                        P.ts("dve", dst, BBh[:, m - 4, :], rk[:, 0:1], rk[:, 2:3], ALU.mult, ALU.add, [bbhB, cB], [bbB])
            Tld = P.alloc([2, NT * 4], F32)
            TldB = P.dbuf("Tld")
            for rho in range(2):
                P.dma("sp", Tld[:, rho, :], tgat[l][rho * 128, 0:NT * 4].partition_broadcast(128), [gatB], [TldB], sem=TldB)
            Trow = P.alloc([NTG, 4], F32)
            scA = P.alloc([NTG, 4], F32)
            scB = P.alloc([NTG, 4], F32)
            offb = P.alloc([4, NG, NTG], F32)
            colt = P.alloc([1], F32)
            offB = Buf("offb")
            T5 = Trow.rearrange("p (j r t) h -> p j r (t h)", r=2, t=4)
            for rho in range(2):
                P.cp("dve", T5[:, :, rho, :], Tld[:, rho, :].rearrange("p (j x) -> p j x", x=16), [TldB], [offB])
            P.cp("dve", scA, Trow, [offB], [offB])
            cur, nxt = scA, scB
            d = 1
            while d < NTG:
                P.cp("dve", nxt[:, 0:d, :], cur[:, 0:d, :], [offB], [offB])
                P.tt("dve", nxt[:, d:NTG, :], cur[:, d:NTG, :], cur[:, 0:NTG - d, :], ALU.add, [offB], [offB])
                cur, nxt = nxt, cur
                d *= 2
            P.tt("dve", nxt, cur, Trow, ALU.subtract, [offB], [offB])
            off = nxt
            for h in range(4):
                for j in range(NG):
                    P.ts("dve", colt, off[:, 8 * j, h:h + 1], rk[:, 1:2], None, ALU.mult, None, [cB, offB], [offB])
                    P.op("dve", (lambda a: lambda e: e.scalar_tensor_tensor(out=colt, in0=a, scalar=rk[:, 0:1], in1=colt,
                                                                              op0=ALU.mult, op1=ALU.add))(off[:, 8 * j + 4, h:h + 1]),
                         [cB, offB], [offB])
                    P.ts("dve", offb[:, h, j, :], off[:, :, h], colt[:, 0:1], None, ALU.subtract, None, [offB], [offB])
            if STOP == 'a0':
                break
            cnt = {"pt": 0, "o": 0, "hd": 0}

            def load_head(krow0, rows, qT_src, vc0, vw):
                KT, QT, V, kvB = slots[cnt["hd"] % 2]
                cnt["hd"] += 1
                KT4 = KT.rearrange("p (j r c) -> p j r c", r=2, c=512)
                V5 = V.rearrange("p (j r t) c -> p j r t c", r=2, t=4)
                kc = [c for c, (st_, n) in enumerate(KCH) if st_ <= krow0 < st_ + n][0]
                kst, kn = KCH[kc]
                for rho in range(2):
                    r0 = rho * kn + (krow0 - kst)
                    P.dma("sp", KT4[0:rows, :, rho, :], kgat[l][kc][r0:r0 + rows, :].rearrange("k (j c) -> k j c", c=512),
                          [gatB], [kvB], sem=kvB)
                P.dma("sp", QT[0:rows, :], qT_src, [], [kvB], sem=kvB)
                for rho in range(2):
                    for j in range(NG):
                        vcn, voff = divmod(j * 512, VR)
                        vr = vgat[l][vcn][rho * VR + voff:rho * VR + voff + 512, :].rearrange("(t p) c -> p t c", p=128)
                        P.dma("sp", V5[:, j, rho, :, 0:vw], vr[:, :, vc0:vc0 + vw], [gatB], [kvB], sem=kvB)
                return KT, QT, V, kvB

            def run_head(hd, rows, kind, scale, yrow0, bias_tiles, biasB, hh):
                KT, QT, V, kvB = hd
                for j in range(NG):
                    if kind == "chk":
                        kts = [(8 * j - 4 + m, m) for m in range(12) if 8 * j - 4 + m >= 0]
                    else:
                        kts = [(kt, (kt - 8 * j) if kt >= 8 * j else None) for kt in range(8 * j + 8)]
                    o2 = cnt["o"] % 2
                    cnt["o"] += 1
                    ob, obB = P.bank_at(2 * o2)
                    db, dbB = P.bank_at(2 * o2 + 1)
                    q = QT[0:rows, j * 512:(j + 1) * 512]
                    n = len(kts)

                    def pv(i, ptb, ptB):
                        kt = kts[i][0]
                        P.mm(ob[:, :], V[:, kt, :], ptb, i == 0, i == n - 1, [kvB, ptB], [obB])
                        if kind == "mla":
                            P.mm(db[:, :], ones_bf, ptb, i == 0, i == n - 1, [cB, ptB], [dbB])

                    prev = None
                    for i, (kt, m) in enumerate(kts):
                        sb_, sbB = P.bank()
                        P.mm(sb_[:, :], KT[0:rows, kt * 128:(kt + 1) * 128], q, True, m is None, [kvB], [sbB])
                        if m is not None:
                            P.mm(sb_[:, :], ident, bias_tiles[m], False, True, [cB, biasB], [sbB])
                        sl = cnt["pt"] % 3
                        cnt["pt"] += 1
                        if kind == "fox":
                            P.act(PT[sl], sb_[:, :], AF.Exp, [sbB, offB], [PTB[sl]], scale=scale, bias=offb[:, hh, j, kt:kt + 1])
                        else:
                            P.act(PT[sl], sb_[:, :], AF.Exp, [sbB], [PTB[sl]], scale=scale)
                        if prev is not None:
                            pv(*prev)
                        prev = (i, PT[sl], PTB[sl])
                    pv(*prev)
                    gsl = slice(j * 512, (j + 1) * 512)
                    if kind == "mla":
                        P.op("dve", lambda e, a=rden[o2], b=db: e.reciprocal(out=a, in_=b[:, :]), [dbB], [rdB[o2]])
                        P.tt("dve", ost[o2], ob[:, :], rden[o2], ALU.mult, [obB, rdB[o2]], [ostB[o2]])
                        P.dma("sp", ymixT[yrow0:yrow0 + 128, gsl], ost[o2], [ostB[o2]], [], sem=ostB[o2])
                    else:
                        P.op("dve", lambda e, a=rden[o2], b=ob: e.reciprocal(out=a[0:64, :], in_=b[64:128, :]), [obB], [rdB[o2]])
                        P.tt("dve", ost[o2][0:64, :], ob[0:64, :], rden[o2][0:64, :], ALU.mult, [obB, rdB[o2]], [ostB[o2]])
                        P.dma("sp", ymixT[yrow0:yrow0 + 64, gsl], ost[o2][0:64, :], [ostB[o2]], [], sem=ostB[o2])

            mla_scale = float(96 ** -0.5)
            specs = []
            for h in range(4):
                specs.append(((h * 96, 96, qTm[h], h * 128, 128),
                              (96, "mla", mla_scale, h * 128, [mbm8[:, m, :] for m in range(8)], cB, h)))
            for h in range(4):
                specs.append(((384 + h * 70, 70, qTf[h], 512 + h * 64, 64),
                              (70, "fox", 1.0, 512 + h * 64, [mbf8[:, m, :] for m in range(8)], cB, h)))
            for h in range(4):
                specs.append(((664 + h * 64, 64, qTc[h * 64:(h + 1) * 64, :], 768 + h * 64, 64),
                              (64, "chk", 1.0, 768 + h * 64, [BBc[:, h, m, :] for m in range(12)], bbB, h)))

            def prefetch(i):
                if i in (4, 5):
                    P.memset("pool", slots[i % 2][2][:, :, 64:128], 1.0, [slots[i % 2][3]])
                return load_head(*specs[i][0])

            hds = {0: prefetch(0)}
            for i in range(12):
                if i + 1 < 12:
                    hds[i + 1] = prefetch(i + 1)
                run_head(hds.pop(i), *specs[i][1])
            P.barrier()

            P.rot = list(range(8))
            stg = [P.alloc([2048], F32) for _ in range(2)]
            stgB = [P.dbuf("stg%d" % i) for i in range(2)]
            gB = P.dbuf("gains")
            g_o = P.alloc([8], F32)
            g_c = P.alloc([8], F32)
            g_m = P.alloc([8], F32)
            P.dma("sp", g_o, w["out_norm"][l].rearrange("(c p) -> p c", p=128), [], [gB], sem=gB, slow=True)
            P.dma("sp", g_c, w["norm_cross"][l].rearrange("(c p) -> p c", p=128), [], [gB], sem=gB, slow=True)
            P.dma("sp", g_m, w["norm_mem"][l].rearrange("(c p) -> p c", p=128), [], [gB], sem=gB, slow=True)
            Wo = P.alloc([8, 1024], BF16)
            Wcq = P.alloc([8, 512], BF16)
            Wco = P.alloc([4, 1024], BF16)
            Wckv = P.alloc([8, 1024], BF16)
            wB = Buf("wO1")
            load_weight(Wo, w["w_o"][l], D, [(0, 1024, 0)], gain=g_o, stg=stg, stgB=stgB, wB=wB)
            load_weight(Wcq, w["w_cq"][l], D, [(0, 512, 0)], gain=g_c, stg=stg, stgB=stgB, wB=wB)
            load_weight(Wco, w["w_co"][l], 512, [(0, 1024, 0)], stg=stg, stgB=stgB, wB=wB)
            load_weight(Wckv, w["w_ckv"][l], D, [(0, 1024, 0)], gain=g_m, stg=stg, stgB=stgB, wB=wB)
            junk = P.alloc([D], BF16)
            junkB = Buf("junk")
            memT = P.alloc([8, 256], BF16)
            KcT = P.alloc([4, 256], BF16)
            Vc = P.alloc([2, 512], BF16)
            mB = Buf("mem")
            mx = P.alloc([D], F32)
            mxB = P.dbuf("mx")
            mst = P.alloc([2], F32)
            mhb = P.alloc([D], BF16)
            for mt in range(2):
                P.dma("sp", mx, mem_in[mt * 128:(mt + 1) * 128, :], [], [mxB], sem=mxB)
                P.act(junk, mx, AF.Square, [mxB], [junkB, mB], accum=mst[:, 0:1])
                rms_rstd(mst[:, 0:1], 1, D, mB)
                P.ts("dve", mhb, mx, mst[:, 0:1], None, ALU.mult, None, [mxB, mB], [mB])
                bt, bb = P.bank()
                bv = bf16_view(bt)
                for c in range(8):
                    P.tr(bv[:, c * 128:(c + 1) * 128], mhb[:, c * 128:(c + 1) * 128], ident, [mB], [bb])
                P.cp("dve", memT[:, :, mt * 128:(mt + 1) * 128], bv.rearrange("p (c t) -> p c t", c=8), [bb], [mB])
            for h in range(4):
                bt, bb = P.bank()
                for c in range(8):
                    P.mm(bt[:, 0:256], Wckv[:, c, h * 128:(h + 1) * 128], memT[:, c, :], c == 0, c == 7, [wB, mB], [bb])
                P.cp("act", KcT[:, h, :], bt[:, 0:256], [bb], [mB])
            for mt in range(2):
                bt, bb = P.bank()
                for c in range(8):
                    P.mm(bt[:, :], memT[:, c, mt * 128:(mt + 1) * 128], Wckv[:, c, 512:1024], c == 0, c == 7, [wB, mB], [bb])
                P.cp("act", Vc[:, mt, :], bt[:, :], [bb], [mB])

            YT = [P.alloc([8, 512], BF16) for _ in range(2)]
            YTB = [P.dbuf("yt%d" % i) for i in range(2)]
            xg = [P.alloc([4, D], F32) for _ in range(2)]
            xgB = [P.dbuf("xg%d" % i) for i in range(2)]
            sq = P.alloc([8, 512], BF16)
            sqB = Buf("sq")
            rr = P.alloc([3, 512], F32)
            rrB = Buf("rr")
            YN = P.alloc([8, 512], BF16)
            YNB = Buf("yn")
            st1 = P.alloc([4], F32)
            st1B = Buf("st1")
            hb1 = P.alloc([D], BF16)
            hb1B = Buf("hb1")
            hcT = P.alloc([8, 512], BF16)
            hcTB = Buf("hcT")
            qcT = P.alloc([4, 512], BF16)
            qcTB = Buf("qcT")
            PT = [P.alloc([512], BF16) for _ in range(3)]
            PTB = [Buf("pt%d" % i) for i in range(3)]
            ocT = P.alloc([4, 512], BF16)
            ocTB = Buf("ocT")
            rdn = P.alloc([512], F32)
            rdnB = Buf("rdn")
            pti = 0
            c_scale = float(128 ** -0.5)
            x_o1 = x_in if l == 0 else xres
            for G in range(NG):
                g2 = G % 2
                gsl = slice(G * 512, (G + 1) * 512)
                P.dma("sp", YT[g2], ymixT[:, gsl].rearrange("(c p) s -> p c s", p=128), [], [YTB[g2]], sem=YTB[g2])
                P.dma("sp", xg[g2], x_o1[gsl, :].rearrange("(t p) d -> p t d", p=128), [], [xgB[g2]], sem=xgB[g2])
                P.tt("pool", sq, YT[g2], YT[g2], ALU.mult, [YTB[g2]], [sqB])
                for gi, (c0, c1, wdt) in enumerate(((0, 4, 512), (4, 6, 256), (6, 8, 256))):
                    bt, bb = P.bank()
                    for c in range(c0, c1):
                        P.mm(bt[:, :], ones_bf, sq[:, c, :], c == c0, c == c1 - 1, [cB, sqB], [bb])
                    P.act(rr[:, gi, :], bt[:, :], AF.Ln, [bb], [rrB], bias=eps_t[:, 0:1], scale=1.0 / wdt)
                    P.act(rr[:, gi, :], rr[:, gi, :], AF.Exp, [rrB], [rrB], scale=-0.5)
                    for c in range(c0, c1):
                        P.tt("dve", YN[:, c, :], YT[g2][:, c, :], rr[:, gi, :], ALU.mult, [YTB[g2], rrB], [YNB])
                for t in range(4):
                    ts_ = slice(t * 128, (t + 1) * 128)
                    for half in range(2):
                        hs = slice(half * 512, (half + 1) * 512)
                        bt, bb = P.bank()
                        for c in range(8):
                            P.mm(bt[:, :], YN[:, c, ts_], Wo[:, c, hs], c == 0, c == 7, [YNB, wB], [bb])
                        P.tt("dve", xg[g2][:, t, hs], bt[:, :], xg[g2][:, t, hs], ALU.add, [bb, xgB[g2]], [xgB[g2]])
                    P.act(junk, xg[g2][:, t, :], AF.Square, [xgB[g2]], [junkB, st1B], accum=st1[:, t:t + 1])
                    rms_rstd(st1[:, t:t + 1], 1, D, st1B)
                    P.ts("dve", hb1, xg[g2][:, t, :], st1[:, t:t + 1], None, ALU.mult, None, [xgB[g2], st1B], [hb1B])
                    bt, bb = P.bank()
                    bv = bf16_view(bt)
                    for c in range(8):
                        P.tr(bv[:, c * 128:(c + 1) * 128], hb1[:, c * 128:(c + 1) * 128], ident, [hb1B], [bb])
                    P.cp("act", hcT[:, :, ts_], bv.rearrange("p (c t) -> p c t", c=8), [bb], [hcTB])
                for h in range(4):
                    bt, bb = P.bank()
                    for c in range(8):
                        P.mm(bt[:, :], Wcq[:, c, h * 128:(h + 1) * 128], hcT[:, c, :], c == 0, c == 7, [wB, hcTB], [bb])
                    P.cp("act", qcT[:, h, :], bt[:, :], [bb], [qcTB])
                for h in range(4):
                    ob, obB = P.bank()
                    db, dbB = P.bank()
                    pts = []
                    for mt in range(2):
                        sb_, sbB = P.bank()
                        P.mm(sb_[:, :], KcT[:, h, mt * 128:(mt + 1) * 128], qcT[:, h, :], True, True, [mB, qcTB], [sbB])
                        sl = pti % 3
                        pti += 1
                        P.act(PT[sl], sb_[:, :], AF.Exp, [sbB], [PTB[sl]], scale=c_scale)
                        pts.append(sl)
                    for mt in range(2):
                        sl = pts[mt]
                        P.mm(ob[:, :], Vc[:, mt, h * 128:(h + 1) * 128], PT[sl], mt == 0, mt == 1, [mB, PTB[sl]], [obB])
                        P.mm(db[:, :], ones_bf, PT[sl], mt == 0, mt == 1, [cB, PTB[sl]], [dbB])
                    P.op("dve", lambda e, a=rdn, b=db: e.reciprocal(out=a, in_=b[:, :]), [dbB], [rdnB])
                    P.tt("dve", ocT[:, h, :], ob[:, :], rdn, ALU.mult, [obB, rdnB], [ocTB])
                for t in range(4):
                    ts_ = slice(t * 128, (t + 1) * 128)
                    for half in range(2):
                        hs = slice(half * 512, (half + 1) * 512)
                        bt, bb = P.bank()
                        for h in range(4):
                            P.mm(bt[:, :], ocT[:, h, ts_], Wco[:, h, hs], h == 0, h == 3, [ocTB, wB], [bb])
                        P.tt("dve", xg[g2][:, t, hs], bt[:, :], xg[g2][:, t, hs], ALU.add, [bb, xgB[g2]], [xgB[g2]])
                P.dma("sp", xres[gsl, :].rearrange("(t p) d -> p t d", p=128), xg[g2], [xgB[g2]], [], sem=xgB[g2])
            P.barrier()

            stg = [P.alloc([1024], F32) for _ in range(2)]
            stgB = [P.dbuf("stg%d" % i) for i in range(2)]
            gB = P.dbuf("gains")
            g_f = P.alloc([8], F32)
            P.dma("sp", g_f, w["norm_ffn"][l].rearrange("(c p) -> p c", p=128), [], [gB], sem=gB, slow=True)
            last = (l == n_layers - 1)
            if last:
                gfin = P.alloc([D], F32)
                P.dma("sp", gfin, w["final_norm"].partition_broadcast(128), [], [gB], sem=gB)
            Wgu = P.alloc([8, 2 * FFN], BF16)
            Wd = P.alloc([22, 1024], BF16)
            wB = Buf("wO2")
            load_weight(Wgu, w["w_gu"][l], D, [(i * 1024, min(1024, 2 * FFN - i * 1024), i * 1024) for i in range(6)], gain=g_f, stg=stg, stgB=stgB, wB=wB)
            load_weight(Wd, w["w_down"][l], FFN, [(0, 1024, 0)], stg=stg, stgB=stgB, wB=wB)
            junk = P.alloc([D], BF16)
            junkB = Buf("junk")
            xg = P.alloc([2, D], F32)
            xgB = P.dbuf("xg")
            st2 = P.alloc([8], F32)
            st2B = Buf("st2")
            hb2 = P.alloc([D], BF16)
            hb2B = Buf("hb2")
            hfT = P.alloc([8, 256], BF16)
            hfTB = Buf("hfT")
            sg = [P.alloc([256], BF16) for _ in range(2)]
            sgB = [Buf("sg%d" % i) for i in range(2)]
            aT = P.alloc([22, 256], BF16)
            aTB = Buf("aT")
            for G in range(SL // 256):
                gsl = slice(G * 256, (G + 1) * 256)
                P.dma("sp", xg, xres[gsl, :].rearrange("(t p) d -> p t d", p=128), [], [xgB], sem=xgB)
                for t in range(2):
                    ts_ = slice(t * 128, (t + 1) * 128)
                    P.act(junk, xg[:, t, :], AF.Square, [xgB], [junkB, st2B], accum=st2[:, t:t + 1])
                    rms_rstd(st2[:, t:t + 1], 1, D, st2B)
                    P.ts("dve", hb2, xg[:, t, :], st2[:, t:t + 1], None, ALU.mult, None, [xgB, st2B], [hb2B])
                    bt, bb = P.bank()
                    bv = bf16_view(bt)
                    for c in range(8):
                        P.tr(bv[:, c * 128:(c + 1) * 128], hb2[:, c * 128:(c + 1) * 128], ident, [hb2B], [bb])
                    P.cp("dve", hfT[:, :, ts_], bv.rearrange("p (c t) -> p c t", c=8), [bb], [hfTB])
                for c2 in range(22):
                    bg, bgB = P.bank()
                    bu, buB = P.bank()
                    for c in range(8):
                        P.mm(bg[:, 0:256], Wgu[:, c, c2 * 128:(c2 + 1) * 128], hfT[:, c, :], c == 0, c == 7, [wB, hfTB], [bgB])
                    for c in range(8):
                        P.mm(bu[:, 0:256], Wgu[:, c, FFN + c2 * 128:FFN + (c2 + 1) * 128], hfT[:, c, :], c == 0, c == 7, [wB, hfTB], [buB])
                    s2_ = c2 % 2
                    P.act(sg[s2_], bg[:, 0:256], AF.Silu, [bgB], [sgB[s2_]])
                    P.tt("dve", aT[:, c2, :], bu[:, 0:256], sg[s2_], ALU.mult, [buB, sgB[s2_]], [aTB])
                for t in range(2):
                    ts_ = slice(t * 128, (t + 1) * 128)
                    for half in range(2):
                        hs = slice(half * 512, (half + 1) * 512)
                        bt, bb = P.bank()
                        for c2 in range(22):
                            P.mm(bt[:, :], aT[:, c2, ts_], Wd[:, c2, hs], c2 == 0, c2 == 21, [aTB, wB], [bb])
                        P.tt("dve", xg[:, t, hs], bt[:, :], xg[:, t, hs], ALU.add, [bb, xgB], [xgB])
                    rows = slice(G * 256 + t * 128, G * 256 + (t + 1) * 128)
                    if last:
                        P.act(junk, xg[:, t, :], AF.Square, [xgB], [junkB, st2B], accum=st2[:, 4 + t:5 + t])
                        rms_rstd(st2[:, 4 + t:5 + t], 1, D, st2B)
                        P.op("dve", lambda e, b=xg[:, t, :], c=st2[:, 4 + t:5 + t]: e.scalar_tensor_tensor(
                            out=b, in0=b, scalar=c, in1=gfin, op0=ALU.mult, op1=ALU.mult), [xgB, st2B, gB], [xgB])
                        P.dma("sp", out_d[rows, :], xg[:, t, :], [xgB], [], sem=xgB)
                if not last:
                    P.dma("sp", xres[gsl, :].rearrange("(t p) d -> p t d", p=128), xg, [xgB], [], sem=xgB)
            P.barrier()

        P.barrier()
        with nc.Block() as block:
            P.emit(block)
    return nc


def make_relT(rel_bias):
    ki = np.arange(128)[:, None, None]
    dl = np.arange(5)[None, :, None]
    qi = np.arange(128)[None, None, :]
    idx = np.clip(dl * 128 + qi - ki, -63, 128) + 63
    return np.ascontiguousarray(rel_bias[:, :, idx])


def kernel(**inputs):
    x = np.asarray(inputs["x"], dtype=np.float32)
    B, S, Dm = x.shape
    NGg = S // 512
    nc = build_program(S)
    relT = make_relT(np.asarray(inputs["rel_bias"], dtype=np.float32))
    in_maps = []
    for c in range(2 * B):
        b, r = divmod(c, 2)
        xc = np.ascontiguousarray(x[b].reshape(NGg, 512, Dm)[r::2].reshape(S // 2, Dm))
        m = {"x": xc, "mem": np.ascontiguousarray(inputs["mem"][b], dtype=np.float32), "relT": relT,
             "rank": np.ascontiguousarray(np.tile(np.array([[r, 1 - r]], dtype=np.float32), (128, 1)))}
        for k, v in inputs.items():
            if k in ("x", "mem", "rel_bias"):
                continue
            m[k] = np.ascontiguousarray(v, dtype=np.float32)
        in_maps.append(m)
    res = run_bass_kernel_spmd(nc, in_maps, core_ids=list(range(2 * B)))
    out = np.empty((B, S, Dm), dtype=np.float32)
    for c in range(2 * B):
        b, r = divmod(c, 2)
        out[b].reshape(NGg, 512, Dm)[r::2] = np.asarray(res.results[c]["out"], dtype=np.float32).reshape(NGg // 2, 512, Dm)
    return out
```
